# Optimizing a Trainium2 kernel written in Bass

```python
import jax, jax.numpy as jnp
from jax import lax
import numpy as np

D_MODEL = 1024
BATCH = 4
SEQ = 4096
DEPTH = 1

HEAD_DIM = 64
NSA_HEADS = 8
NSA_KV_HEADS = 2
SWA_HEADS = 8
SWA_KV_HEADS = 2
CMP_BLOCK = 32
CMP_STRIDE = 16
CMP_HIDDEN = 256
SLC_BLOCK = 64
N_SELECT = 16
NSA_WINDOW = 512
SWA_WINDOW = 128
Q_BLOCK = 128
RMS_EPS = 1e-6
NEG = -1e30
BIG = 1e30
NSA_WIDTH = NSA_HEADS * HEAD_DIM
SWA_WIDTH = SWA_HEADS * HEAD_DIM
KV_NSA = NSA_KV_HEADS * HEAD_DIM
KV_SWA = SWA_KV_HEADS * HEAD_DIM
SPLIT_SIZES = (NSA_WIDTH, 2 * KV_NSA, 2 * KV_NSA, 2 * KV_NSA, 3 * NSA_HEADS, NSA_WIDTH,
               SWA_WIDTH, 2 * KV_SWA, SWA_WIDTH, 2 * D_MODEL)
D_IN = sum(SPLIT_SIZES)

kernel_name = 'hybrid_nsa_swa_sink_gated_block'


def split_points():
    pts, acc = [], 0
    for s in SPLIT_SIZES[:-1]:
        acc += s
        pts.append(acc)
    return pts


def alibi_slopes(n, n_kv):
    s = 2.0 ** (-8.0 * (np.arange(n) + 1) / n)
    return jnp.asarray(s, dtype=jnp.float32).reshape(n_kv, n // n_kv)


def rms_norm(x, g):
    xf = x.astype(jnp.float32)
    y = xf * lax.rsqrt(jnp.mean(xf * xf, axis=-1, keepdims=True) + RMS_EPS)
    return (y * g.astype(jnp.float32)).astype(x.dtype)


def q_heads(t, n_kv, r):
    B, S, _ = t.shape
    return t.reshape(B, S, n_kv, r, HEAD_DIM).transpose(0, 2, 3, 1, 4)


def kv_heads(t, n_kv):
    B, S, _ = t.shape
    return t.reshape(B, S, n_kv, HEAD_DIM).transpose(0, 2, 1, 3)


def merge_heads(o):
    B, G, r, S, dh = o.shape
    return o.transpose(0, 3, 1, 2, 4).reshape(B, S, G * r * dh)


def banded_attention(q, k, v, slopes, window, sinks=None):
    B, G, r, S, dh = q.shape
    nprev = window // Q_BLOCK
    nqb = S // Q_BLOCK
    lk = (nprev + 1) * Q_BLOCK
    pad = nprev * Q_BLOCK
    kp = jnp.pad(k, ((0, 0), (0, 0), (pad, 0), (0, 0)))
    vp = jnp.pad(v, ((0, 0), (0, 0), (pad, 0), (0, 0)))
    idx = jnp.arange(nqb)[:, None] * Q_BLOCK + jnp.arange(lk)[None, :]
    kb = kp[:, :, idx].astype(jnp.float32)
    vb = vp[:, :, idx].astype(jnp.float32)
    qb = q.reshape(B, G, r, nqb, Q_BLOCK, dh).astype(jnp.float32)
    s = jnp.einsum('bgrnqd,bgnkd->bgrnqk', qb, kb) * (dh ** -0.5)
    qpos = jnp.arange(nqb)[:, None] * Q_BLOCK + jnp.arange(Q_BLOCK)[None, :]
    kpos = idx - pad
    dist = qpos[:, :, None] - kpos[:, None, :]
    valid = (dist >= 0) & (dist < window) & (kpos[:, None, :] >= 0)
    s = s - slopes[None, :, :, None, None, None] * dist.astype(jnp.float32)
    s = jnp.where(valid, s, NEG)
    if sinks is not None:
        sink = jnp.broadcast_to(sinks.astype(jnp.float32).reshape(1, G, r, 1, 1, 1), s.shape[:-1] + (1,))
        p = jax.nn.softmax(jnp.concatenate([s, sink], axis=-1), axis=-1)[..., :-1]
    else:
        p = jax.nn.softmax(s, axis=-1)
    o = jnp.einsum('bgrnqk,bgnkd->bgrnqd', p, vb)
    return o.reshape(B, G, r, S, dh).astype(q.dtype)


def compress_blocks(t, pe, w1, w2):
    B, G, S, dh = t.shape
    nc = (S - CMP_BLOCK) // CMP_STRIDE + 1
    idx = jnp.arange(nc)[:, None] * CMP_STRIDE + jnp.arange(CMP_BLOCK)[None, :]
    blk = t[:, :, idx] + pe
    flat = blk.reshape(B, G, nc, CMP_BLOCK * dh)
    return jax.nn.silu(flat @ w1) @ w2


def compressed_attention(q, kc, vc, slopes):
    B, G, r, S, dh = q.shape
    nc = kc.shape[2]
    s = jnp.einsum('bgrqd,bgcd->bgrqc', q.astype(jnp.float32), kc.astype(jnp.float32)) * (dh ** -0.5)
    end = jnp.arange(nc) * CMP_STRIDE + CMP_BLOCK - 1
    dist = jnp.arange(S)[:, None] - end[None, :]
    valid = dist >= 0
    s = s - slopes[None, :, :, None, None] * dist.astype(jnp.float32)
    s = jnp.where(valid, s, NEG)
    p = jax.nn.softmax(s, axis=-1) * jnp.any(valid, axis=-1)[:, None].astype(jnp.float32)
    o = jnp.einsum('bgrqc,bgcd->bgrqd', p, vc.astype(jnp.float32))
    return o.astype(q.dtype), p


def select_blocks(p_cmp):
    p = p_cmp.sum(axis=2)
    S, nc = p.shape[2], p.shape[3]
    nsb = S // SLC_BLOCK
    ratio = SLC_BLOCK // CMP_STRIDE
    span = CMP_BLOCK // CMP_STRIDE
    offs = (jnp.arange(ratio)[:, None] + jnp.arange(span)[None, :]).reshape(-1)
    cidx = ratio * jnp.arange(nsb)[:, None] - offs[None, :]
    ok = (cidx >= 0) & (cidx < nc)
    imp = jnp.where(ok, p[..., jnp.clip(cidx, 0, nc - 1)], 0.0).sum(axis=-1)
    cur = jnp.arange(S) // SLC_BLOCK
    j = jnp.arange(nsb)
    causal = j[None, :] <= cur[:, None]
    forced = (j[None, :] == 0) | (j[None, :] == cur[:, None]) | (j[None, :] == cur[:, None] - 1)
    score = jnp.where(causal & forced, BIG, jnp.where(causal, imp, NEG))
    _, idx = lax.top_k(score, min(N_SELECT, nsb))
    return idx


def selected_attention(q, k, v, blk_idx, slopes):
    B, G, r, S, dh = q.shape
    nsb = S // SLC_BLOCK
    n_sel = blk_idx.shape[-1]
    nqc = S // Q_BLOCK
    kb = k.reshape(B, G, nsb, SLC_BLOCK, dh)
    vb = v.reshape(B, G, nsb, SLC_BLOCK, dh)
    qc = q.reshape(B, G, r, nqc, Q_BLOCK, dh).transpose(3, 0, 1, 2, 4, 5)
    ic = blk_idx.reshape(B, G, nqc, Q_BLOCK, n_sel).transpose(2, 0, 1, 3, 4)
    qpos = jnp.arange(S).reshape(nqc, Q_BLOCK)
    gather = jax.vmap(jax.vmap(lambda blocks, ix: blocks[ix]))

    def one_block(args):
        q_i, ix, qp = args
        k_sel = gather(kb, ix).astype(jnp.float32)
        v_sel = gather(vb, ix).astype(jnp.float32)
        kpos = ix[..., None] * SLC_BLOCK + jnp.arange(SLC_BLOCK)
        dist = (qp[None, None, :, None, None] - kpos)[:, :, None]
        s = jnp.einsum('bgrqd,bgqnkd->bgrqnk', q_i.astype(jnp.float32), k_sel) * (dh ** -0.5)
        s = s - slopes[None, :, :, None, None, None] * dist.astype(jnp.float32)
        s = jnp.where(dist >= 0, s, NEG).reshape(B, G, r, Q_BLOCK, n_sel * SLC_BLOCK)
        p = jax.nn.softmax(s, axis=-1)
        return jnp.einsum('bgrqk,bgqkd->bgrqd', p, v_sel.reshape(B, G, Q_BLOCK, n_sel * SLC_BLOCK, dh))

    o = lax.map(one_block, (qc, ic, qpos))
    return o.transpose(1, 2, 3, 0, 4, 5).reshape(B, G, r, S, dh).astype(q.dtype)


def setup_inputs(seed: int = 0) -> dict:
    key = jax.random.key(seed)
    ks = jax.random.split(key, 20)
    D, L, dh = D_MODEL, DEPTH, HEAD_DIM
    nrm = lambda k, shape, fan: jax.random.normal(k, shape, jnp.float32) * (fan ** -0.5)
    return {
        'x': jax.random.normal(ks[0], (BATCH, SEQ, D), jnp.float32),
        'c': jax.random.normal(ks[1], (BATCH, D), jnp.float32),
        'w_ada': nrm(ks[2], (L, D, 3 * D), D) * 0.5,
        'b_ada': 0.01 * jax.random.normal(ks[3], (L, 3 * D), jnp.float32),
        'g_pre': 1.0 + 0.05 * jax.random.normal(ks[4], (L, D), jnp.float32),
        'g_post': 1.0 + 0.05 * jax.random.normal(ks[5], (L, D), jnp.float32),
        'w_in': nrm(ks[6], (L, D, D_IN), D),
        'pe_cmp_k': 0.02 * jax.random.normal(ks[7], (L, CMP_BLOCK, dh), jnp.float32),
        'pe_cmp_v': 0.02 * jax.random.normal(ks[8], (L, CMP_BLOCK, dh), jnp.float32),
        'w_cmp_k1': nrm(ks[9], (L, CMP_BLOCK * dh, CMP_HIDDEN), CMP_BLOCK * dh),
        'w_cmp_k2': nrm(ks[10], (L, CMP_HIDDEN, dh), CMP_HIDDEN),
        'w_cmp_v1': nrm(ks[11], (L, CMP_BLOCK * dh, CMP_HIDDEN), CMP_BLOCK * dh),
        'w_cmp_v2': nrm(ks[12], (L, CMP_HIDDEN, dh), CMP_HIDDEN),
        'w_o_nsa': nrm(ks[13], (L, NSA_WIDTH, D), NSA_WIDTH),
        'w_o_swa': nrm(ks[14], (L, SWA_WIDTH, D), SWA_WIDTH),
        'w_out': nrm(ks[15], (L, D, D), D),
        'sinks': jax.random.normal(ks[16], (L, SWA_HEADS), jnp.float32),
    }


def reference(x, c, w_ada, b_ada, g_pre, g_post, w_in, pe_cmp_k, pe_cmp_v, w_cmp_k1, w_cmp_k2,
              w_cmp_v1, w_cmp_v2, w_o_nsa, w_o_swa, w_out, sinks):
    B, S, D = x.shape
    ra = NSA_HEADS // NSA_KV_HEADS
    rb = SWA_HEADS // SWA_KV_HEADS
    slopes_a = alibi_slopes(NSA_HEADS, NSA_KV_HEADS)
    slopes_b = alibi_slopes(SWA_HEADS, SWA_KV_HEADS)
    pts = split_points()
    for l in range(DEPTH):
        mod = c @ w_ada[l] + b_ada[l]
        shift, scale, gate = jnp.split(mod, 3, axis=-1)
        h = rms_norm(x, g_pre[l]) * (1.0 + scale[:, None, :]) + shift[:, None, :]
        proj = h @ w_in[l]
        q_a, kv_c, kv_s, kv_w, g_nsa, z_a, q_b, kv_b, z_b, merge = jnp.split(proj, pts, axis=-1)

        qa = q_heads(q_a, NSA_KV_HEADS, ra)
        kc_raw, vc_raw = jnp.split(kv_c, 2, axis=-1)
        ks_, vs_ = jnp.split(kv_s, 2, axis=-1)
        kw_, vw_ = jnp.split(kv_w, 2, axis=-1)
        kc = compress_blocks(kv_heads(kc_raw, NSA_KV_HEADS), pe_cmp_k[l], w_cmp_k1[l], w_cmp_k2[l])
        vc = compress_blocks(kv_heads(vc_raw, NSA_KV_HEADS), pe_cmp_v[l], w_cmp_v1[l], w_cmp_v2[l])
        o_cmp, p_cmp = compressed_attention(qa, kc, vc, slopes_a)
        blk_idx = select_blocks(p_cmp)
        o_slc = selected_attention(qa, kv_heads(ks_, NSA_KV_HEADS), kv_heads(vs_, NSA_KV_HEADS), blk_idx, slopes_a)
        o_win = banded_attention(qa, kv_heads(kw_, NSA_KV_HEADS), kv_heads(vw_, NSA_KV_HEADS), slopes_a, NSA_WINDOW)
        gts = jax.nn.sigmoid(g_nsa.reshape(B, S, 3, NSA_KV_HEADS, ra)).transpose(2, 0, 3, 4, 1)[..., None]
        o_a = gts[0] * o_cmp + gts[1] * o_slc + gts[2] * o_win
        y_a = (merge_heads(o_a) * jax.nn.silu(z_a)) @ w_o_nsa[l]

        qb = q_heads(q_b, SWA_KV_HEADS, rb)
        kb_, vb_ = jnp.split(kv_b, 2, axis=-1)
        o_b = banded_attention(qb, kv_heads(kb_, SWA_KV_HEADS), kv_heads(vb_, SWA_KV_HEADS), slopes_b,
                               SWA_WINDOW, sinks=sinks[l])
        y_b = (merge_heads(o_b) * jax.nn.silu(z_b)) @ w_o_swa[l]

        m_a, m_b = jnp.split(merge, 2, axis=-1)
        y = (jax.nn.sigmoid(m_a) * y_a + jax.nn.sigmoid(m_b) * y_b) @ w_out[l]
        x = x + gate[:, None, :] * rms_norm(y, g_post[l])
    return x
```

```python
import numpy as np
from contextlib import ExitStack
import concourse.bass as bass
import concourse.mybir as mybir
from concourse.bass_utils import run_bass_kernel_spmd

F32 = mybir.dt.float32
BF16 = mybir.dt.bfloat16
AF = mybir.ActivationFunctionType
ALU = mybir.AluOpType

NEGM = -30000.0
IMP_INLINE = True
S = 4096
D = 1024
NSLOT = 16


class Buf:
    __slots__ = ("name", "w", "r", "xr", "dcount", "sem", "last")

    def __init__(self, name):
        self.name = name
        self.w = None
        self.r = {}
        self.xr = {}
        self.dcount = 0
        self.sem = None
        self.last = None


class Op:
    __slots__ = ("eng", "fn", "deps", "signal", "dma", "sigval")

    def __init__(self, eng, fn, dma):
        self.eng = eng
        self.fn = fn
        self.dma = dma
        self.deps = []
        self.signal = False
        self.sigval = 0


class Prog:
    ENGS = ("pe", "act", "dve", "pool", "sp")

    def __init__(self):
        self.ops = {e: [] for e in self.ENGS}
        self.dma_bufs = []
        self.pending = {e: [] for e in self.ENGS}

    def buf(self, name):
        return Buf(name)

    def barrier(self):
        lasts = [self.ops[e][-1] for e in self.ENGS if self.ops[e]]
        lasts += [b.last for b in self.dma_bufs if b.last is not None]
        for e in self.ENGS:
            self.pending[e] = list(lasts)

    def add(self, eng, fn, reads=(), writes=(), pwrites=(), dma=None, xreads=()):
        op = Op(eng, fn, dma)
        deps = {}
        for b in reads:
            if b.w is not None:
                deps[id(b.w)] = b.w
        for b in xreads:
            if b.w is not None:
                deps[id(b.w)] = b.w
            for k_, r in b.xr.items():
                if k_ != eng:
                    deps[id(r)] = r
        for b in list(writes) + list(pwrites):
            for r in b.r.values():
                deps[id(r)] = r
            for r in b.xr.values():
                deps[id(r)] = r
        for b in writes:
            if b.w is not None:
                deps[id(b.w)] = b.w
        for d in self.pending[eng]:
            deps[id(d)] = d
        self.pending[eng] = []
        for d in deps.values():
            if d.dma is None and dma is None and d.eng == "pe" and eng == "pe":
                continue
            if d.dma is None and d.eng == eng and eng == "sp":
                continue
            op.deps.append(d)
            d.signal = True
        key = eng if dma is None else ("dma", id(dma))
        for b in reads:
            b.r[key] = op
        for b in xreads:
            b.xr[eng] = op
        for b in list(writes) + list(pwrites):
            b.w = op
            b.r = {}
            b.xr = {}
        if dma is not None:
            if dma not in self.dma_bufs:
                self.dma_bufs.append(dma)
            dma.dcount += 16
            dma.last = op
            op.sigval = dma.dcount
            op.signal = True
        self.ops[eng].append(op)
        return op

    def emit(self, nc, final_waits=()):
        with ExitStack() as st:
            esem = {e: st.enter_context(nc.semaphore("sem_" + e)) for e in self.ENGS}
            for i, b in enumerate(self.dma_bufs):
                b.sem = st.enter_context(nc.semaphore("dsem%d" % i))
            for e in self.ENGS:
                c = 0
                for op in self.ops[e]:
                    if op.dma is None and op.signal:
                        c += 1
                        op.sigval = c
            block = st.enter_context(nc.Block())

            def run(eng_name, engine):
                waited = {}
                for op in self.ops[eng_name]:
                    for d in op.deps:
                        if d.dma is not None:
                            sem, key = d.dma.sem, ("d", id(d.dma))
                        else:
                            sem, key = esem[d.eng], d.eng
                        if waited.get(key, 0) >= d.sigval:
                            continue
                        waited[key] = d.sigval
                        engine.wait_ge(sem, d.sigval)
                    ins = op.fn(engine)
                    if op.dma is not None:
                        ins.then_inc(op.dma.sem, 16)
                    elif op.signal:
                        ins.then_inc(esem[eng_name], 1)
                if eng_name == "sp":
                    for b in final_waits:
                        engine.wait_ge(b.sem, b.dcount)

            @block.tensor
            def _(e):
                run("pe", e)

            @block.scalar
            def _(e):
                run("act", e)

            @block.vector
            def _(e):
                run("dve", e)

            @block.gpsimd
            def _(e):
                run("pool", e)

            @block.sync
            def _(e):
                run("sp", e)


class Arena:
    def __init__(self, nc, st, nbytes):
        self.t = st.enter_context(nc.sbuf_tensor("arena", [128, nbytes // 2], BF16))
        self.top = 0
        self.cap = nbytes

    def alloc(self, shape, dtype):
        esz = 4 if dtype == F32 else 2
        n = 1
        for s_ in shape[1:]:
            n *= s_
        nb = (n * esz + 31) // 32 * 32
        off = self.top
        self.top += nb
        assert self.top <= self.cap, ("SBUF arena overflow", self.top, self.cap)
        v = self.t[:, off // 2: off // 2 + n * esz // 2]
        if dtype == F32:
            v = v.bitcast(F32)
        if len(shape) == 3:
            v = v.rearrange("p (a b) -> p a b", a=shape[1])
        elif len(shape) == 4:
            v = v.rearrange("p (a b c) -> p a b c", a=shape[1], b=shape[2])
        elif len(shape) == 5:
            v = v.rearrange("p (a b c d) -> p a b c d", a=shape[1], b=shape[2], c=shape[3])
        return v

    def mark(self):
        return self.top

    def release(self, m):
        self.top = m


def _slopes():
    return 2.0 ** (-(np.arange(8) + 1.0))


def _shared_tables():
    t = {}
    t["ident"] = np.eye(128, dtype=np.float32)
    kpos = np.arange(S)
    t["kaug"] = np.stack([np.ones(S), np.ones(S), kpos // 128, kpos % 128]).astype(np.float32)
    c = np.arange(256)
    end = 16 * c + 31
    t["caug"] = np.stack([np.ones(256), np.ones(256), end // 128, end % 128]).astype(np.float32)
    M = np.zeros((256, 64), np.float32)
    for j in range(64):
        for m in range(4):
            for n in range(2):
                ci = 4 * j - m - n
                if 0 <= ci < 255:
                    M[ci, j] += 1.0
    t["mimp"] = M
    R = np.zeros((64, S), np.float32)
    R[kpos // 64, kpos] = 1.0
    t["ksel"] = np.concatenate([R[1:62], np.stack([np.ones(S), kpos // 128, kpos % 128])]).astype(np.float32)
    return t


def _core_tables(p):
    sl = _slopes()
    t = {}
    qa = np.zeros((NSLOT, 4, 4, 4, 128), np.float32)
    q = np.arange(128)
    for j in range(NSLOT):
        for tg in range(4):
            g = tg % 2
            for r in range(4):
                s_ = sl[g * 4 + r]
                qa[j, 0, tg, r] = -s_ * 128.0 * (2 * j + p)
                qa[j, 1, tg, r] = -s_ * q
                qa[j, 2, tg, r] = s_ * 128.0
                qa[j, 3, tg, r] = s_
    t["qaug"] = qa.reshape(NSLOT, 4, 2048)
    qs = np.zeros((NSLOT, 3, 2, 4, 128), np.float32)
    for j in range(NSLOT):
        qpos = (2 * j + p) * 128 + q
        for g in range(2):
            for r in range(4):
                s_ = sl[g * 4 + r]
                qs[j, 0, g, r] = -s_ * qpos
                qs[j, 1, g, r] = s_ * 128.0
                qs[j, 2, g, r] = s_
    t["qaugs"] = qs.reshape(NSLOT, 3, 1024)
    k = np.arange(128)[:, None]
    qq = np.arange(128)[None, :]
    masks = np.zeros((10, 128, 128), np.float32)
    for i in range(2):
        dist = (p - i) * 128 + qq - k
        masks[i] = np.where(dist >= 0, 0.0, NEGM)
    for n, i in enumerate((0, 1, 4, 5)):
        dist = (p - i + 4) * 128 + qq - k
        masks[2 + n] = np.where((dist >= 0) & (dist < 512), 0.0, NEGM)
    for i in range(3):
        dist = (p - i + 1) * 128 + qq - k
        masks[6 + i] = np.where((dist >= 0) & (dist < 128), 0.0, NEGM)
    t["masks"] = masks.transpose(1, 0, 2).copy()
    cm = np.zeros((NSLOT, 2, 128, 128), np.float32)
    for j in range(NSLOT):
        qpos = (2 * j + p) * 128 + qq
        for cc in range(2):
            c = cc * 128 + k
            ok = (qpos >= 16 * c + 31) & (c < 255)
            cm[j, cc] = np.where(ok, 0.0, NEGM)
    t["cmask"] = cm
    cf = np.zeros((NSLOT, 2, 128, 64), np.float32)
    jj = np.arange(64)[None, :]
    for j in range(NSLOT):
        qpos = (2 * j + p) * 128 + np.arange(128)[:, None]
        cur = qpos // 64
        causal = jj <= cur
        fb = np.where(causal, 0.0, -1e30)
        fb = np.where(causal & (jj == 0), 1e30, fb)
        fb = np.where(causal & (jj == cur - 1), 2e30, fb)
        fb = np.where(causal & (jj == cur), 4e30, fb)
        cf[j, 0] = causal.astype(np.float32)
        cf[j, 1] = fb
    t["cmfb"] = cf
    return t


def build_nc(debug=None, stop_after=None):
    debug = debug or []
    nc = bass.Bass("TRN2", target_bir_lowering=False)
    P = Prog()
    st = ExitStack()

    def din(name, shape):
        return nc.dram_tensor(name, list(shape), F32, kind="ExternalInput").ap()

    xf = din("xf", [S, D])
    xo = din("xo", [NSLOT * 128, D])
    cT_d = din("cT", [128, 8])
    w_ada = din("w_ada", [D, 3 * D])
    badaT = din("badaT", [128, 24])
    gpreT = din("gpreT", [128, 8])
    gpostT = din("gpostT", [128, 8])
    w_in = din("w_in", [D, 5144])
    pekT = din("pekT", [64, 64])
    pevT = din("pevT", [64, 64])
    w1k_d = din("w1k", [2048, 256])
    w2k_d = din("w2k", [256, 64])
    w1v_d = din("w1v", [2048, 256])
    w2v_d = din("w2v", [256, 64])
    wona_d = din("w_o_nsa", [512, D])
    wosw_d = din("w_o_swa", [512, D])
    wout_d = din("w_out", [D, D])
    sinks_d = din("sinksb", [128, 8])
    ident_d = din("ident", [128, 128])
    kaug_d = din("kaug", [4, S])
    caug_d = din("caug", [4, 256])
    mimp_d = din("mimp", [256, 64])
    ksel_d = din("ksel", [64, S])
    qaugs_d = din("qaugs", [NSLOT, 3, 1024])
    qaug_d = din("qaug", [NSLOT, 4, 2048])
    masks_d = din("masks", [128, 10, 128])
    cmask_d = din("cmask", [NSLOT, 2, 128, 128])
    cmfb_d = din("cmfb", [NSLOT, 2, 128, 64])
    out_d = nc.dram_tensor("out", [NSLOT * 128, D], F32, kind="ExternalOutput").ap()
    out_bufs = [P.buf("out%d" % i) for i in range(5)]
    dbg_out = {}

    A = Arena(nc, st, 212800)
    ps = st.enter_context(nc.psum_tensor("ps", [128, 4096], F32))
    bank = [ps[:, i * 512:(i + 1) * 512] for i in range(8)]
    bankb = [bank[i].bitcast(BF16) for i in range(8)]
    PB = [P.buf("bank%d" % i) for i in range(8)]

    def dma(eng, out, in_, b, reads=(), part=False):
        if part:
            return P.add(eng, lambda e: e.dma_start(out=out, in_=in_), reads=reads, pwrites=[b], dma=b)
        return P.add(eng, lambda e: e.dma_start(out=out, in_=in_), reads=reads, writes=[b], dma=b)

    def dump(name, ap, b, shape):
        if name not in debug:
            return
        dt = nc.dram_tensor("dbg_" + name, list(shape), ap.dtype, kind="ExternalOutput").ap()
        db = P.buf("dbg_" + name)
        P.add("sp", lambda e: e.dma_start(out=dt, in_=ap), reads=[b], writes=[db], dma=db)
        dbg_out[name] = db

    ident_f = A.alloc([128, 128], F32)
    ident_b = A.alloc([128, 128], BF16)
    ones_f = A.alloc([128, 128], F32)
    cols = A.alloc([128, 96], F32)
    B_const = P.buf("const")
    B_identf = P.buf("identf")
    B_ones = P.buf("ones")
    B_cols = P.buf("cols")
    B_eps = P.buf("eps")
    epsT = A.alloc([128, 8], F32)
    dma("sp", ident_f, ident_d[:, :], B_identf)
    dma("pool", ident_b, ident_d[:, :], B_const)
    P.add("pool", lambda e: e.memset(ones_f, 1.0), writes=[B_ones])
    P.add("pool", lambda e: e.memset(epsT, 1e-6), writes=[B_eps])
    P.add("pool", lambda e: e.memset(epsT[:, 1:2], 16e-6), writes=[B_eps])
    P.add("pool", lambda e: e.memset(cols, 0.0), writes=[B_cols])
    dma("sp", cols[:, 0:8], cT_d[:, :], B_cols)
    dma("sp", cols[:, 8:32], badaT[:, :], B_cols, part=True)
    dma("sp", cols[:, 32:40], gpreT[:, :], B_cols, part=True)
    dma("sp", cols[:, 40:48], gpostT[:, :], B_cols, part=True)
    sinks_t = A.alloc([128, 8], F32)
    B_sinks = P.buf("sinks")
    dma("sp", sinks_t, sinks_d[:, :], B_sinks)

    A_scr = A.alloc([128, 8], F32)
    B_scr = P.buf("scr")
    junk = A.alloc([128, 1024], BF16)
    B_junk = P.buf("junk")
    XR = A.alloc([128, 16384], BF16)
    kv_mark = A.mark()
    KT = [[A.alloc([128, S], BF16) for g in range(2)] for t in range(3)]
    B_KT = [[P.buf("KT%d%d" % (t, g)) for g in range(2)] for t in range(3)]
    Vt = A.alloc([128, 32, 6, 65], BF16)
    B_Vt = P.buf("Vt")
    KC = [A.alloc([128, 256], BF16) for g in range(2)]
    B_KC = [P.buf("KC%d" % g) for g in range(2)]
    VC = A.alloc([128, 2, 2, 129], BF16)
    B_VC = P.buf("VC")
    for t in range(3):
        for g in range(2):
            if t == 0:
                dma("pool", KT[t][g][0:64, :], ksel_d[:, :], B_KT[t][g], part=True)
            else:
                dma("pool", KT[t][g][64:68, :], kaug_d[:, :], B_KT[t][g], part=True)
    for g in range(2):
        P.add("pool", lambda e, g=g: e.memset(KC[g][0:64, :], 0.0), pwrites=[B_KC[g]])
        dma("pool", KC[g][64:68, :], caug_d[:, :], B_KC[g], part=True)
    P.add("pool", lambda e: e.memset(Vt[:, :, :, 64:65], 1.0), pwrites=[B_Vt])
    P.add("pool", lambda e: e.memset(VC[:, :, :, 64:65], 1.0), pwrites=[B_VC])
    for g in range(2):
        dma("pool", VC[:, :, g, 65:129], mimp_d.rearrange("(cc p) j -> p cc j", p=128), B_VC, part=True)

    small_mark = A.mark()
    if stop_after == "consts":
        dump("VC", VC, B_VC, [128, 2, 2, 129])
        dump("KC0", KC[0][0:68, :], B_KC[0], [68, 256])
        P.emit(nc, final_waits=list(dbg_out.values()))
        return nc

    wst = [A.alloc([128, 1536], F32) for _ in range(2)]
    B_wst = [P.buf("wst%d" % i) for i in range(2)]
    first = True
    for kc in range(8):
        for hf in range(2):
            i = (kc * 2 + hf) % 2
            dma("sp", wst[i], w_ada[kc * 128:(kc + 1) * 128, hf * 1536:(hf + 1) * 1536], B_wst[i])
            for n in range(12):
                col = hf * 12 + n
                P.add("pe", lambda e, i=i, n=n, col=col, kc=kc, first=first: e.matmul(
                    bank[0][:, col:col + 1], lhsT=wst[i][:, n * 128:(n + 1) * 128], rhs=cols[:, kc:kc + 1],
                    start=first, stop=(kc == 7), skip_group_check=True),
                    reads=[B_wst[i], B_cols], writes=[PB[0]])
                first = False
    P.add("dve", lambda e: e.tensor_tensor(out=cols[:, 48:72], in0=bank[0][:, 0:24], in1=cols[:, 8:32], op=ALU.add),
          reads=[B_cols], writes=[PB[0]], pwrites=[B_cols])
    P.add("dve", lambda e: e.scalar_tensor_tensor(out=cols[:, 72:80], in0=cols[:, 56:64], scalar=1.0, in1=cols[:, 32:40],
                                                  op0=ALU.add, op1=ALU.mult), reads=[B_cols], writes=[B_cols])
    P.add("dve", lambda e: e.tensor_tensor(out=cols[:, 80:88], in0=cols[:, 64:72], in1=cols[:, 40:48], op=ALU.mult),
          reads=[B_cols], writes=[B_cols])
    dump("cols", cols, B_cols, [128, 96])
    if stop_after == "phase0":
        P.emit(nc, final_waits=list(dbg_out.values()))
        return nc

    def rms_a(xt, B_xt, hn, B_hn):
        sc = A_scr
        P.add("act", lambda e: e.activation(out=junk, in_=xt, func=AF.Square, accum_out=sc[:, 0:1]),
              reads=[B_xt], writes=[B_junk, B_scr])
        P.add("act", lambda e: e.activation(out=sc[:, 1:2], in_=sc[:, 0:1], func=AF.Ln, scale=1.0 / D, bias=epsT[:, 0:1]),
              reads=[B_eps], writes=[B_scr])
        P.add("act", lambda e: e.activation(out=sc[:, 2:3], in_=sc[:, 1:2], func=AF.Exp, scale=-0.5), writes=[B_scr])
        P.add("dve", lambda e: e.tensor_scalar(out=hn, in0=xt, scalar1=sc[:, 2:3], scalar2=None, op0=ALU.mult),
              reads=[B_xt, B_scr], writes=[B_hn])

    def rms_b(hn, B_hn, hT_dst, B_hT, tbanks):
        for c in range(8):
            tb = tbanks[c // 4]
            P.add("pe", lambda e, c=c, tb=tb: e.transpose(out=bankb[tb][:, (c % 4) * 128:(c % 4 + 1) * 128], in_=hn[:, c * 128:(c + 1) * 128],
                                                         identity=ident_b), reads=[B_hn, B_const], writes=[PB[tb]])
        for c in range(8):
            tb = tbanks[c // 4]
            src = bankb[tb][:, (c % 4) * 128:(c % 4 + 1) * 128]
            if c >= 4:
                P.add("act", lambda e, c=c, src=src: e.activation(out=hT_dst(c), in_=src, func=AF.Identity,
                                                                  scale=cols[:, 72 + c:73 + c], bias=cols[:, 48 + c:49 + c]),
                      reads=[B_cols], xreads=[PB[tb]], pwrites=[B_hT[1]])
            else:
                P.add("dve", lambda e, c=c, src=src: e.tensor_scalar(out=hT_dst(c), in0=src,
                                                                     scalar1=cols[:, 72 + c:73 + c], scalar2=cols[:, 48 + c:49 + c],
                                                                     op0=ALU.mult, op1=ALU.add),
                      reads=[B_cols], xreads=[PB[tb]], pwrites=[B_hT[0]])

    wkv = A.alloc([128, 8, 1024], BF16)
    B_wkv = P.buf("wkv")
    kvcols = [512, 640, 768, 1024, 2328, 896, 1152, 2456]
    for i, c0 in enumerate(kvcols):
        dma("pool", wkv[:, :, i * 128:(i + 1) * 128], w_in[:, c0:c0 + 128].rearrange("(kc p) n -> p kc n", p=128), B_wkv, part=True)
    xt = [A.alloc([128, 1024], F32) for _ in range(2)]
    B_xt = [P.buf("xt%d" % i) for i in range(2)]
    hn = [A.alloc([128, 1024], BF16) for _ in range(2)]
    B_hn = [P.buf("hn%d" % i) for i in range(2)]
    hTc = [A.alloc([128, 8, 512], BF16) for _ in range(2)]
    B_hTc = [(P.buf("hTc%da" % i), P.buf("hTc%db" % i)) for i in range(2)]
    rawk = A.alloc([128, S], BF16)
    rawv = A.alloc([128, S], BF16)
    B_rawk, B_rawv = P.buf("rawk"), P.buf("rawv")
    rawk3 = rawk.rearrange("p (r m) -> p r m", r=16)
    rawv3 = rawv.rearrange("p (r m) -> p r m", r=16)
    w1k = XR[:, 0:8192].rearrange("p (a b) -> p a b", a=32)
    w1v = XR[:, 8192:16384].rearrange("p (a b) -> p a b", a=32)
    w2k = A.alloc([128, 2, 64], BF16)
    w2v = A.alloc([128, 2, 64], BF16)
    peT = A.alloc([128, 2, 64], BF16)
    B_w1 = P.buf("w1")
    for hh in range(2):
        dma("pool", w1k[hh * 64:(hh + 1) * 64], w1k_d.rearrange("(l d) h -> d l h", d=64), B_w1, part=True)
        dma("pool", w1v[hh * 64:(hh + 1) * 64], w1v_d.rearrange("(l d) h -> d l h", d=64), B_w1, part=True)
    dma("pool", w2k, w2k_d.rearrange("(hc p) d -> p hc d", p=128), B_w1, part=True)
    dma("pool", w2v, w2v_d.rearrange("(hc p) d -> p hc d", p=128), B_w1, part=True)
    dma("pool", peT[0:64, 0, :], pekT[:, :], B_w1, part=True)
    dma("pool", peT[0:64, 1, :], pevT[:, :], B_w1, part=True)

    def p1_a(tb):
        xi = tb % 2
        dma("sp", xt[xi], xf[tb * 128:(tb + 1) * 128, :], B_xt[xi])
        rms_a(xt[xi], B_xt[xi], hn[xi], B_hn[xi])

    def kgroup(ch, i):
        hb = ch % 2
        pb = 1 + (i % 2)
        for kc in range(8):
            P.add("pe", lambda e, i=i, kc=kc, pb=pb, hb=hb: e.matmul(
                bank[pb], lhsT=wkv[:, kc, i * 128:(i + 1) * 128], rhs=hTc[hb][:, kc, :], start=(kc == 0), stop=(kc == 7)),
                reads=[B_wkv, *B_hTc[hb]], writes=[PB[pb]])
        csl = slice(ch * 512, (ch + 1) * 512)
        if i == 0:
            P.add("act", lambda e, pb=pb, ch=ch: e.copy(out=rawk3[:, :, ch * 32:(ch + 1) * 32], in_=bank[pb].rearrange("p (m r) -> p r m", r=16)),
                  xreads=[PB[pb]], pwrites=[B_rawk])
        elif i == 1:
            P.add("dve", lambda e, pb=pb, ch=ch: e.tensor_copy(out=rawv3[:, :, ch * 32:(ch + 1) * 32], in_=bank[pb].rearrange("p (m r) -> p r m", r=16)),
                  xreads=[PB[pb]], pwrites=[B_rawv])
        else:
            t = i - 2
            dlo = 64 if t == 0 else 0
            P.add("act", lambda e, pb=pb, csl=csl, t=t, dlo=dlo: e.copy(out=KT[t][0][dlo:dlo + 64, csl], in_=bank[pb][0:64, :]),
                  xreads=[PB[pb]], pwrites=[B_KT[t][0]])
            P.add("dve", lambda e, pb=pb, csl=csl, t=t, dlo=dlo: e.tensor_copy(out=KT[t][1][dlo:dlo + 64, csl], in_=bank[pb][64:128, :]),
                  xreads=[PB[pb]], pwrites=[B_KT[t][1]])

    def vgroup(ch, bi):
        hb = ch % 2
        tb = ch * 4 + bi
        for kc in range(8):
            P.add("pe", lambda e, kc=kc, hb=hb, bi=bi: e.matmul(
                bank[3][:, 0:384], lhsT=hTc[hb][:, kc, bi * 128:(bi + 1) * 128], rhs=wkv[:, kc, 640:1024],
                start=(kc == 0), stop=(kc == 7)), reads=[B_wkv, *B_hTc[hb]], writes=[PB[3]])
        if bi % 2 == 0:
            P.add("act", lambda e, tb=tb: e.copy(out=Vt[:, tb, :, 0:64], in_=bank[3][:, 0:384].rearrange("p (a b) -> p a b", a=6)),
                  xreads=[PB[3]], pwrites=[B_Vt])
        else:
            P.add("dve", lambda e, tb=tb: e.tensor_copy(out=Vt[:, tb, :, 0:64], in_=bank[3][:, 0:384].rearrange("p (a b) -> p a b", a=6)),
                  xreads=[PB[3]], pwrites=[B_Vt])

    def block(tb):
        ch, bi = tb // 4, tb % 4
        hb, xi = ch % 2, tb % 2
        if tb + 1 < 32:
            p1_a(tb + 1)
        rms_b(hn[xi], B_hn[xi], lambda c, hb=hb, bi=bi: hTc[hb][:, c, bi * 128:(bi + 1) * 128], B_hTc[hb], (0, 7))

    p1_a(0)
    for ch in range(9):
        groups = []
        if ch >= 1:
            groups = [lambda i=i, c_=ch - 1: kgroup(c_, i) for i in range(5)] + [lambda b_=b_, c_=ch - 1: vgroup(c_, b_) for b_ in range(4)]
        blocks = [lambda tb=ch * 4 + bi: block(tb) for bi in range(4)] if ch < 8 else []
        order = []
        gi = 0
        for bi in range(4):
            order += groups[gi:gi + 2]
            gi += 2
            if bi < len(blocks):
                order.append(blocks[bi])
        order += groups[gi:]
        for f_ in order:
            f_()

    if stop_after == "phase1a":
        dump("KTs0", KT[0][0][0:68, :], B_KT[0][0], [68, S])
        dump("Vt", Vt, B_Vt, [128, 32, 6, 65])
        P.emit(nc, final_waits=list(dbg_out.values()))
        return nc
    hid = A.alloc([128, 2, 256], BF16)
    B_hid = P.buf("hid")
    cu = A.alloc([128, 256], F32)
    ce = A.alloc([128, 256], F32)
    B_cu, B_ce = P.buf("cu"), P.buf("ce")
    cb = A.alloc([128, 8], F32)
    B_cb = P.buf("cb")
    P.add("pool", lambda e: e.memset(hid, 0.0), writes=[B_hid])
    for kv in range(2):
        w1 = w1k if kv == 0 else w1v
        for hc in range(2):
            q4 = kv * 2 + hc
            for l in range(32):
                P.add("pe", lambda e, l=l, hc=hc, w1=w1, kv=kv, q4=q4: e.matmul(
                    bank[7][:, q4 * 32:(q4 + 1) * 32], lhsT=w1[0:64, l, hc * 128:(hc + 1) * 128],
                    rhs=peT[0:64, kv, 2 * l:2 * l + 1].to_broadcast([64, 32]), start=(l == 0), stop=(l == 31)), reads=[B_w1], writes=[PB[7]])
    P.add("dve", lambda e: e.tensor_copy(out=cb[:, 0:4], in_=bank[7][:, 0:128:32]), writes=[PB[7], B_cb])
    P.add("dve", lambda e: e.tensor_scalar(out=cb[:, 4:8], in0=cb[:, 0:4], scalar1=0.5, scalar2=None, op0=ALU.mult), writes=[B_cb])
    for kv in range(2):
        w1 = w1k if kv == 0 else w1v
        w2 = w2k if kv == 0 else w2v
        raw, B_raw = (rawk3, B_rawk) if kv == 0 else (rawv3, B_rawv)
        for g in range(2):
            for hc in range(2):
                pb = 4 + hc
                for l in range(32):
                    P.add("pe", lambda e, l=l, g=g, hc=hc, w1=w1, raw=raw, pb=pb: e.matmul(
                        bank[pb][:, 0:255], lhsT=w1[g * 64:(g + 1) * 64, l, hc * 128:(hc + 1) * 128],
                        rhs=raw[g * 64:(g + 1) * 64, l % 16, (l // 16):(l // 16) + 255], start=(l == 0), stop=(l == 31)),
                        reads=[B_w1, B_raw], writes=[PB[pb]])
                q4 = kv * 2 + hc
                P.add("act", lambda e, pb=pb, q4=q4: e.activation(out=ce[:, 0:255], in_=bank[pb][:, 0:255], func=AF.Tanh, scale=0.5, bias=cb[:, 4 + q4:5 + q4]),
                      reads=[B_cb], xreads=[PB[pb]], writes=[B_ce])
                P.add("dve", lambda e, pb=pb, q4=q4: e.tensor_scalar(out=cu[:, 0:255], in0=bank[pb][:, 0:255], scalar1=cb[:, q4:q4 + 1], scalar2=None, op0=ALU.add),
                      reads=[B_cb], xreads=[PB[pb]], writes=[B_cu])
                P.add("dve", lambda e, hc=hc: e.scalar_tensor_tensor(out=hid[:, hc, 0:255], in0=ce[:, 0:255], scalar=1.0, in1=cu[:, 0:255], op0=ALU.add, op1=ALU.mult),
                      reads=[B_cu, B_ce], writes=[B_hid])
            if kv == 0:
                for hc in range(2):
                    P.add("pe", lambda e, hc=hc, w2=w2: e.matmul(bank[6][0:64, 0:256], lhsT=w2[:, hc, :], rhs=hid[:, hc, :],
                                                                start=(hc == 0), stop=(hc == 1)), reads=[B_w1, B_hid], writes=[PB[6]])
                P.add("dve", lambda e, g=g: e.tensor_scalar(out=KC[g][0:64, :], in0=bank[6][0:64, 0:256], scalar1=0.5, scalar2=None, op0=ALU.mult),
                      writes=[PB[6]], pwrites=[B_KC[g]])
            else:
                for cc in range(2):
                    for hc in range(2):
                        P.add("pe", lambda e, hc=hc, cc=cc, w2=w2: e.matmul(bank[6][:, cc * 64:(cc + 1) * 64], lhsT=hid[:, hc, cc * 128:(cc + 1) * 128],
                                                                            rhs=w2[:, hc, :], start=(hc == 0), stop=(hc == 1)),
                              reads=[B_w1, B_hid], writes=[PB[6]])
                P.add("dve", lambda e, g=g: e.tensor_scalar(out=VC[:, :, g, 0:64], in0=bank[6][:, 0:128].rearrange("p (a b) -> p a b", a=2),
                                                            scalar1=0.5, scalar2=None, op0=ALU.mult), writes=[PB[6]], pwrites=[B_VC])
    if debug:
        P.barrier()
    dump("KTs0", KT[0][0][0:68, :], B_KT[0][0], [68, S])
    dump("KTb1", KT[2][1][0:68, :], B_KT[2][1], [68, S])
    dump("Vt", Vt, B_Vt, [128, 32, 6, 65])
    dump("KC0", KC[0][0:68, :], B_KC[0], [68, 256])
    dump("VC", VC, B_VC, [128, 2, 2, 129])
    if stop_after == "phase1":
        P.emit(nc, final_waits=list(dbg_out.values()))
        return nc

    P.barrier()
    A.release(small_mark)
    OZT = XR.rearrange("p (a b c d) -> p a b c d", a=NSLOT, b=2, c=4)
    B_OZT = P.buf("OZT")
    masks = A.alloc([128, 10, 128], BF16)
    B_masks = P.buf("masks")
    dma("pool", masks, masks_d[:, :, :], B_masks)
    wq = A.alloc([128, 8, 2072], BF16)
    B_wq = [P.buf("wq%d" % i) for i in range(5)]
    for wi, (dst, src, n) in enumerate(((0, 0, 512), (512, 1816, 512), (1024, 1304, 512), (1536, 2584, 512), (2048, 1280, 24))):
        dma("pool", wq[:, :, dst:dst + n], w_in[:, src:src + n].rearrange("(kc p) n -> p kc n", p=128), B_wq[wi])
    xq1 = A.alloc([128, 1024], F32)
    xq = [xq1, xq1]
    B_xq1 = P.buf("xq")
    B_xq = [B_xq1, B_xq1]
    hq1 = A.alloc([128, 1024], BF16)
    hq = [hq1, hq1]
    B_hq1 = P.buf("hq")
    B_hq = [B_hq1, B_hq1]
    hTq = [A.alloc([128, 8, 128], BF16) for _ in range(2)]
    B_hTq = [(P.buf("hTq%da" % i), P.buf("hTq%db" % i)) for i in range(2)]
    QT = [A.alloc([128, 4, 512], BF16) for _ in range(2)]
    B_QT = [P.buf("QT%d" % i) for i in range(2)]
    B_QTa = [P.buf("QTa%d" % i) for i in range(2)]
    PT = [A.alloc([128, 1024], BF16) for _ in range(2)]
    B_PT = [P.buf("PT%d" % i) for i in range(2)]
    cmk = [A.alloc([128, 2, 128], BF16) for _ in range(2)]
    B_cmk = [P.buf("cmk%d" % i) for i in range(2)]
    cmfb = [A.alloc([128, 2, 64], F32) for _ in range(2)]
    B_cmfb = [P.buf("cmfb%d" % i) for i in range(2)]
    QS = [[A.alloc([128, 512], BF16) for _ in range(2)] for _ in range(2)]
    B_QSq = [[P.buf("QSq%d%d" % (i, g)) for g in range(2)] for i in range(2)]
    B_QSa = [[P.buf("QSa%d%d" % (i, g)) for g in range(2)] for i in range(2)]
    B_QSs = [[P.buf("QSs%d%d" % (i, g)) for g in range(2)] for i in range(2)]
    zs = [[A.alloc([128, 512], F32) for _ in range(2)] for _ in range(2)]
    B_zs = [[P.buf("zs%d%d" % (i, k)) for k in range(2)] for i in range(2)]
    ze = A.alloc([128, 512], F32)
    B_ze = P.buf("ze")
    sg = [A.alloc([128, 24], F32) for _ in range(2)]
    B_sg = [P.buf("sg%d" % i) for i in range(2)]
    oa = A.alloc([128, 2, 4, 64], F32)
    ob = A.alloc([128, 2, 4, 64], F32)
    B_oa = [P.buf("oa0"), P.buf("oa1")]
    B_ob = [P.buf("ob0"), P.buf("ob1")]
    otmp = A.alloc([128, 4, 64], F32)
    B_otmp = P.buf("otmp")
    ozb = [A.alloc([128, 512], BF16) for _ in range(2)]
    B_ozb = [P.buf("ozb0"), P.buf("ozb1")]
    sm = A.alloc([128, 64], F32)
    B_sm = P.buf("sm")
    imp = A.alloc([128, 64], F32)
    imp4 = A.alloc([128, 256], F32)
    B_imp4 = P.buf("imp4")
    sc1 = A.alloc([128, 64], F32)
    sc2 = A.alloc([128, 64], F32)
    selb = [A.alloc([128, 64], BF16) for _ in range(2)]
    B_imp, B_sc1, B_sc2 = P.buf("imp"), P.buf("sc1"), P.buf("sc2")
    B_selb = [P.buf("selb0"), P.buf("selb1")]
    es = A.alloc([128, 8], F32)
    B_es = P.buf("es")
    woa = A.alloc([128, 4, 1024], BF16)
    B_woa = P.buf("woa")
    wo_top = A.mark()
    dma("pool", woa, wona_d.rearrange("(c p) n -> p c n", p=128), B_woa)
    P.add("act", lambda e: e.activation(out=es, in_=sinks_t, func=AF.Exp), reads=[B_sinks], writes=[B_es])

    st_pairs = [(3, 4), (5, 6)]
    st_ctr = [0]
    o_ctr = [0]
    pending = []

    delayed = []

    def flush(all_=False):
        fs = list(pending)
        del pending[:]
        for f_ in fs:
            f_()
        keep = []
        for item in list(delayed):
            item[0] -= 1
            if item[0] <= 0 or all_:
                item[1]()
            else:
                keep.append(item)
        delayed[:] = keep

    def O3v(obk):
        return bank[obk][:, 0:260].rearrange("p (a b) -> p a b", a=4)

    def attention(sp, tg, Kt, B_K, kblocks, v_of, B_V, selg, addmask, B_am, extra_rhs=None):
        obk = (7, 2)[o_ctr[0] % 2]
        o_ctr[0] += 1
        n = len(kblocks)
        first_pv = [True]
        for i0 in range(0, n, 2):
            grp = kblocks[i0:i0 + 2]
            pr = st_pairs[st_ctr[0] % 2]
            pt = st_ctr[0] % 2
            st_ctr[0] += 1
            for ii, kb in enumerate(grp):
                bk = pr[ii]
                am = addmask(kb) if addmask is not None else None
                nmm = 1 + (1 if am is not None else 0)
                if selg is None:
                    P.add("pe", lambda e, bk=bk, kb=kb, last=(nmm == 1): e.matmul(
                        bank[bk], lhsT=Kt[0:68, kb * 128:(kb + 1) * 128], rhs=QT[sp][0:68, tg, :], start=True, stop=last),
                        reads=[B_K, B_QT[sp], B_QTa[sp]], writes=[PB[bk]])
                else:
                    P.add("pe", lambda e, bk=bk, kb=kb, last=(nmm == 1): e.matmul(
                        bank[bk], lhsT=Kt[:, kb * 128:(kb + 1) * 128], rhs=QS[sp][selg], start=True, stop=last),
                        reads=[B_K, B_QSq[sp][selg], B_QSa[sp][selg], B_QSs[sp][selg]], writes=[PB[bk]])
                if am is not None:
                    P.add("pe", lambda e, bk=bk, am=am: e.matmul(
                        bank[bk].rearrange("p (a b) -> p a b", a=4), lhsT=ident_b,
                        rhs=am.unsqueeze(1).to_broadcast([128, 4, 128]), start=False, stop=True),
                        reads=[B_const, B_am], writes=[PB[bk]])
            w = len(grp) * 512
            b0 = pr[0]
            P.add("act", lambda e, b0=b0, w=w, pt=pt: e.activation(out=PT[pt][:, 0:w], in_=ps[:, b0 * 512:b0 * 512 + w], func=AF.Exp),
                  xreads=[PB[pr[ii]] for ii in range(len(grp))], writes=[B_PT[pt]])
            flush()

            def pv(grp=grp, i0=i0, pt=pt):
                for ii, kb in enumerate(grp):
                    lastk = (i0 + ii == n - 1)
                    for r in range(4):
                        P.add("pe", lambda e, ii=ii, kb=kb, r=r, fp=first_pv[0], lastk=lastk: e.matmul(
                            bank[obk][:, r * 65:(r + 1) * 65], lhsT=PT[pt][:, ii * 512 + r * 128: ii * 512 + (r + 1) * 128], rhs=v_of(kb),
                            start=fp, stop=lastk, skip_group_check=True), reads=[B_PT[pt], B_V], writes=[PB[obk]])
                        if extra_rhs is not None:
                            P.add("pe", lambda e, ii=ii, kb=kb, r=r, fp=first_pv[0], lastk=lastk: e.matmul(
                                bank[1][:, r * 64:(r + 1) * 64], lhsT=PT[pt][:, ii * 512 + r * 128: ii * 512 + (r + 1) * 128], rhs=extra_rhs(kb),
                                start=fp, stop=lastk, skip_group_check=True), reads=[B_PT[pt], B_V], writes=[PB[1]])
                        first_pv[0] = False
            pending.append(pv)
        return obk

    def finish(obk, sp, gate_col0, dst, B_dst, first_branch, extra_den=None):
        O3 = O3v(obk)
        if extra_den is not None:
            P.add("dve", lambda e: e.tensor_tensor(out=sm[:, 0:4], in0=O3[:, :, 64], in1=extra_den, op=ALU.add), reads=[B_es], xreads=[PB[obk]], writes=[B_sm])
        else:
            P.add("dve", lambda e: e.tensor_scalar(out=sm[:, 0:4], in0=O3[:, :, 64], scalar1=1e-30, scalar2=None, op0=ALU.max),
                  xreads=[PB[obk]], writes=[B_sm])
        P.add("dve", lambda e: e.reciprocal(out=sm[:, 4:8], in_=sm[:, 0:4]), writes=[B_sm])
        if gate_col0 is not None:
            P.add("dve", lambda e: e.tensor_tensor(out=sm[:, 8:12], in0=sm[:, 4:8], in1=sg[sp][:, gate_col0:gate_col0 + 4], op=ALU.mult),
                  reads=[B_sg[sp]], writes=[B_sm])
            fac = sm[:, 8:12]
        else:
            fac = sm[:, 4:8]
        facb = fac.unsqueeze(2).to_broadcast([128, 4, 64])
        if first_branch:
            P.add("dve", lambda e: e.tensor_tensor(out=dst, in0=O3[:, :, 0:64], in1=facb, op=ALU.mult),
                  reads=[B_sm], xreads=[PB[obk]], writes=[B_dst])
        else:
            P.add("dve", lambda e: e.tensor_tensor(out=otmp, in0=O3[:, :, 0:64], in1=facb, op=ALU.mult),
                  reads=[B_sm], xreads=[PB[obk]], writes=[B_otmp])
            P.add("dve", lambda e: e.tensor_tensor(out=dst, in0=dst, in1=otmp, op=ALU.add), reads=[B_otmp], writes=[B_dst])

    def importance(obk, sp, g):
        O3 = O3v(obk)
        P.add("dve", lambda e: e.tensor_scalar(out=sm[:, 16:20], in0=O3[:, :, 64], scalar1=1e-30, scalar2=None, op0=ALU.max),
              xreads=[PB[obk]], writes=[B_sm])
        P.add("dve", lambda e: e.reciprocal(out=sm[:, 20:24], in_=sm[:, 16:20]), writes=[B_sm])
        P.add("dve", lambda e: e.tensor_tensor(out=imp4.rearrange("p (a b) -> p a b", a=4), in0=bank[1][:, 0:256].rearrange("p (a b) -> p a b", a=4),
                                               in1=sm[:, 20:24].unsqueeze(2).to_broadcast([128, 4, 64]), op=ALU.mult),
              reads=[B_sm], xreads=[PB[1]], writes=[B_imp4])
        P.add("dve", lambda e: e.tensor_reduce(out=imp, in_=imp4.rearrange("p (r j) -> p j r", r=4), axis=mybir.AxisListType.X, op=ALU.add),
              reads=[B_imp4], writes=[B_imp])
        P.add("dve", lambda e: e.tensor_tensor(out=sc1, in0=imp, in1=cmfb[sp][:, 1, :], op=ALU.add), reads=[B_imp, B_cmfb[sp]], writes=[B_sc1])
        P.add("dve", lambda e: e.max(out=sm[:, 24:32], in_=sc1), reads=[B_sc1], writes=[B_sm])
        P.add("dve", lambda e: e.match_replace(out=sc2, in_to_replace=sm[:, 24:32], in_values=sc1, imm_value=-3e38),
              reads=[B_sc1, B_sm], writes=[B_sc2])
        P.add("dve", lambda e: e.max(out=sm[:, 32:40], in_=sc2), reads=[B_sc2], writes=[B_sm])
        P.add("dve", lambda e, g=g: e.tensor_scalar(out=selb[g][:, 0:61], in0=sc1[:, 1:62], scalar1=sm[:, 39:40], scalar2=NEGM, op0=ALU.is_lt, op1=ALU.mult),
              reads=[B_sc1, B_sm], writes=[B_selb[g]])

    def importance_b(g, sp):
        P.add("pe", lambda e, g=g: e.transpose(out=bankb[0][0:61, g * 128:(g + 1) * 128], in_=selb[g][:, 0:61], identity=ident_b),
              reads=[B_selb[g], B_const], writes=[PB[0]])
        P.add("dve", lambda e, g=g, sp=sp: e.tensor_copy(out=QS[sp][g][0:61, :].rearrange("p (a b) -> p a b", a=4),
                                                        in_=bankb[0][0:61, g * 128:(g + 1) * 128].unsqueeze(1).to_broadcast([61, 4, 128])),
              xreads=[PB[0]], writes=[B_QSs[sp][g]])

    def prologue_a(j):
        sp = j % 2
        dma("sp", xq[sp], xo[j * 128:(j + 1) * 128, :], B_xq[sp])
        rms_a(xq[sp], B_xq[sp], hq[sp], B_hq[sp])

    def prologue(j):
        sp = j % 2
        dma("pool", QT[sp][64:68, :, :], qaug_d[j].rearrange("r (a b) -> r a b", a=4), B_QTa[sp])
        for g in range(2):
            dma("pool", QS[sp][g][61:64, :], qaugs_d[j][:, g * 512:(g + 1) * 512], B_QSa[sp][g])
        dma("pool", cmk[sp], cmask_d[j].rearrange("c k q -> k c q"), B_cmk[sp])
        dma("sp", cmfb[sp], cmfb_d[j].rearrange("c q n -> q c n"), B_cmfb[sp])
        rms_b(hq[sp], B_hq[sp], lambda c, sp=sp: hTq[sp][:, c, :], B_hTq[sp], (0, 1))
        for tg in range(4):
            t, g = tg // 2, tg % 2
            pb = (1, 0)[tg % 2]
            for kc in range(8):
                for hp in range(2):
                    c0 = t * 512 + (g * 4 + 2 * hp) * 64
                    P.add("pe", lambda e, pb=pb, hp=hp, c0=c0, kc=kc, sp=sp: e.matmul(
                        bank[pb][:, hp * 128:(hp + 1) * 128], lhsT=wq[:, kc, c0:c0 + 128], rhs=hTq[sp][:, kc, :],
                        start=(kc == 0 and hp == 0), stop=(kc == 7), skip_group_check=True), reads=[B_wq[t], *B_hTq[sp]], writes=[PB[pb]])
            for odd in range(2):
                src = bank[pb][odd * 64:(odd + 1) * 64, 0:256].rearrange("p (a b) -> p a b", a=2)
                dstv = QT[sp][0:64, tg, :].rearrange("p (a two b) -> p a two b", a=2, two=2)[:, :, odd, :]
                P.add("dve", lambda e, src=src, dstv=dstv: e.tensor_scalar(out=dstv, in0=src, scalar1=0.125, scalar2=None, op0=ALU.mult),
                      xreads=[PB[pb]], pwrites=[B_QT[sp]])
                if t == 0:
                    dsts = QS[sp][g][64:128, :].rearrange("p (a two b) -> p a two b", a=2, two=2)[:, :, odd, :]
                    P.add("dve", lambda e, src=src, dsts=dsts: e.tensor_scalar(out=dsts, in0=src, scalar1=0.125, scalar2=None, op0=ALU.mult),
                          xreads=[PB[pb]], pwrites=[B_QSq[sp][g]])
        for zi in range(2):
            pb = (1, 0)[zi]
            for kc in range(8):
                P.add("pe", lambda e, pb=pb, zi=zi, kc=kc, sp=sp: e.matmul(
                    bank[pb], lhsT=hTq[sp][:, kc, :], rhs=wq[:, kc, 1024 + zi * 512:1536 + zi * 512], start=(kc == 0), stop=(kc == 7)),
                    reads=[B_wq[2 + zi], *B_hTq[sp]], writes=[PB[pb]])
            P.add("act", lambda e, pb=pb: e.activation(out=ze, in_=bank[pb], func=AF.Tanh, scale=0.5), xreads=[PB[pb]], writes=[B_ze])
            P.add("dve", lambda e, pb=pb, zi=zi, sp=sp: e.scalar_tensor_tensor(out=zs[sp][zi], in0=ze, scalar=1.0, in1=bank[pb],
                                                                              op0=ALU.add, op1=ALU.mult),
                  reads=[B_ze], xreads=[PB[pb]], writes=[B_zs[sp][zi]])
        for kc in range(8):
            P.add("pe", lambda e, kc=kc, sp=sp: e.matmul(bank[1][:, 0:24], lhsT=hTq[sp][:, kc, :], rhs=wq[:, kc, 2048:2072], start=(kc == 0), stop=(kc == 7)),
                  reads=[B_wq[4], *B_hTq[sp]], writes=[PB[1]])
        P.add("act", lambda e, sp=sp: e.activation(out=sg[sp], in_=bank[1][:, 0:24], func=AF.Exp, scale=-1.0), xreads=[PB[1]], writes=[B_sg[sp]])
        P.add("dve", lambda e, sp=sp: e.tensor_scalar(out=sg[sp], in0=sg[sp], scalar1=1.0, scalar2=None, op0=ALU.add), writes=[B_sg[sp]])
        P.add("dve", lambda e, sp=sp: e.reciprocal(out=sg[sp], in_=sg[sp]), writes=[B_sg[sp]])

    nslots = NSLOT if stop_after is None or not stop_after.startswith("slot") else int(stop_after[4:])
    prologue_a(0)
    prologue(0)
    wmi = {0: 2, 1: 3, 4: 4, 5: 5}
    for j in range(nslots):
        sp = j % 2
        if j + 1 < nslots:
            prologue_a(j + 1)
        chunks = [0] if j <= 7 else [0, 1]

        def cm_mask(cc, j=j, sp=sp):
            if cc == 0 and j >= 9:
                return None
            return cmk[sp][:, cc, :]
        for g in range(2):
            obk = attention(sp, 0 * 2 + g, KC[g], B_KC[g], chunks, lambda cc, g=g: VC[:, cc, g, 0:65], B_VC, None, cm_mask, B_cmk[sp],
                            extra_rhs=lambda cc, g=g: VC[:, cc, g, 65:129])
            pending.append(lambda obk=obk, sp=sp, g=g: (importance(obk, sp, g), delayed.append([2, lambda: importance_b(g, sp)])))
            pending.append(lambda obk=obk, sp=sp, g=g: finish(obk, sp, 0 * 8 + g * 4, oa[:, g], B_oa[g], True))
        for g in range(2):
            kbs = [kb for kb in range(2 * j - 4, 2 * j + 2) if kb >= 0]
            obk = attention(sp, 0 * 2 + g, KT[1][g], B_KT[1][g], kbs, lambda kb, g=g: Vt[:, kb, 1 * 2 + g, :], B_Vt, None,
                            lambda kb, j=j: masks[:, wmi[kb - (2 * j - 4)], :] if (kb - (2 * j - 4)) in wmi else None, B_masks)
            pending.append(lambda obk=obk, sp=sp, g=g: finish(obk, sp, 2 * 8 + g * 4, oa[:, g], B_oa[g], False))
        if j + 1 < nslots:
            flush()
            prologue(j + 1)
        for g in range(2):
            kbs = [kb for kb in range(2 * j - 1, 2 * j + 2) if kb >= 0]
            obk = attention(sp, 1 * 2 + g, KT[2][g], B_KT[2][g], kbs, lambda kb, g=g: Vt[:, kb, 2 * 2 + g, :], B_Vt, None,
                            lambda kb, j=j: masks[:, 6 + kb - (2 * j - 1), :], B_masks)
            pending.append(lambda obk=obk, sp=sp, g=g: finish(obk, sp, None, ob[:, g], B_ob[g], True, extra_den=es[:, g * 4:(g + 1) * 4]))
        flush(all_=True)
        for g in range(2):
            obk = attention(sp, 0 * 2 + g, KT[0][g], B_KT[0][g], list(range(2 * j + 2)), lambda kb, g=g: Vt[:, kb, 0 * 2 + g, :], B_Vt, g,
                            lambda kb, j=j: masks[:, kb - 2 * j, :] if kb >= 2 * j else None, B_masks)
            pending.append(lambda obk=obk, sp=sp, g=g: finish(obk, sp, 1 * 8 + g * 4, oa[:, g], B_oa[g], False))
        flush()
        for mi, (osrc, B_os) in enumerate(((oa, B_oa), (ob, B_ob))):
            P.add("dve", lambda e, osrc=osrc, mi=mi, sp=sp: e.tensor_tensor(out=ozb[mi], in0=osrc.rearrange("p a b c -> p (a b c)"), in1=zs[sp][mi], op=ALU.mult),
                  reads=[*B_os, B_zs[sp][mi]], writes=[B_ozb[mi]])

        def epi_b(j=j):
            for mi in range(2):
                for c in range(4):
                    P.add("pe", lambda e, c=c, mi=mi: e.transpose(out=bankb[0][:, c * 128:(c + 1) * 128], in_=ozb[mi][:, c * 128:(c + 1) * 128], identity=ident_b),
                          reads=[B_ozb[mi], B_const], writes=[PB[0]])
                P.add("dve", lambda e, j=j, mi=mi: e.tensor_copy(out=OZT[:, j, mi, :, :], in_=bankb[0][:, 0:512].rearrange("p (a b) -> p a b", a=4)),
                      xreads=[PB[0]], pwrites=[B_OZT])
        pending.append(epi_b)
    flush()
    dump("OZT", OZT, B_OZT, [128, NSLOT, 2, 4, 128])
    if stop_after is not None and (stop_after == "sweepA" or stop_after.startswith("slot")):
        P.emit(nc, final_waits=list(dbg_out.values()))
        return nc

    P.barrier()
    A.release(kv_mark)
    ggbc = A.alloc([128, 1024], F32)
    B_gg = P.buf("ggbc")
    dg = A.alloc([128, 128], F32)
    B_dg = P.buf("dg")
    for c in range(8):
        P.add("dve", lambda e, c=c: e.tensor_scalar(out=dg, in0=ident_f, scalar1=cols[:, 80 + c:81 + c], scalar2=None, op0=ALU.mult),
              reads=[B_identf, B_cols], writes=[B_dg])
        pb = c // 4
        P.add("pe", lambda e, c=c, pb=pb: e.matmul(bank[pb][:, (c % 4) * 128:(c % 4 + 1) * 128], lhsT=ones_f, rhs=dg, start=True, stop=True,
                                                   skip_group_check=True), reads=[B_ones, B_dg], writes=[PB[pb]])
    P.add("dve", lambda e: e.tensor_copy(out=ggbc, in_=ps[:, 0:1024]), writes=[PB[0], PB[1], B_gg])
    wm = A.alloc([128, 8, 2048], BF16)
    wob = A.alloc([128, 4, 1024], BF16)
    wout = A.alloc([128, 8, 1024], BF16)
    B_wm = [P.buf("wm%d" % i) for i in range(4)]
    B_wob, B_wout = P.buf("wob"), P.buf("wout")
    xb_ = [A.alloc([128, 1024], F32) for _ in range(5)]
    B_xb = [P.buf("xb%d" % i) for i in range(5)]
    hq2 = [A.alloc([128, 1024], BF16) for _ in range(2)]
    B_hq2 = [P.buf("hq2%d" % i) for i in range(2)]
    hTq2 = [A.alloc([128, 8, 128], BF16) for _ in range(2)]
    B_hTq2 = [(P.buf("hTq2a%d" % i), P.buf("hTq2b%d" % i)) for i in range(2)]
    sig = [[A.alloc([128, 1024], F32) for _ in range(2)] for _ in range(2)]
    B_sig = [[P.buf("sig%d%d" % (i, k)) for k in range(2)] for i in range(2)]
    uu = A.alloc([128, 1024], F32)
    B_uu = P.buf("uu")
    ub = [A.alloc([128, 1024], BF16) for _ in range(2)]
    B_ub = [P.buf("ub%d" % i) for i in range(2)]
    uT = A.alloc([128, 8, 128], BF16)
    B_uT = P.buf("uT")
    fin = A.alloc([128, 1024], F32)
    B_fin = P.buf("fin")

    def stage0a(j):
        xi, s2 = j % 5, j % 2
        if j >= 2:
            dma("pool", xb_[xi], xo[j * 128:(j + 1) * 128, :], B_xb[xi], reads=[])
        rms_a(xb_[xi], B_xb[xi], hq2[s2], B_hq2[s2])

    def stage0b(j):
        s2 = j % 2
        rms_b(hq2[s2], B_hq2[s2], lambda c, s2=s2: hTq2[s2][:, c, :], B_hTq2[s2], (0, 5))

    def stage1(j, mi):
        xi, s2 = j % 5, j % 2
        if True:
            for hf in range(2):
                pb = 1 + hf
                for kc in range(8):
                    P.add("pe", lambda e, pb=pb, mi=mi, hf=hf, kc=kc, s2=s2: e.matmul(
                        bank[pb], lhsT=hTq2[s2][:, kc, :], rhs=wm[:, kc, mi * 1024 + hf * 512: mi * 1024 + (hf + 1) * 512],
                        start=(kc == 0), stop=(kc == 7)), reads=[B_wm[mi * 2 + hf], *B_hTq2[s2]], writes=[PB[pb]])
            P.add("act", lambda e, mi=mi, s2=s2: e.activation(out=sig[s2][mi], in_=ps[:, 512:1536], func=AF.Tanh, scale=0.5),
                  xreads=[PB[1], PB[2]], writes=[B_sig[s2][mi]])

    def stage2(j):
        s2 = j % 2
        for mi in range(2):
            wo, B_wo = (woa, B_woa) if mi == 0 else (wob, B_wob)
            for hf in range(2):
                pb = 3 + hf
                for c in range(4):
                    P.add("pe", lambda e, pb=pb, mi=mi, hf=hf, c=c, wo=wo, j=j: e.matmul(
                        bank[pb], lhsT=OZT[:, j, mi, c, :], rhs=wo[:, c, hf * 512:(hf + 1) * 512], start=(c == 0), stop=(c == 3)),
                        reads=[B_wo, B_OZT], writes=[PB[pb]])
            if mi == 0:
                P.add("dve", lambda e, s2=s2: e.scalar_tensor_tensor(out=uu, in0=sig[s2][0], scalar=1.0, in1=ps[:, 1536:2560], op0=ALU.add, op1=ALU.mult),
                      reads=[B_sig[s2][0]], xreads=[PB[3], PB[4]], writes=[B_uu])
            else:
                P.add("dve", lambda e, s2=s2: e.scalar_tensor_tensor(out=sig[s2][1], in0=sig[s2][1], scalar=1.0, in1=ps[:, 1536:2560], op0=ALU.add, op1=ALU.mult),
                      xreads=[PB[3], PB[4]], writes=[B_sig[s2][1]])
                P.add("dve", lambda e, s2=s2: e.tensor_tensor(out=ub[s2], in0=uu, in1=sig[s2][1], op=ALU.add),
                      reads=[B_uu, B_sig[s2][1]], writes=[B_ub[s2]])

    def stage3a(j):
        s2 = j % 2
        for c in range(8):
            P.add("pe", lambda e, c=c, s2=s2: e.transpose(out=bankb[5][:, c * 128:(c + 1) * 128], in_=ub[s2][:, c * 128:(c + 1) * 128], identity=ident_b),
                  reads=[B_ub[s2], B_const], writes=[PB[5]])
        P.add("act", lambda e: e.copy(out=uT.rearrange("p a b -> p (a b)"), in_=bankb[5]), xreads=[PB[5]], writes=[B_uT])

    def stage3b(j):
        xi = j % 5
        xq2 = xb_[xi]
        for hf in range(2):
            pb = 6 + hf
            for c in range(8):
                P.add("pe", lambda e, pb=pb, hf=hf, c=c: e.matmul(bank[pb], lhsT=uT[:, c, :], rhs=wout[:, c, hf * 512:(hf + 1) * 512],
                                                                  start=(c == 0), stop=(c == 7)), reads=[B_wout, B_uT], writes=[PB[pb]])
        sc = A_scr
        P.add("act", lambda e: e.activation(out=junk, in_=ps[:, 3072:4096], func=AF.Square, accum_out=sc[:, 4:5]),
              xreads=[PB[6], PB[7]], writes=[B_junk, B_scr2])
        P.add("act", lambda e: e.activation(out=sc[:, 5:6], in_=sc[:, 4:5], func=AF.Ln, scale=1.0 / D, bias=epsT[:, 1:2]),
              reads=[B_eps], writes=[B_scr2])
        P.add("act", lambda e: e.activation(out=sc[:, 6:7], in_=sc[:, 5:6], func=AF.Exp, scale=-0.5), writes=[B_scr2])
        P.add("dve", lambda e: e.scalar_tensor_tensor(out=fin, in0=ps[:, 3072:4096], scalar=sc[:, 6:7], in1=ggbc, op0=ALU.mult, op1=ALU.mult),
              reads=[B_scr2, B_gg], xreads=[PB[6], PB[7]], writes=[B_fin])
        P.add("dve", lambda e, xq2=xq2: e.tensor_tensor(out=xq2, in0=xq2, in1=fin, op=ALU.add), reads=[B_fin], writes=[B_xb[xi]])
        P.add("sp", lambda e, xq2=xq2, j=j: e.dma_start(out=out_d[j * 128:(j + 1) * 128, :], in_=xq2), reads=[B_xb[xi]], pwrites=[out_bufs[xi]],
              dma=out_bufs[xi])

    B_scr2 = P.buf("scr2")
    for j0 in range(2):
        dma("pool", xb_[j0], xo[j0 * 128:(j0 + 1) * 128, :], B_xb[j0], reads=[])
    for hf in range(4):
        dma("pool", wm[:, :, hf * 512:(hf + 1) * 512], w_in[:, 3096 + hf * 512:3096 + (hf + 1) * 512].rearrange("(kc p) n -> p kc n", p=128),
            B_wm[hf])
    dma("pool", wob, wosw_d.rearrange("(c p) n -> p c n", p=128), B_wob)
    dma("pool", wout, wout_d.rearrange("(c p) n -> p c n", p=128), B_wout)
    stage0a(0)
    stage0b(0)
    stage0a(1)
    for it in range(NSLOT + 2):
        if it < NSLOT:
            stage1(it, 0)
        if it - 2 >= 0:
            stage3a(it - 2)
        if it + 1 < NSLOT:
            stage0b(it + 1)
        if it < NSLOT:
            stage1(it, 1)
        if 0 <= it - 1 < NSLOT:
            stage2(it - 1)
        if it - 2 >= 0:
            stage3b(it - 2)
        if it + 2 < NSLOT:
            stage0a(it + 2)
    P.emit(nc, final_waits=out_bufs + list(dbg_out.values()))
    return nc


def _inputs_for_core(core, inp, shared):
    b, p = core // 2, core % 2
    f = lambda a: np.ascontiguousarray(a, dtype=np.float32)
    x = inp["x"][b]
    m = dict(shared)
    m["xf"] = f(x)
    m["xo"] = f(x.reshape(32, 128, D)[p::2].reshape(NSLOT * 128, D))
    m["cT"] = f(inp["c"][b].reshape(8, 128).T)
    m.update(_core_tables(p))
    return m


def kernel(debug=None, stop_after=None, cores=None, **inp):
    inp = {k: np.asarray(v) for k, v in inp.items()}
    f = lambda a: np.ascontiguousarray(a, dtype=np.float32)
    shared = _shared_tables()
    shared["w_ada"] = f(inp["w_ada"][0])
    shared["badaT"] = f(inp["b_ada"][0].reshape(24, 128).T)
    shared["gpreT"] = f(inp["g_pre"][0].reshape(8, 128).T)
    shared["gpostT"] = f(inp["g_post"][0].reshape(8, 128).T)
    shared["w_in"] = f(inp["w_in"][0])
    shared["pekT"] = f(np.stack([inp["pe_cmp_k"][0].T, np.zeros((64, 32))], -1).reshape(64, 64))
    shared["pevT"] = f(np.stack([inp["pe_cmp_v"][0].T, np.zeros((64, 32))], -1).reshape(64, 64))
    shared["w1k"] = f(inp["w_cmp_k1"][0])
    shared["w2k"] = f(inp["w_cmp_k2"][0])
    shared["w1v"] = f(inp["w_cmp_v1"][0])
    shared["w2v"] = f(inp["w_cmp_v2"][0])
    shared["w_o_nsa"] = f(inp["w_o_nsa"][0])
    shared["w_o_swa"] = f(inp["w_o_swa"][0])
    shared["w_out"] = f(inp["w_out"][0])
    shared["sinksb"] = f(np.tile(inp["sinks"][0][None, :], (128, 1)))
    nc = build_nc(debug=debug, stop_after=stop_after)
    cores = list(range(8)) if cores is None else cores
    in_maps = [_inputs_for_core(c, inp, shared) for c in cores]
    if stop_after == "phase1" or (stop_after or "").startswith("slot") or stop_after == "sweepA":
        for m in in_maps:
            pass
    res = run_bass_kernel_spmd(nc, in_maps, core_ids=list(range(len(cores))))
    if stop_after:
        return res.results
    out = np.zeros((4, 32, 128, D), np.float32)
    for i, c in enumerate(cores):
        b, p = c // 2, c % 2
        out[b, p::2] = res.results[i]["out"].reshape(NSLOT, 128, D)
    if debug:
        return out.reshape(4, S, D), res.results
    return out.reshape(4, S, D)
```

```python
import numpy as np
from contextlib import ExitStack
import concourse.bass as bass
import concourse.mybir as mybir
from concourse.bass_utils import run_bass_kernel_spmd

F32 = mybir.dt.float32
BF16 = mybir.dt.bfloat16
AF = mybir.ActivationFunctionType
ALU = mybir.AluOpType

NEGM = -30000.0
IMP_INLINE = True
S = 4096
D = 1024
NSLOT = 16


class Buf:
    __slots__ = ("name", "w", "r", "xr", "dcount", "sem", "last")

    def __init__(self, name):
        self.name = name
        self.w = None
        self.r = {}
        self.xr = {}
        self.dcount = 0
        self.sem = None
        self.last = None


class Op:
    __slots__ = ("eng", "fn", "deps", "signal", "dma", "sigval")

    def __init__(self, eng, fn, dma):
        self.eng = eng
        self.fn = fn
        self.dma = dma
        self.deps = []
        self.signal = False
        self.sigval = 0


class Prog:
    ENGS = ("pe", "act", "dve", "pool", "sp")

    def __init__(self):
        self.ops = {e: [] for e in self.ENGS}
        self.dma_bufs = []
        self.pending = {e: [] for e in self.ENGS}

    def buf(self, name):
        return Buf(name)

    def barrier(self):
        lasts = [self.ops[e][-1] for e in self.ENGS if self.ops[e]]
        lasts += [b.last for b in self.dma_bufs if b.last is not None]
        for e in self.ENGS:
            self.pending[e] = list(lasts)

    def add(self, eng, fn, reads=(), writes=(), pwrites=(), dma=None, xreads=()):
        op = Op(eng, fn, dma)
        deps = {}
        for b in reads:
            if b.w is not None:
                deps[id(b.w)] = b.w
        for b in xreads:
            if b.w is not None:
                deps[id(b.w)] = b.w
            for k_, r in b.xr.items():
                if k_ != eng:
                    deps[id(r)] = r
        for b in list(writes) + list(pwrites):
            for r in b.r.values():
                deps[id(r)] = r
            for r in b.xr.values():
                deps[id(r)] = r
        for b in writes:
            if b.w is not None:
                deps[id(b.w)] = b.w
        for d in self.pending[eng]:
            deps[id(d)] = d
        self.pending[eng] = []
        for d in deps.values():
            if d.dma is None and dma is None and d.eng == "pe" and eng == "pe":
                continue
            if d.dma is None and d.eng == eng and eng == "sp":
                continue
            op.deps.append(d)
            d.signal = True
        key = eng if dma is None else ("dma", id(dma))
        for b in reads:
            b.r[key] = op
        for b in xreads:
            b.xr[eng] = op
        for b in list(writes) + list(pwrites):
            b.w = op
            b.r = {}
            b.xr = {}
        if dma is not None:
            if dma not in self.dma_bufs:
                self.dma_bufs.append(dma)
            dma.dcount += 16
            dma.last = op
            op.sigval = dma.dcount
            op.signal = True
        self.ops[eng].append(op)
        return op

    def emit(self, nc, final_waits=()):
        with ExitStack() as st:
            esem = {e: st.enter_context(nc.semaphore("sem_" + e)) for e in self.ENGS}
            for i, b in enumerate(self.dma_bufs):
                b.sem = st.enter_context(nc.semaphore("dsem%d" % i))
            for e in self.ENGS:
                c = 0
                for op in self.ops[e]:
                    if op.dma is None and op.signal:
                        c += 1
                        op.sigval = c
            block = st.enter_context(nc.Block())

            def run(eng_name, engine):
                waited = {}
                for op in self.ops[eng_name]:
                    for d in op.deps:
                        if d.dma is not None:
                            sem, key = d.dma.sem, ("d", id(d.dma))
                        else:
                            sem, key = esem[d.eng], d.eng
                        if waited.get(key, 0) >= d.sigval:
                            continue
                        waited[key] = d.sigval
                        engine.wait_ge(sem, d.sigval)
                    ins = op.fn(engine)
                    if op.dma is not None:
                        ins.then_inc(op.dma.sem, 16)
                    elif op.signal:
                        ins.then_inc(esem[eng_name], 1)
                if eng_name == "sp":
                    for b in final_waits:
                        engine.wait_ge(b.sem, b.dcount)

            @block.tensor
            def _(e):
                run("pe", e)

            @block.scalar
            def _(e):
                run("act", e)

            @block.vector
            def _(e):
                run("dve", e)

            @block.gpsimd
            def _(e):
                run("pool", e)

            @block.sync
            def _(e):
                run("sp", e)


class Arena:
    def __init__(self, nc, st, nbytes):
        self.t = st.enter_context(nc.sbuf_tensor("arena", [128, nbytes // 2], BF16))
        self.top = 0
        self.cap = nbytes

    def alloc(self, shape, dtype):
        esz = 4 if dtype == F32 else 2
        n = 1
        for s_ in shape[1:]:
            n *= s_
        nb = (n * esz + 31) // 32 * 32
        off = self.top
        self.top += nb
        assert self.top <= self.cap, ("SBUF arena overflow", self.top, self.cap)
        v = self.t[:, off // 2: off // 2 + n * esz // 2]
        if dtype == F32:
            v = v.bitcast(F32)
        if len(shape) == 3:
            v = v.rearrange("p (a b) -> p a b", a=shape[1])
        elif len(shape) == 4:
            v = v.rearrange("p (a b c) -> p a b c", a=shape[1], b=shape[2])
        elif len(shape) == 5:
            v = v.rearrange("p (a b c d) -> p a b c d", a=shape[1], b=shape[2], c=shape[3])
        return v

    def mark(self):
        return self.top

    def release(self, m):
        self.top = m


def _slopes():
    return 2.0 ** (-(np.arange(8) + 1.0))


def _shared_tables():
    t = {}
    t["ident"] = np.eye(128, dtype=np.float32)
    kpos = np.arange(S)
    t["kaug"] = np.stack([np.ones(S), np.ones(S), kpos // 128, kpos % 128]).astype(np.float32)
    c = np.arange(256)
    end = 16 * c + 31
    t["caug"] = np.stack([np.ones(256), np.ones(256), end // 128, end % 128]).astype(np.float32)
    M = np.zeros((256, 64), np.float32)
    for j in range(64):
        for m in range(4):
            for n in range(2):
                ci = 4 * j - m - n
                if 0 <= ci < 255:
                    M[ci, j] += 1.0
    t["mimp"] = M
    R = np.zeros((64, S), np.float32)
    R[kpos // 64, kpos] = 1.0
    t["ksel"] = np.concatenate([R[1:62], np.stack([np.ones(S), kpos // 128, kpos % 128])]).astype(np.float32)
    return t


def _core_tables(p):
    sl = _slopes()
    t = {}
    qa = np.zeros((NSLOT, 4, 4, 4, 128), np.float32)
    q = np.arange(128)
    for j in range(NSLOT):
        for tg in range(4):
            g = tg % 2
            for r in range(4):
                s_ = sl[g * 4 + r]
                qa[j, 0, tg, r] = -s_ * 128.0 * (2 * j + p)
                qa[j, 1, tg, r] = -s_ * q
                qa[j, 2, tg, r] = s_ * 128.0
                qa[j, 3, tg, r] = s_
    t["qaug"] = qa.reshape(NSLOT, 4, 2048)
    qs = np.zeros((NSLOT, 3, 2, 4, 128), np.float32)
    for j in range(NSLOT):
        qpos = (2 * j + p) * 128 + q
        for g in range(2):
            for r in range(4):
                s_ = sl[g * 4 + r]
                qs[j, 0, g, r] = -s_ * qpos
                qs[j, 1, g, r] = s_ * 128.0
                qs[j, 2, g, r] = s_
    t["qaugs"] = qs.reshape(NSLOT, 3, 1024)
    k = np.arange(128)[:, None]
    qq = np.arange(128)[None, :]
    masks = np.zeros((10, 128, 128), np.float32)
    for i in range(2):
        dist = (p - i) * 128 + qq - k
        masks[i] = np.where(dist >= 0, 0.0, NEGM)
    for n, i in enumerate((0, 1, 4, 5)):
        dist = (p - i + 4) * 128 + qq - k
        masks[2 + n] = np.where((dist >= 0) & (dist < 512), 0.0, NEGM)
    for i in range(3):
        dist = (p - i + 1) * 128 + qq - k
        masks[6 + i] = np.where((dist >= 0) & (dist < 128), 0.0, NEGM)
    t["masks"] = masks.transpose(1, 0, 2).copy()
    cm = np.zeros((NSLOT, 2, 128, 128), np.float32)
    for j in range(NSLOT):
        qpos = (2 * j + p) * 128 + qq
        for cc in range(2):
            c = cc * 128 + k
            ok = (qpos >= 16 * c + 31) & (c < 255)
            cm[j, cc] = np.where(ok, 0.0, NEGM)
    t["cmask"] = cm
    cf = np.zeros((NSLOT, 2, 128, 64), np.float32)
    jj = np.arange(64)[None, :]
    for j in range(NSLOT):
        qpos = (2 * j + p) * 128 + np.arange(128)[:, None]
        cur = qpos // 64
        causal = jj <= cur
        fb = np.where(causal, 0.0, -1e30)
        fb = np.where(causal & (jj == 0), 1e30, fb)
        fb = np.where(causal & (jj == cur - 1), 2e30, fb)
        fb = np.where(causal & (jj == cur), 4e30, fb)
        cf[j, 0] = causal.astype(np.float32)
        cf[j, 1] = fb
    t["cmfb"] = cf
    return t


def build_nc(debug=None, stop_after=None):
    debug = debug or []
    nc = bass.Bass("TRN2", target_bir_lowering=False)
    P = Prog()
    st = ExitStack()

    def din(name, shape):
        return nc.dram_tensor(name, list(shape), F32, kind="ExternalInput").ap()

    xf = din("xf", [S, D])
    xo = din("xo", [NSLOT * 128, D])
    cT_d = din("cT", [128, 8])
    w_ada = din("w_ada", [D, 3 * D])
    badaT = din("badaT", [128, 24])
    gpreT = din("gpreT", [128, 8])
    gpostT = din("gpostT", [128, 8])
    w_in = din("w_in", [D, 5144])
    pekT = din("pekT", [64, 64])
    pevT = din("pevT", [64, 64])
    w1k_d = din("w1k", [2048, 256])
    w2k_d = din("w2k", [256, 64])
    w1v_d = din("w1v", [2048, 256])
    w2v_d = din("w2v", [256, 64])
    wona_d = din("w_o_nsa", [512, D])
    wosw_d = din("w_o_swa", [512, D])
    wout_d = din("w_out", [D, D])
    sinks_d = din("sinksb", [128, 8])
    ident_d = din("ident", [128, 128])
    kaug_d = din("kaug", [4, S])
    caug_d = din("caug", [4, 256])
    mimp_d = din("mimp", [256, 64])
    ksel_d = din("ksel", [64, S])
    qaugs_d = din("qaugs", [NSLOT, 3, 1024])
    qaug_d = din("qaug", [NSLOT, 4, 2048])
    masks_d = din("masks", [128, 10, 128])
    cmask_d = din("cmask", [NSLOT, 2, 128, 128])
    cmfb_d = din("cmfb", [NSLOT, 2, 128, 64])
    out_d = nc.dram_tensor("out", [NSLOT * 128, D], F32, kind="ExternalOutput").ap()
    out_bufs = [P.buf("out%d" % i) for i in range(5)]
    dbg_out = {}

    A = Arena(nc, st, 212800)
    ps = st.enter_context(nc.psum_tensor("ps", [128, 4096], F32))
    bank = [ps[:, i * 512:(i + 1) * 512] for i in range(8)]
    bankb = [bank[i].bitcast(BF16) for i in range(8)]
    PB = [P.buf("bank%d" % i) for i in range(8)]

    def dma(eng, out, in_, b, reads=(), part=False):
        if part:
            return P.add(eng, lambda e: e.dma_start(out=out, in_=in_), reads=reads, pwrites=[b], dma=b)
        return P.add(eng, lambda e: e.dma_start(out=out, in_=in_), reads=reads, writes=[b], dma=b)

    def dump(name, ap, b, shape):
        if name not in debug:
            return
        dt = nc.dram_tensor("dbg_" + name, list(shape), ap.dtype, kind="ExternalOutput").ap()
        db = P.buf("dbg_" + name)
        P.add("sp", lambda e: e.dma_start(out=dt, in_=ap), reads=[b], writes=[db], dma=db)
        dbg_out[name] = db

    ident_f = A.alloc([128, 128], F32)
    ident_b = A.alloc([128, 128], BF16)
    ones_f = A.alloc([128, 128], F32)
    cols = A.alloc([128, 96], F32)
    B_const = P.buf("const")
    B_identf = P.buf("identf")
    B_ones = P.buf("ones")
    B_cols = P.buf("cols")
    B_eps = P.buf("eps")
    epsT = A.alloc([128, 8], F32)
    dma("sp", ident_f, ident_d[:, :], B_identf)
    dma("pool", ident_b, ident_d[:, :], B_const)
    P.add("pool", lambda e: e.memset(ones_f, 1.0), writes=[B_ones])
    P.add("pool", lambda e: e.memset(epsT, 1e-6), writes=[B_eps])
    P.add("pool", lambda e: e.memset(epsT[:, 1:2], 16e-6), writes=[B_eps])
    P.add("pool", lambda e: e.memset(cols, 0.0), writes=[B_cols])
    dma("sp", cols[:, 0:8], cT_d[:, :], B_cols)
    dma("sp", cols[:, 8:32], badaT[:, :], B_cols, part=True)
    dma("sp", cols[:, 32:40], gpreT[:, :], B_cols, part=True)
    dma("sp", cols[:, 40:48], gpostT[:, :], B_cols, part=True)
    sinks_t = A.alloc([128, 8], F32)
    B_sinks = P.buf("sinks")
    dma("sp", sinks_t, sinks_d[:, :], B_sinks)

    A_scr = A.alloc([128, 8], F32)
    B_scr = P.buf("scr")
    junk = A.alloc([128, 1024], BF16)
    B_junk = P.buf("junk")
    XR = A.alloc([128, 16384], BF16)
    kv_mark = A.mark()
    KT = [[A.alloc([128, S], BF16) for g in range(2)] for t in range(3)]
    B_KT = [[P.buf("KT%d%d" % (t, g)) for g in range(2)] for t in range(3)]
    Vt = A.alloc([128, 32, 6, 65], BF16)
    B_Vt = P.buf("Vt")
    KC = [A.alloc([128, 256], BF16) for g in range(2)]
    B_KC = [P.buf("KC%d" % g) for g in range(2)]
    VC = A.alloc([128, 2, 2, 129], BF16)
    B_VC = P.buf("VC")
    for t in range(3):
        for g in range(2):
            if t == 0:
                dma("pool", KT[t][g][0:64, :], ksel_d[:, :], B_KT[t][g], part=True)
            else:
                dma("pool", KT[t][g][64:68, :], kaug_d[:, :], B_KT[t][g], part=True)
    for g in range(2):
        P.add("pool", lambda e, g=g: e.memset(KC[g][0:64, :], 0.0), pwrites=[B_KC[g]])
        dma("pool", KC[g][64:68, :], caug_d[:, :], B_KC[g], part=True)
    P.add("pool", lambda e: e.memset(Vt[:, :, :, 64:65], 1.0), pwrites=[B_Vt])
    P.add("pool", lambda e: e.memset(VC[:, :, :, 64:65], 1.0), pwrites=[B_VC])
    for g in range(2):
        dma("pool", VC[:, :, g, 65:129], mimp_d.rearrange("(cc p) j -> p cc j", p=128), B_VC, part=True)

    small_mark = A.mark()
    if stop_after == "consts":
        dump("VC", VC, B_VC, [128, 2, 2, 129])
        dump("KC0", KC[0][0:68, :], B_KC[0], [68, 256])
        P.emit(nc, final_waits=list(dbg_out.values()))
        return nc

    wst = [A.alloc([128, 1536], F32) for _ in range(2)]
    B_wst = [P.buf("wst%d" % i) for i in range(2)]
    first = True
    for kc in range(8):
        for hf in range(2):
            i = (kc * 2 + hf) % 2
            dma("sp", wst[i], w_ada[kc * 128:(kc + 1) * 128, hf * 1536:(hf + 1) * 1536], B_wst[i])
            for n in range(12):
                col = hf * 12 + n
                P.add("pe", lambda e, i=i, n=n, col=col, kc=kc, first=first: e.matmul(
                    bank[0][:, col:col + 1], lhsT=wst[i][:, n * 128:(n + 1) * 128], rhs=cols[:, kc:kc + 1],
                    start=first, stop=(kc == 7), skip_group_check=True),
                    reads=[B_wst[i], B_cols], writes=[PB[0]])
                first = False
    P.add("dve", lambda e: e.tensor_tensor(out=cols[:, 48:72], in0=bank[0][:, 0:24], in1=cols[:, 8:32], op=ALU.add),
          reads=[B_cols], writes=[PB[0]], pwrites=[B_cols])
    P.add("dve", lambda e: e.scalar_tensor_tensor(out=cols[:, 72:80], in0=cols[:, 56:64], scalar=1.0, in1=cols[:, 32:40],
                                                  op0=ALU.add, op1=ALU.mult), reads=[B_cols], writes=[B_cols])
    P.add("dve", lambda e: e.tensor_tensor(out=cols[:, 80:88], in0=cols[:, 64:72], in1=cols[:, 40:48], op=ALU.mult),
          reads=[B_cols], writes=[B_cols])
    dump("cols", cols, B_cols, [128, 96])
    if stop_after == "phase0":
        P.emit(nc, final_waits=list(dbg_out.values()))
        return nc

    def rms_a(xt, B_xt, hn, B_hn):
        sc = A_scr
        P.add("act", lambda e: e.activation(out=junk, in_=xt, func=AF.Square, accum_out=sc[:, 0:1]),
              reads=[B_xt], writes=[B_junk, B_scr])
        P.add("act", lambda e: e.activation(out=sc[:, 1:2], in_=sc[:, 0:1], func=AF.Ln, scale=1.0 / D, bias=epsT[:, 0:1]),
              reads=[B_eps], writes=[B_scr])
        P.add("act", lambda e: e.activation(out=sc[:, 2:3], in_=sc[:, 1:2], func=AF.Exp, scale=-0.5), writes=[B_scr])
        P.add("dve", lambda e: e.tensor_scalar(out=hn, in0=xt, scalar1=sc[:, 2:3], scalar2=None, op0=ALU.mult),
              reads=[B_xt, B_scr], writes=[B_hn])

    def rms_b(hn, B_hn, hT_dst, B_hT, tbanks):
        for c in range(8):
            tb = tbanks[c // 4]
            P.add("pe", lambda e, c=c, tb=tb: e.transpose(out=bankb[tb][:, (c % 4) * 128:(c % 4 + 1) * 128], in_=hn[:, c * 128:(c + 1) * 128],
                                                         identity=ident_b), reads=[B_hn, B_const], writes=[PB[tb]])
        for c in range(8):
            tb = tbanks[c // 4]
            src = bankb[tb][:, (c % 4) * 128:(c % 4 + 1) * 128]
            if c >= 4:
                P.add("act", lambda e, c=c, src=src: e.activation(out=hT_dst(c), in_=src, func=AF.Identity,
                                                                  scale=cols[:, 72 + c:73 + c], bias=cols[:, 48 + c:49 + c]),
                      reads=[B_cols], xreads=[PB[tb]], pwrites=[B_hT[1]])
            else:
                P.add("dve", lambda e, c=c, src=src: e.tensor_scalar(out=hT_dst(c), in0=src,
                                                                     scalar1=cols[:, 72 + c:73 + c], scalar2=cols[:, 48 + c:49 + c],
                                                                     op0=ALU.mult, op1=ALU.add),
                      reads=[B_cols], xreads=[PB[tb]], pwrites=[B_hT[0]])

    wkv = A.alloc([128, 8, 1024], BF16)
    B_wkv = P.buf("wkv")
    kvcols = [512, 640, 768, 1024, 2328, 896, 1152, 2456]
    for i, c0 in enumerate(kvcols):
        dma("pool", wkv[:, :, i * 128:(i + 1) * 128], w_in[:, c0:c0 + 128].rearrange("(kc p) n -> p kc n", p=128), B_wkv, part=True)
    xt = [A.alloc([128, 1024], F32) for _ in range(2)]
    B_xt = [P.buf("xt%d" % i) for i in range(2)]
    hn = [A.alloc([128, 1024], BF16) for _ in range(2)]
    B_hn = [P.buf("hn%d" % i) for i in range(2)]
    hTc = [A.alloc([128, 8, 512], BF16) for _ in range(2)]
    B_hTc = [(P.buf("hTc%da" % i), P.buf("hTc%db" % i)) for i in range(2)]
    rawk = A.alloc([128, S], BF16)
    rawv = A.alloc([128, S], BF16)
    B_rawk, B_rawv = P.buf("rawk"), P.buf("rawv")
    rawk3 = rawk.rearrange("p (r m) -> p r m", r=16)
    rawv3 = rawv.rearrange("p (r m) -> p r m", r=16)
    w1k = XR[:, 0:8192].rearrange("p (a b) -> p a b", a=32)
    w1v = XR[:, 8192:16384].rearrange("p (a b) -> p a b", a=32)
    w2k = A.alloc([128, 2, 64], BF16)
    w2v = A.alloc([128, 2, 64], BF16)
    peT = A.alloc([128, 2, 64], BF16)
    B_w1 = P.buf("w1")
    for hh in range(2):
        dma("pool", w1k[hh * 64:(hh + 1) * 64], w1k_d.rearrange("(l d) h -> d l h", d=64), B_w1, part=True)
        dma("pool", w1v[hh * 64:(hh + 1) * 64], w1v_d.rearrange("(l d) h -> d l h", d=64), B_w1, part=True)
    dma("pool", w2k, w2k_d.rearrange("(hc p) d -> p hc d", p=128), B_w1, part=True)
    dma("pool", w2v, w2v_d.rearrange("(hc p) d -> p hc d", p=128), B_w1, part=True)
    dma("pool", peT[0:64, 0, :], pekT[:, :], B_w1, part=True)
    dma("pool", peT[0:64, 1, :], pevT[:, :], B_w1, part=True)

    def p1_a(tb):
        xi = tb % 2
        dma("sp", xt[xi], xf[tb * 128:(tb + 1) * 128, :], B_xt[xi])
        rms_a(xt[xi], B_xt[xi], hn[xi], B_hn[xi])

    def kgroup(ch, i):
        hb = ch % 2
        pb = 1 + (i % 2)
        for kc in range(8):
            P.add("pe", lambda e, i=i, kc=kc, pb=pb, hb=hb: e.matmul(
                bank[pb], lhsT=wkv[:, kc, i * 128:(i + 1) * 128], rhs=hTc[hb][:, kc, :], start=(kc == 0), stop=(kc == 7)),
                reads=[B_wkv, *B_hTc[hb]], writes=[PB[pb]])
        csl = slice(ch * 512, (ch + 1) * 512)
        if i == 0:
            P.add("act", lambda e, pb=pb, ch=ch: e.copy(out=rawk3[:, :, ch * 32:(ch + 1) * 32], in_=bank[pb].rearrange("p (m r) -> p r m", r=16)),
                  xreads=[PB[pb]], pwrites=[B_rawk])
        elif i == 1:
            P.add("dve", lambda e, pb=pb, ch=ch: e.tensor_copy(out=rawv3[:, :, ch * 32:(ch + 1) * 32], in_=bank[pb].rearrange("p (m r) -> p r m", r=16)),
                  xreads=[PB[pb]], pwrites=[B_rawv])
        else:
            t = i - 2
            dlo = 64 if t == 0 else 0
            P.add("act", lambda e, pb=pb, csl=csl, t=t, dlo=dlo: e.copy(out=KT[t][0][dlo:dlo + 64, csl], in_=bank[pb][0:64, :]),
                  xreads=[PB[pb]], pwrites=[B_KT[t][0]])
            P.add("dve", lambda e, pb=pb, csl=csl, t=t, dlo=dlo: e.tensor_copy(out=KT[t][1][dlo:dlo + 64, csl], in_=bank[pb][64:128, :]),
                  xreads=[PB[pb]], pwrites=[B_KT[t][1]])

    def vgroup(ch, bi):
        hb = ch % 2
        tb = ch * 4 + bi
        for kc in range(8):
            P.add("pe", lambda e, kc=kc, hb=hb, bi=bi: e.matmul(
                bank[3][:, 0:384], lhsT=hTc[hb][:, kc, bi * 128:(bi + 1) * 128], rhs=wkv[:, kc, 640:1024],
                start=(kc == 0), stop=(kc == 7)), reads=[B_wkv, *B_hTc[hb]], writes=[PB[3]])
        if bi % 2 == 0:
            P.add("act", lambda e, tb=tb: e.copy(out=Vt[:, tb, :, 0:64], in_=bank[3][:, 0:384].rearrange("p (a b) -> p a b", a=6)),
                  xreads=[PB[3]], pwrites=[B_Vt])
        else:
            P.add("dve", lambda e, tb=tb: e.tensor_copy(out=Vt[:, tb, :, 0:64], in_=bank[3][:, 0:384].rearrange("p (a b) -> p a b", a=6)),
                  xreads=[PB[3]], pwrites=[B_Vt])

    def block(tb):
        ch, bi = tb // 4, tb % 4
        hb, xi = ch % 2, tb % 2
        if tb + 1 < 32:
            p1_a(tb + 1)
        rms_b(hn[xi], B_hn[xi], lambda c, hb=hb, bi=bi: hTc[hb][:, c, bi * 128:(bi + 1) * 128], B_hTc[hb], (0, 7))

    p1_a(0)
    for ch in range(9):
        groups = []
        if ch >= 1:
            groups = [lambda i=i, c_=ch - 1: kgroup(c_, i) for i in range(5)] + [lambda b_=b_, c_=ch - 1: vgroup(c_, b_) for b_ in range(4)]
        blocks = [lambda tb=ch * 4 + bi: block(tb) for bi in range(4)] if ch < 8 else []
        order = []
        gi = 0
        for bi in range(4):
            order += groups[gi:gi + 2]
            gi += 2
            if bi < len(blocks):
                order.append(blocks[bi])
        order += groups[gi:]
        for f_ in order:
            f_()

    if stop_after == "phase1a":
        dump("KTs0", KT[0][0][0:68, :], B_KT[0][0], [68, S])
        dump("Vt", Vt, B_Vt, [128, 32, 6, 65])
        P.emit(nc, final_waits=list(dbg_out.values()))
        return nc
    hid = A.alloc([128, 2, 256], BF16)
    B_hid = P.buf("hid")
    cu = A.alloc([128, 256], F32)
    ce = A.alloc([128, 256], F32)
    B_cu, B_ce = P.buf("cu"), P.buf("ce")
    cb = A.alloc([128, 8], F32)
    B_cb = P.buf("cb")
    P.add("pool", lambda e: e.memset(hid, 0.0), writes=[B_hid])
    for kv in range(2):
        w1 = w1k if kv == 0 else w1v
        for hc in range(2):
            q4 = kv * 2 + hc
            for l in range(32):
                P.add("pe", lambda e, l=l, hc=hc, w1=w1, kv=kv, q4=q4: e.matmul(
                    bank[7][:, q4 * 32:(q4 + 1) * 32], lhsT=w1[0:64, l, hc * 128:(hc + 1) * 128],
                    rhs=peT[0:64, kv, 2 * l:2 * l + 1].to_broadcast([64, 32]), start=(l == 0), stop=(l == 31)), reads=[B_w1], writes=[PB[7]])
    P.add("dve", lambda e: e.tensor_copy(out=cb[:, 0:4], in_=bank[7][:, 0:128:32]), writes=[PB[7], B_cb])
    P.add("dve", lambda e: e.tensor_scalar(out=cb[:, 4:8], in0=cb[:, 0:4], scalar1=0.5, scalar2=None, op0=ALU.mult), writes=[B_cb])
    for kv in range(2):
        w1 = w1k if kv == 0 else w1v
        w2 = w2k if kv == 0 else w2v
        raw, B_raw = (rawk3, B_rawk) if kv == 0 else (rawv3, B_rawv)
        for g in range(2):
            for hc in range(2):
                pb = 4 + hc
                for l in range(32):
                    P.add("pe", lambda e, l=l, g=g, hc=hc, w1=w1, raw=raw, pb=pb: e.matmul(
                        bank[pb][:, 0:255], lhsT=w1[g * 64:(g + 1) * 64, l, hc * 128:(hc + 1) * 128],
                        rhs=raw[g * 64:(g + 1) * 64, l % 16, (l // 16):(l // 16) + 255], start=(l == 0), stop=(l == 31)),
                        reads=[B_w1, B_raw], writes=[PB[pb]])
                q4 = kv * 2 + hc
                P.add("act", lambda e, pb=pb, q4=q4: e.activation(out=ce[:, 0:255], in_=bank[pb][:, 0:255], func=AF.Tanh, scale=0.5, bias=cb[:, 4 + q4:5 + q4]),
                      reads=[B_cb], xreads=[PB[pb]], writes=[B_ce])
                P.add("dve", lambda e, pb=pb, q4=q4: e.tensor_scalar(out=cu[:, 0:255], in0=bank[pb][:, 0:255], scalar1=cb[:, q4:q4 + 1], scalar2=None, op0=ALU.add),
                      reads=[B_cb], xreads=[PB[pb]], writes=[B_cu])
                P.add("dve", lambda e, hc=hc: e.scalar_tensor_tensor(out=hid[:, hc, 0:255], in0=ce[:, 0:255], scalar=1.0, in1=cu[:, 0:255], op0=ALU.add, op1=ALU.mult),
                      reads=[B_cu, B_ce], writes=[B_hid])
            if kv == 0:
                for hc in range(2):
                    P.add("pe", lambda e, hc=hc, w2=w2: e.matmul(bank[6][0:64, 0:256], lhsT=w2[:, hc, :], rhs=hid[:, hc, :],
                                                                start=(hc == 0), stop=(hc == 1)), reads=[B_w1, B_hid], writes=[PB[6]])
                P.add("dve", lambda e, g=g: e.tensor_scalar(out=KC[g][0:64, :], in0=bank[6][0:64, 0:256], scalar1=0.5, scalar2=None, op0=ALU.mult),
                      writes=[PB[6]], pwrites=[B_KC[g]])
            else:
                for cc in range(2):
                    for hc in range(2):
                        P.add("pe", lambda e, hc=hc, cc=cc, w2=w2: e.matmul(bank[6][:, cc * 64:(cc + 1) * 64], lhsT=hid[:, hc, cc * 128:(cc + 1) * 128],
                                                                            rhs=w2[:, hc, :], start=(hc == 0), stop=(hc == 1)),
                              reads=[B_w1, B_hid], writes=[PB[6]])
                P.add("dve", lambda e, g=g: e.tensor_scalar(out=VC[:, :, g, 0:64], in0=bank[6][:, 0:128].rearrange("p (a b) -> p a b", a=2),
                                                            scalar1=0.5, scalar2=None, op0=ALU.mult), writes=[PB[6]], pwrites=[B_VC])
    if debug:
        P.barrier()
    dump("KTs0", KT[0][0][0:68, :], B_KT[0][0], [68, S])
    dump("KTb1", KT[2][1][0:68, :], B_KT[2][1], [68, S])
    dump("Vt", Vt, B_Vt, [128, 32, 6, 65])
    dump("KC0", KC[0][0:68, :], B_KC[0], [68, 256])
    dump("VC", VC, B_VC, [128, 2, 2, 129])
    if stop_after == "phase1":
        P.emit(nc, final_waits=list(dbg_out.values()))
        return nc

    P.barrier()
    A.release(small_mark)
    OZT = XR.rearrange("p (a b c d) -> p a b c d", a=NSLOT, b=2, c=4)
    B_OZT = P.buf("OZT")
    masks = A.alloc([128, 10, 128], BF16)
    B_masks = P.buf("masks")
    dma("pool", masks, masks_d[:, :, :], B_masks)
    wq = A.alloc([128, 8, 2072], BF16)
    B_wq = [P.buf("wq%d" % i) for i in range(5)]
    for wi, (dst, src, n) in enumerate(((0, 0, 512), (512, 1816, 512), (1024, 1304, 512), (1536, 2584, 512), (2048, 1280, 24))):
        dma("pool", wq[:, :, dst:dst + n], w_in[:, src:src + n].rearrange("(kc p) n -> p kc n", p=128), B_wq[wi])
    xq1 = A.alloc([128, 1024], F32)
    xq = [xq1, xq1]
    B_xq1 = P.buf("xq")
    B_xq = [B_xq1, B_xq1]
    hq1 = A.alloc([128, 1024], BF16)
    hq = [hq1, hq1]
    B_hq1 = P.buf("hq")
    B_hq = [B_hq1, B_hq1]
    hTq = [A.alloc([128, 8, 128], BF16) for _ in range(2)]
    B_hTq = [(P.buf("hTq%da" % i), P.buf("hTq%db" % i)) for i in range(2)]
    QT = [A.alloc([128, 4, 512], BF16) for _ in range(2)]
    B_QT = [P.buf("QT%d" % i) for i in range(2)]
    B_QTa = [P.buf("QTa%d" % i) for i in range(2)]
    PT = [A.alloc([128, 1024], BF16) for _ in range(2)]
    B_PT = [P.buf("PT%d" % i) for i in range(2)]
    cmk = [A.alloc([128, 2, 128], BF16) for _ in range(2)]
    B_cmk = [P.buf("cmk%d" % i) for i in range(2)]
    cmfb = [A.alloc([128, 2, 64], F32) for _ in range(2)]
    B_cmfb = [P.buf("cmfb%d" % i) for i in range(2)]
    QS = [[A.alloc([128, 512], BF16) for _ in range(2)] for _ in range(2)]
    B_QSq = [[P.buf("QSq%d%d" % (i, g)) for g in range(2)] for i in range(2)]
    B_QSa = [[P.buf("QSa%d%d" % (i, g)) for g in range(2)] for i in range(2)]
    B_QSs = [[P.buf("QSs%d%d" % (i, g)) for g in range(2)] for i in range(2)]
    zs = [[A.alloc([128, 512], F32) for _ in range(2)] for _ in range(2)]
    B_zs = [[P.buf("zs%d%d" % (i, k)) for k in range(2)] for i in range(2)]
    ze = A.alloc([128, 512], F32)
    B_ze = P.buf("ze")
    sg = [A.alloc([128, 24], F32) for _ in range(2)]
    B_sg = [P.buf("sg%d" % i) for i in range(2)]
    oa = A.alloc([128, 2, 4, 64], F32)
    ob = A.alloc([128, 2, 4, 64], F32)
    B_oa = [P.buf("oa0"), P.buf("oa1")]
    B_ob = [P.buf("ob0"), P.buf("ob1")]
    otmp = A.alloc([128, 4, 64], F32)
    B_otmp = P.buf("otmp")
    ozb = [A.alloc([128, 512], BF16) for _ in range(2)]
    B_ozb = [P.buf("ozb0"), P.buf("ozb1")]
    sm = A.alloc([128, 64], F32)
    B_sm = P.buf("sm")
    imp = A.alloc([128, 64], F32)
    imp4 = A.alloc([128, 256], F32)
    B_imp4 = P.buf("imp4")
    sc1 = A.alloc([128, 64], F32)
    sc2 = A.alloc([128, 64], F32)
    selb = [A.alloc([128, 64], BF16) for _ in range(2)]
    B_imp, B_sc1, B_sc2 = P.buf("imp"), P.buf("sc1"), P.buf("sc2")
    B_selb = [P.buf("selb0"), P.buf("selb1")]
    es = A.alloc([128, 8], F32)
    B_es = P.buf("es")
    woa = A.alloc([128, 4, 1024], BF16)
    B_woa = P.buf("woa")
    wo_top = A.mark()
    dma("pool", woa, wona_d.rearrange("(c p) n -> p c n", p=128), B_woa)
    P.add("act", lambda e: e.activation(out=es, in_=sinks_t, func=AF.Exp), reads=[B_sinks], writes=[B_es])

    st_pairs = [(3, 4), (5, 6)]
    st_ctr = [0]
    o_ctr = [0]
    pending = []

    delayed = []

    def flush(all_=False):
        fs = list(pending)
        del pending[:]
        for f_ in fs:
            f_()
        keep = []
        for item in list(delayed):
            item[0] -= 1
            if item[0] <= 0 or all_:
                item[1]()
            else:
                keep.append(item)
        delayed[:] = keep

    def O3v(obk):
        return bank[obk][:, 0:260].rearrange("p (a b) -> p a b", a=4)

    def attention(sp, tg, Kt, B_K, kblocks, v_of, B_V, selg, addmask, B_am, extra_rhs=None):
        obk = (7, 2)[o_ctr[0] % 2]
        o_ctr[0] += 1
        n = len(kblocks)
        first_pv = [True]
        for i0 in range(0, n, 2):
            grp = kblocks[i0:i0 + 2]
            pr = st_pairs[st_ctr[0] % 2]
            pt = st_ctr[0] % 2
            st_ctr[0] += 1
            for ii, kb in enumerate(grp):
                bk = pr[ii]
                am = addmask(kb) if addmask is not None else None
                nmm = 1 + (1 if am is not None else 0)
                if selg is None:
                    P.add("pe", lambda e, bk=bk, kb=kb, last=(nmm == 1): e.matmul(
                        bank[bk], lhsT=Kt[0:68, kb * 128:(kb + 1) * 128], rhs=QT[sp][0:68, tg, :], start=True, stop=last),
                        reads=[B_K, B_QT[sp], B_QTa[sp]], writes=[PB[bk]])
                else:
                    P.add("pe", lambda e, bk=bk, kb=kb, last=(nmm == 1): e.matmul(
                        bank[bk], lhsT=Kt[:, kb * 128:(kb + 1) * 128], rhs=QS[sp][selg], start=True, stop=last),
                        reads=[B_K, B_QSq[sp][selg], B_QSa[sp][selg], B_QSs[sp][selg]], writes=[PB[bk]])
                if am is not None:
                    P.add("pe", lambda e, bk=bk, am=am: e.matmul(
                        bank[bk].rearrange("p (a b) -> p a b", a=4), lhsT=ident_b,
                        rhs=am.unsqueeze(1).to_broadcast([128, 4, 128]), start=False, stop=True),
                        reads=[B_const, B_am], writes=[PB[bk]])
            w = len(grp) * 512
            b0 = pr[0]
            P.add("act", lambda e, b0=b0, w=w, pt=pt: e.activation(out=PT[pt][:, 0:w], in_=ps[:, b0 * 512:b0 * 512 + w], func=AF.Exp),
                  xreads=[PB[pr[ii]] for ii in range(len(grp))], writes=[B_PT[pt]])
            flush()

            def pv(grp=grp, i0=i0, pt=pt):
                for ii, kb in enumerate(grp):
                    lastk = (i0 + ii == n - 1)
                    for r in range(4):
                        P.add("pe", lambda e, ii=ii, kb=kb, r=r, fp=first_pv[0], lastk=lastk: e.matmul(
                            bank[obk][:, r * 65:(r + 1) * 65], lhsT=PT[pt][:, ii * 512 + r * 128: ii * 512 + (r + 1) * 128], rhs=v_of(kb),
                            start=fp, stop=lastk, skip_group_check=True), reads=[B_PT[pt], B_V], writes=[PB[obk]])
                        if extra_rhs is not None:
                            P.add("pe", lambda e, ii=ii, kb=kb, r=r, fp=first_pv[0], lastk=lastk: e.matmul(
                                bank[1][:, r * 64:(r + 1) * 64], lhsT=PT[pt][:, ii * 512 + r * 128: ii * 512 + (r + 1) * 128], rhs=extra_rhs(kb),
                                start=fp, stop=lastk, skip_group_check=True), reads=[B_PT[pt], B_V], writes=[PB[1]])
                        first_pv[0] = False
            pending.append(pv)
        return obk

    def finish(obk, sp, gate_col0, dst, B_dst, first_branch, extra_den=None):
        O3 = O3v(obk)
        if extra_den is not None:
            P.add("dve", lambda e: e.tensor_tensor(out=sm[:, 0:4], in0=O3[:, :, 64], in1=extra_den, op=ALU.add), reads=[B_es], xreads=[PB[obk]], writes=[B_sm])
        else:
            P.add("dve", lambda e: e.tensor_scalar(out=sm[:, 0:4], in0=O3[:, :, 64], scalar1=1e-30, scalar2=None, op0=ALU.max),
                  xreads=[PB[obk]], writes=[B_sm])
        P.add("dve", lambda e: e.reciprocal(out=sm[:, 4:8], in_=sm[:, 0:4]), writes=[B_sm])
        if gate_col0 is not None:
            P.add("dve", lambda e: e.tensor_tensor(out=sm[:, 8:12], in0=sm[:, 4:8], in1=sg[sp][:, gate_col0:gate_col0 + 4], op=ALU.mult),
                  reads=[B_sg[sp]], writes=[B_sm])
            fac = sm[:, 8:12]
        else:
            fac = sm[:, 4:8]
        facb = fac.unsqueeze(2).to_broadcast([128, 4, 64])
        if first_branch:
            P.add("dve", lambda e: e.tensor_tensor(out=dst, in0=O3[:, :, 0:64], in1=facb, op=ALU.mult),
                  reads=[B_sm], xreads=[PB[obk]], writes=[B_dst])
        else:
            P.add("dve", lambda e: e.tensor_tensor(out=otmp, in0=O3[:, :, 0:64], in1=facb, op=ALU.mult),
                  reads=[B_sm], xreads=[PB[obk]], writes=[B_otmp])
            P.add("dve", lambda e: e.tensor_tensor(out=dst, in0=dst, in1=otmp, op=ALU.add), reads=[B_otmp], writes=[B_dst])

    def importance(obk, sp, g):
        O3 = O3v(obk)
        P.add("dve", lambda e: e.tensor_scalar(out=sm[:, 16:20], in0=O3[:, :, 64], scalar1=1e-30, scalar2=None, op0=ALU.max),
              xreads=[PB[obk]], writes=[B_sm])
        P.add("dve", lambda e: e.reciprocal(out=sm[:, 20:24], in_=sm[:, 16:20]), writes=[B_sm])
        P.add("dve", lambda e: e.tensor_tensor(out=imp4.rearrange("p (a b) -> p a b", a=4), in0=bank[1][:, 0:256].rearrange("p (a b) -> p a b", a=4),
                                               in1=sm[:, 20:24].unsqueeze(2).to_broadcast([128, 4, 64]), op=ALU.mult),
              reads=[B_sm], xreads=[PB[1]], writes=[B_imp4])
        P.add("dve", lambda e: e.tensor_reduce(out=imp, in_=imp4.rearrange("p (r j) -> p j r", r=4), axis=mybir.AxisListType.X, op=ALU.add),
              reads=[B_imp4], writes=[B_imp])
        P.add("dve", lambda e: e.tensor_tensor(out=sc1, in0=imp, in1=cmfb[sp][:, 1, :], op=ALU.add), reads=[B_imp, B_cmfb[sp]], writes=[B_sc1])
        P.add("dve", lambda e: e.max(out=sm[:, 24:32], in_=sc1), reads=[B_sc1], writes=[B_sm])
        P.add("dve", lambda e: e.match_replace(out=sc2, in_to_replace=sm[:, 24:32], in_values=sc1, imm_value=-3e38),
              reads=[B_sc1, B_sm], writes=[B_sc2])
        P.add("dve", lambda e: e.max(out=sm[:, 32:40], in_=sc2), reads=[B_sc2], writes=[B_sm])
        P.add("dve", lambda e, g=g: e.tensor_scalar(out=selb[g][:, 0:61], in0=sc1[:, 1:62], scalar1=sm[:, 39:40], scalar2=NEGM, op0=ALU.is_lt, op1=ALU.mult),
              reads=[B_sc1, B_sm], writes=[B_selb[g]])

    def importance_b(g, sp):
        P.add("pe", lambda e, g=g: e.transpose(out=bankb[0][0:61, g * 128:(g + 1) * 128], in_=selb[g][:, 0:61], identity=ident_b),
              reads=[B_selb[g], B_const], writes=[PB[0]])
        P.add("dve", lambda e, g=g, sp=sp: e.tensor_copy(out=QS[sp][g][0:61, :].rearrange("p (a b) -> p a b", a=4),
                                                        in_=bankb[0][0:61, g * 128:(g + 1) * 128].unsqueeze(1).to_broadcast([61, 4, 128])),
              xreads=[PB[0]], writes=[B_QSs[sp][g]])

    def prologue_a(j):
        sp = j % 2
        dma("sp", xq[sp], xo[j * 128:(j + 1) * 128, :], B_xq[sp])
        rms_a(xq[sp], B_xq[sp], hq[sp], B_hq[sp])

    def prologue(j):
        sp = j % 2
        dma("pool", QT[sp][64:68, :, :], qaug_d[j].rearrange("r (a b) -> r a b", a=4), B_QTa[sp])
        for g in range(2):
            dma("pool", QS[sp][g][61:64, :], qaugs_d[j][:, g * 512:(g + 1) * 512], B_QSa[sp][g])
        dma("pool", cmk[sp], cmask_d[j].rearrange("c k q -> k c q"), B_cmk[sp])
        dma("sp", cmfb[sp], cmfb_d[j].rearrange("c q n -> q c n"), B_cmfb[sp])
        rms_b(hq[sp], B_hq[sp], lambda c, sp=sp: hTq[sp][:, c, :], B_hTq[sp], (0, 1))
        for tg in range(4):
            t, g = tg // 2, tg % 2
            pb = (1, 0)[tg % 2]
            for kc in range(8):
                for hp in range(2):
                    c0 = t * 512 + (g * 4 + 2 * hp) * 64
                    P.add("pe", lambda e, pb=pb, hp=hp, c0=c0, kc=kc, sp=sp: e.matmul(
                        bank[pb][:, hp * 128:(hp + 1) * 128], lhsT=wq[:, kc, c0:c0 + 128], rhs=hTq[sp][:, kc, :],
                        start=(kc == 0 and hp == 0), stop=(kc == 7), skip_group_check=True), reads=[B_wq[t], *B_hTq[sp]], writes=[PB[pb]])
            for odd in range(2):
                src = bank[pb][odd * 64:(odd + 1) * 64, 0:256].rearrange("p (a b) -> p a b", a=2)
                dstv = QT[sp][0:64, tg, :].rearrange("p (a two b) -> p a two b", a=2, two=2)[:, :, odd, :]
                P.add("dve", lambda e, src=src, dstv=dstv: e.tensor_scalar(out=dstv, in0=src, scalar1=0.125, scalar2=None, op0=ALU.mult),
                      xreads=[PB[pb]], pwrites=[B_QT[sp]])
                if t == 0:
                    dsts = QS[sp][g][64:128, :].rearrange("p (a two b) -> p a two b", a=2, two=2)[:, :, odd, :]
                    P.add("dve", lambda e, src=src, dsts=dsts: e.tensor_scalar(out=dsts, in0=src, scalar1=0.125, scalar2=None, op0=ALU.mult),
                          xreads=[PB[pb]], pwrites=[B_QSq[sp][g]])
        for zi in range(2):
            pb = (1, 0)[zi]
            for kc in range(8):
                P.add("pe", lambda e, pb=pb, zi=zi, kc=kc, sp=sp: e.matmul(
                    bank[pb], lhsT=hTq[sp][:, kc, :], rhs=wq[:, kc, 1024 + zi * 512:1536 + zi * 512], start=(kc == 0), stop=(kc == 7)),
                    reads=[B_wq[2 + zi], *B_hTq[sp]], writes=[PB[pb]])
            P.add("act", lambda e, pb=pb: e.activation(out=ze, in_=bank[pb], func=AF.Tanh, scale=0.5), xreads=[PB[pb]], writes=[B_ze])
            P.add("dve", lambda e, pb=pb, zi=zi, sp=sp: e.scalar_tensor_tensor(out=zs[sp][zi], in0=ze, scalar=1.0, in1=bank[pb],
                                                                              op0=ALU.add, op1=ALU.mult),
                  reads=[B_ze], xreads=[PB[pb]], writes=[B_zs[sp][zi]])
        for kc in range(8):
            P.add("pe", lambda e, kc=kc, sp=sp: e.matmul(bank[1][:, 0:24], lhsT=hTq[sp][:, kc, :], rhs=wq[:, kc, 2048:2072], start=(kc == 0), stop=(kc == 7)),
                  reads=[B_wq[4], *B_hTq[sp]], writes=[PB[1]])
        P.add("act", lambda e, sp=sp: e.activation(out=sg[sp], in_=bank[1][:, 0:24], func=AF.Exp, scale=-1.0), xreads=[PB[1]], writes=[B_sg[sp]])
        P.add("dve", lambda e, sp=sp: e.tensor_scalar(out=sg[sp], in0=sg[sp], scalar1=1.0, scalar2=None, op0=ALU.add), writes=[B_sg[sp]])
        P.add("dve", lambda e, sp=sp: e.reciprocal(out=sg[sp], in_=sg[sp]), writes=[B_sg[sp]])

    nslots = NSLOT if stop_after is None or not stop_after.startswith("slot") else int(stop_after[4:])
    prologue_a(0)
    prologue(0)
    wmi = {0: 2, 1: 3, 4: 4, 5: 5}
    for j in range(nslots):
        sp = j % 2
        if j + 1 < nslots:
            prologue_a(j + 1)
        chunks = [0] if j <= 7 else [0, 1]

        def cm_mask(cc, j=j, sp=sp):
            if cc == 0 and j >= 9:
                return None
            return cmk[sp][:, cc, :]
        for g in range(2):
            obk = attention(sp, 0 * 2 + g, KC[g], B_KC[g], chunks, lambda cc, g=g: VC[:, cc, g, 0:65], B_VC, None, cm_mask, B_cmk[sp],
                            extra_rhs=lambda cc, g=g: VC[:, cc, g, 65:129])
            pending.append(lambda obk=obk, sp=sp, g=g: (importance(obk, sp, g), delayed.append([2, lambda: importance_b(g, sp)])))
            pending.append(lambda obk=obk, sp=sp, g=g: finish(obk, sp, 0 * 8 + g * 4, oa[:, g], B_oa[g], True))
        for g in range(2):
            kbs = [kb for kb in range(2 * j - 1, 2 * j + 2) if kb >= 0]
            obk = attention(sp, 1 * 2 + g, KT[2][g], B_KT[2][g], kbs, lambda kb, g=g: Vt[:, kb, 2 * 2 + g, :], B_Vt, None,
                            lambda kb, j=j: masks[:, 6 + kb - (2 * j - 1), :], B_masks)
            pending.append(lambda obk=obk, sp=sp, g=g: finish(obk, sp, None, ob[:, g], B_ob[g], True, extra_den=es[:, g * 4:(g + 1) * 4]))
        for g in range(2):
            kbs = [kb for kb in range(2 * j - 4, 2 * j + 2) if kb >= 0]
            obk = attention(sp, 0 * 2 + g, KT[1][g], B_KT[1][g], kbs, lambda kb, g=g: Vt[:, kb, 1 * 2 + g, :], B_Vt, None,
                            lambda kb, j=j: masks[:, wmi[kb - (2 * j - 4)], :] if (kb - (2 * j - 4)) in wmi else None, B_masks)
            pending.append(lambda obk=obk, sp=sp, g=g: finish(obk, sp, 2 * 8 + g * 4, oa[:, g], B_oa[g], False))
        if j + 1 < nslots:
            flush()
            prologue(j + 1)
        flush(all_=True)
        for g in range(2):
            obk = attention(sp, 0 * 2 + g, KT[0][g], B_KT[0][g], list(range(2 * j + 2)), lambda kb, g=g: Vt[:, kb, 0 * 2 + g, :], B_Vt, g,
                            lambda kb, j=j: masks[:, kb - 2 * j, :] if kb >= 2 * j else None, B_masks)
            pending.append(lambda obk=obk, sp=sp, g=g: finish(obk, sp, 1 * 8 + g * 4, oa[:, g], B_oa[g], False))
        flush()
        for mi, (osrc, B_os) in enumerate(((oa, B_oa), (ob, B_ob))):
            P.add("dve", lambda e, osrc=osrc, mi=mi, sp=sp: e.tensor_tensor(out=ozb[mi], in0=osrc.rearrange("p a b c -> p (a b c)"), in1=zs[sp][mi], op=ALU.mult),
                  reads=[*B_os, B_zs[sp][mi]], writes=[B_ozb[mi]])

        def epi_b(j=j):
            for mi in range(2):
                for c in range(4):
                    P.add("pe", lambda e, c=c, mi=mi: e.transpose(out=bankb[0][:, c * 128:(c + 1) * 128], in_=ozb[mi][:, c * 128:(c + 1) * 128], identity=ident_b),
                          reads=[B_ozb[mi], B_const], writes=[PB[0]])
                P.add("dve", lambda e, j=j, mi=mi: e.tensor_copy(out=OZT[:, j, mi, :, :], in_=bankb[0][:, 0:512].rearrange("p (a b) -> p a b", a=4)),
                      xreads=[PB[0]], pwrites=[B_OZT])
        pending.append(epi_b)
    flush()
    dump("OZT", OZT, B_OZT, [128, NSLOT, 2, 4, 128])
    if stop_after is not None and (stop_after == "sweepA" or stop_after.startswith("slot")):
        P.emit(nc, final_waits=list(dbg_out.values()))
        return nc

    P.barrier()
    A.release(kv_mark)
    ggbc = A.alloc([128, 1024], F32)
    B_gg = P.buf("ggbc")
    dg = A.alloc([128, 128], F32)
    B_dg = P.buf("dg")
    for c in range(8):
        P.add("dve", lambda e, c=c: e.tensor_scalar(out=dg, in0=ident_f, scalar1=cols[:, 80 + c:81 + c], scalar2=None, op0=ALU.mult),
              reads=[B_identf, B_cols], writes=[B_dg])
        pb = c // 4
        P.add("pe", lambda e, c=c, pb=pb: e.matmul(bank[pb][:, (c % 4) * 128:(c % 4 + 1) * 128], lhsT=ones_f, rhs=dg, start=True, stop=True,
                                                   skip_group_check=True), reads=[B_ones, B_dg], writes=[PB[pb]])
    P.add("dve", lambda e: e.tensor_copy(out=ggbc, in_=ps[:, 0:1024]), writes=[PB[0], PB[1], B_gg])
    wm = A.alloc([128, 8, 2048], BF16)
    wob = A.alloc([128, 4, 1024], BF16)
    wout = A.alloc([128, 8, 1024], BF16)
    B_wm = [P.buf("wm%d" % i) for i in range(4)]
    B_wob, B_wout = P.buf("wob"), P.buf("wout")
    xb_ = [A.alloc([128, 1024], F32) for _ in range(5)]
    B_xb = [P.buf("xb%d" % i) for i in range(5)]
    hq2 = [A.alloc([128, 1024], BF16) for _ in range(2)]
    B_hq2 = [P.buf("hq2%d" % i) for i in range(2)]
    hTq2 = [A.alloc([128, 8, 128], BF16) for _ in range(2)]
    B_hTq2 = [(P.buf("hTq2a%d" % i), P.buf("hTq2b%d" % i)) for i in range(2)]
    sig = [[A.alloc([128, 1024], F32) for _ in range(2)] for _ in range(2)]
    B_sig = [[P.buf("sig%d%d" % (i, k)) for k in range(2)] for i in range(2)]
    uu = A.alloc([128, 1024], F32)
    B_uu = P.buf("uu")
    ub = [A.alloc([128, 1024], BF16) for _ in range(2)]
    B_ub = [P.buf("ub%d" % i) for i in range(2)]
    uT = A.alloc([128, 8, 128], BF16)
    B_uT = P.buf("uT")
    fin = A.alloc([128, 1024], F32)
    B_fin = P.buf("fin")

    def stage0a(j):
        xi, s2 = j % 5, j % 2
        if j >= 2:
            dma("pool", xb_[xi], xo[j * 128:(j + 1) * 128, :], B_xb[xi], reads=[])
        rms_a(xb_[xi], B_xb[xi], hq2[s2], B_hq2[s2])

    def stage0b(j):
        s2 = j % 2
        rms_b(hq2[s2], B_hq2[s2], lambda c, s2=s2: hTq2[s2][:, c, :], B_hTq2[s2], (0, 5))

    def stage1(j, mi):
        xi, s2 = j % 5, j % 2
        if True:
            for hf in range(2):
                pb = 1 + hf
                for kc in range(8):
                    P.add("pe", lambda e, pb=pb, mi=mi, hf=hf, kc=kc, s2=s2: e.matmul(
                        bank[pb], lhsT=hTq2[s2][:, kc, :], rhs=wm[:, kc, mi * 1024 + hf * 512: mi * 1024 + (hf + 1) * 512],
                        start=(kc == 0), stop=(kc == 7)), reads=[B_wm[mi * 2 + hf], *B_hTq2[s2]], writes=[PB[pb]])
            P.add("act", lambda e, mi=mi, s2=s2: e.activation(out=sig[s2][mi], in_=ps[:, 512:1536], func=AF.Tanh, scale=0.5),
                  xreads=[PB[1], PB[2]], writes=[B_sig[s2][mi]])

    def stage2(j):
        s2 = j % 2
        for mi in range(2):
            wo, B_wo = (woa, B_woa) if mi == 0 else (wob, B_wob)
            for hf in range(2):
                pb = 3 + hf
                for c in range(4):
                    P.add("pe", lambda e, pb=pb, mi=mi, hf=hf, c=c, wo=wo, j=j: e.matmul(
                        bank[pb], lhsT=OZT[:, j, mi, c, :], rhs=wo[:, c, hf * 512:(hf + 1) * 512], start=(c == 0), stop=(c == 3)),
                        reads=[B_wo, B_OZT], writes=[PB[pb]])
            if mi == 0:
                P.add("dve", lambda e, s2=s2: e.scalar_tensor_tensor(out=uu, in0=sig[s2][0], scalar=1.0, in1=ps[:, 1536:2560], op0=ALU.add, op1=ALU.mult),
                      reads=[B_sig[s2][0]], xreads=[PB[3], PB[4]], writes=[B_uu])
            else:
                P.add("dve", lambda e, s2=s2: e.scalar_tensor_tensor(out=sig[s2][1], in0=sig[s2][1], scalar=1.0, in1=ps[:, 1536:2560], op0=ALU.add, op1=ALU.mult),
                      xreads=[PB[3], PB[4]], writes=[B_sig[s2][1]])
                P.add("dve", lambda e, s2=s2: e.tensor_tensor(out=ub[s2], in0=uu, in1=sig[s2][1], op=ALU.add),
                      reads=[B_uu, B_sig[s2][1]], writes=[B_ub[s2]])

    def stage3a(j):
        s2 = j % 2
        for c in range(8):
            P.add("pe", lambda e, c=c, s2=s2: e.transpose(out=bankb[5][:, c * 128:(c + 1) * 128], in_=ub[s2][:, c * 128:(c + 1) * 128], identity=ident_b),
                  reads=[B_ub[s2], B_const], writes=[PB[5]])
        P.add("act", lambda e: e.copy(out=uT.rearrange("p a b -> p (a b)"), in_=bankb[5]), xreads=[PB[5]], writes=[B_uT])

    def stage3b(j):
        xi = j % 5
        xq2 = xb_[xi]
        for hf in range(2):
            pb = 6 + hf
            for c in range(8):
                P.add("pe", lambda e, pb=pb, hf=hf, c=c: e.matmul(bank[pb], lhsT=uT[:, c, :], rhs=wout[:, c, hf * 512:(hf + 1) * 512],
                                                                  start=(c == 0), stop=(c == 7)), reads=[B_wout, B_uT], writes=[PB[pb]])
        sc = A_scr
        P.add("act", lambda e: e.activation(out=junk, in_=ps[:, 3072:4096], func=AF.Square, accum_out=sc[:, 4:5]),
              xreads=[PB[6], PB[7]], writes=[B_junk, B_scr2])
        P.add("act", lambda e: e.activation(out=sc[:, 5:6], in_=sc[:, 4:5], func=AF.Ln, scale=1.0 / D, bias=epsT[:, 1:2]),
              reads=[B_eps], writes=[B_scr2])
        P.add("act", lambda e: e.activation(out=sc[:, 6:7], in_=sc[:, 5:6], func=AF.Exp, scale=-0.5), writes=[B_scr2])
        P.add("dve", lambda e: e.scalar_tensor_tensor(out=fin, in0=ps[:, 3072:4096], scalar=sc[:, 6:7], in1=ggbc, op0=ALU.mult, op1=ALU.mult),
              reads=[B_scr2, B_gg], xreads=[PB[6], PB[7]], writes=[B_fin])
        P.add("dve", lambda e, xq2=xq2: e.tensor_tensor(out=xq2, in0=xq2, in1=fin, op=ALU.add), reads=[B_fin], writes=[B_xb[xi]])
        P.add("sp", lambda e, xq2=xq2, j=j: e.dma_start(out=out_d[j * 128:(j + 1) * 128, :], in_=xq2), reads=[B_xb[xi]], pwrites=[out_bufs[xi]],
              dma=out_bufs[xi])

    B_scr2 = P.buf("scr2")
    for j0 in range(2):
        dma("pool", xb_[j0], xo[j0 * 128:(j0 + 1) * 128, :], B_xb[j0], reads=[])
    for hf in range(4):
        dma("pool", wm[:, :, hf * 512:(hf + 1) * 512], w_in[:, 3096 + hf * 512:3096 + (hf + 1) * 512].rearrange("(kc p) n -> p kc n", p=128),
            B_wm[hf])
    dma("pool", wob, wosw_d.rearrange("(c p) n -> p c n", p=128), B_wob)
    dma("pool", wout, wout_d.rearrange("(c p) n -> p c n", p=128), B_wout)
    stage0a(0)
    stage0b(0)
    stage0a(1)
    for it in range(NSLOT + 2):
        if it < NSLOT:
            stage1(it, 0)
        if it - 2 >= 0:
            stage3a(it - 2)
        if it + 1 < NSLOT:
            stage0b(it + 1)
        if it < NSLOT:
            stage1(it, 1)
        if 0 <= it - 1 < NSLOT:
            stage2(it - 1)
        if it - 2 >= 0:
            stage3b(it - 2)
        if it + 2 < NSLOT:
            stage0a(it + 2)
    P.emit(nc, final_waits=out_bufs + list(dbg_out.values()))
    return nc


def _inputs_for_core(core, inp, shared):
    b, p = core // 2, core % 2
    f = lambda a: np.ascontiguousarray(a, dtype=np.float32)
    x = inp["x"][b]
    m = dict(shared)
    m["xf"] = f(x)
    m["xo"] = f(x.reshape(32, 128, D)[p::2].reshape(NSLOT * 128, D))
    m["cT"] = f(inp["c"][b].reshape(8, 128).T)
    m.update(_core_tables(p))
    return m


def kernel(debug=None, stop_after=None, cores=None, **inp):
    inp = {k: np.asarray(v) for k, v in inp.items()}
    f = lambda a: np.ascontiguousarray(a, dtype=np.float32)
    shared = _shared_tables()
    shared["w_ada"] = f(inp["w_ada"][0])
    shared["badaT"] = f(inp["b_ada"][0].reshape(24, 128).T)
    shared["gpreT"] = f(inp["g_pre"][0].reshape(8, 128).T)
    shared["gpostT"] = f(inp["g_post"][0].reshape(8, 128).T)
    shared["w_in"] = f(inp["w_in"][0])
    shared["pekT"] = f(np.stack([inp["pe_cmp_k"][0].T, np.zeros((64, 32))], -1).reshape(64, 64))
    shared["pevT"] = f(np.stack([inp["pe_cmp_v"][0].T, np.zeros((64, 32))], -1).reshape(64, 64))
    shared["w1k"] = f(inp["w_cmp_k1"][0])
    shared["w2k"] = f(inp["w_cmp_k2"][0])
    shared["w1v"] = f(inp["w_cmp_v1"][0])
    shared["w2v"] = f(inp["w_cmp_v2"][0])
    shared["w_o_nsa"] = f(inp["w_o_nsa"][0])
    shared["w_o_swa"] = f(inp["w_o_swa"][0])
    shared["w_out"] = f(inp["w_out"][0])
    shared["sinksb"] = f(np.tile(inp["sinks"][0][None, :], (128, 1)))
    nc = build_nc(debug=debug, stop_after=stop_after)
    cores = list(range(8)) if cores is None else cores
    in_maps = [_inputs_for_core(c, inp, shared) for c in cores]
    if stop_after == "phase1" or (stop_after or "").startswith("slot") or stop_after == "sweepA":
        for m in in_maps:
            pass
    res = run_bass_kernel_spmd(nc, in_maps, core_ids=list(range(len(cores))))
    if stop_after:
        return res.results
    out = np.zeros((4, 32, 128, D), np.float32)
    for i, c in enumerate(cores):
        b, p = c // 2, c % 2
        out[b, p::2] = res.results[i]["out"].reshape(NSLOT, 128, D)
    if debug:
        return out.reshape(4, S, D), res.results
    return out.reshape(4, S, D)
```

```python
import numpy as np
from contextlib import ExitStack
import concourse.bass as bass
import concourse.mybir as mybir
from concourse.bass_utils import run_bass_kernel_spmd

F32 = mybir.dt.float32
BF16 = mybir.dt.bfloat16
AF = mybir.ActivationFunctionType
ALU = mybir.AluOpType

NEGM = -30000.0
IMP_INLINE = True
S = 4096
D = 1024
NSLOT = 16


class Buf:
    __slots__ = ("name", "w", "r", "xr", "dcount", "sem", "last")

    def __init__(self, name):
        self.name = name
        self.w = None
        self.r = {}
        self.xr = {}
        self.dcount = 0
        self.sem = None
        self.last = None


class Op:
    __slots__ = ("eng", "fn", "deps", "signal", "dma", "sigval")

    def __init__(self, eng, fn, dma):
        self.eng = eng
        self.fn = fn
        self.dma = dma
        self.deps = []
        self.signal = False
        self.sigval = 0


class Prog:
    ENGS = ("pe", "act", "dve", "pool", "sp")

    def __init__(self):
        self.ops = {e: [] for e in self.ENGS}
        self.dma_bufs = []
        self.pending = {e: [] for e in self.ENGS}

    def buf(self, name):
        return Buf(name)

    def barrier(self):
        lasts = [self.ops[e][-1] for e in self.ENGS if self.ops[e]]
        lasts += [b.last for b in self.dma_bufs if b.last is not None]
        for e in self.ENGS:
            self.pending[e] = list(lasts)

    def add(self, eng, fn, reads=(), writes=(), pwrites=(), dma=None, xreads=()):
        op = Op(eng, fn, dma)
        deps = {}
        for b in reads:
            if b.w is not None:
                deps[id(b.w)] = b.w
        for b in xreads:
            if b.w is not None:
                deps[id(b.w)] = b.w
            for k_, r in b.xr.items():
                if k_ != eng:
                    deps[id(r)] = r
        for b in list(writes) + list(pwrites):
            for r in b.r.values():
                deps[id(r)] = r
            for r in b.xr.values():
                deps[id(r)] = r
        for b in writes:
            if b.w is not None:
                deps[id(b.w)] = b.w
        for d in self.pending[eng]:
            deps[id(d)] = d
        self.pending[eng] = []
        for d in deps.values():
            if d.dma is None and dma is None and d.eng == "pe" and eng == "pe":
                continue
            if d.dma is None and d.eng == eng and eng == "sp":
                continue
            op.deps.append(d)
            d.signal = True
        key = eng if dma is None else ("dma", id(dma))
        for b in reads:
            b.r[key] = op
        for b in xreads:
            b.xr[eng] = op
        for b in list(writes) + list(pwrites):
            b.w = op
            b.r = {}
            b.xr = {}
        if dma is not None:
            if dma not in self.dma_bufs:
                self.dma_bufs.append(dma)
            dma.dcount += 16
            dma.last = op
            op.sigval = dma.dcount
            op.signal = True
        self.ops[eng].append(op)
        return op

    def emit(self, nc, final_waits=()):
        with ExitStack() as st:
            esem = {e: st.enter_context(nc.semaphore("sem_" + e)) for e in self.ENGS}
            for i, b in enumerate(self.dma_bufs):
                b.sem = st.enter_context(nc.semaphore("dsem%d" % i))
            for e in self.ENGS:
                c = 0
                for op in self.ops[e]:
                    if op.dma is None and op.signal:
                        c += 1
                        op.sigval = c
            block = st.enter_context(nc.Block())

            def run(eng_name, engine):
                waited = {}
                for op in self.ops[eng_name]:
                    for d in op.deps:
                        if d.dma is not None:
                            sem, key = d.dma.sem, ("d", id(d.dma))
                        else:
                            sem, key = esem[d.eng], d.eng
                        if waited.get(key, 0) >= d.sigval:
                            continue
                        waited[key] = d.sigval
                        engine.wait_ge(sem, d.sigval)
                    ins = op.fn(engine)
                    if op.dma is not None:
                        ins.then_inc(op.dma.sem, 16)
                    elif op.signal:
                        ins.then_inc(esem[eng_name], 1)
                if eng_name == "sp":
                    for b in final_waits:
                        engine.wait_ge(b.sem, b.dcount)

            @block.tensor
            def _(e):
                run("pe", e)

            @block.scalar
            def _(e):
                run("act", e)

            @block.vector
            def _(e):
                run("dve", e)

            @block.gpsimd
            def _(e):
                run("pool", e)

            @block.sync
            def _(e):
                run("sp", e)


class Arena:
    def __init__(self, nc, st, nbytes):
        self.t = st.enter_context(nc.sbuf_tensor("arena", [128, nbytes // 2], BF16))
        self.top = 0
        self.cap = nbytes

    def alloc(self, shape, dtype):
        esz = 4 if dtype == F32 else 2
        n = 1
        for s_ in shape[1:]:
            n *= s_
        nb = (n * esz + 31) // 32 * 32
        off = self.top
        self.top += nb
        assert self.top <= self.cap, ("SBUF arena overflow", self.top, self.cap)
        v = self.t[:, off // 2: off // 2 + n * esz // 2]
        if dtype == F32:
            v = v.bitcast(F32)
        if len(shape) == 3:
            v = v.rearrange("p (a b) -> p a b", a=shape[1])
        elif len(shape) == 4:
            v = v.rearrange("p (a b c) -> p a b c", a=shape[1], b=shape[2])
        elif len(shape) == 5:
            v = v.rearrange("p (a b c d) -> p a b c d", a=shape[1], b=shape[2], c=shape[3])
        return v

    def mark(self):
        return self.top

    def release(self, m):
        self.top = m


def _slopes():
    return 2.0 ** (-(np.arange(8) + 1.0))


def _shared_tables():
    t = {}
    t["ident"] = np.eye(128, dtype=np.float32)
    kpos = np.arange(S)
    t["kaug"] = np.stack([np.ones(S), np.ones(S), kpos // 128, kpos % 128]).astype(np.float32)
    c = np.arange(256)
    end = 16 * c + 31
    t["caug"] = np.stack([np.ones(256), np.ones(256), end // 128, end % 128]).astype(np.float32)
    M = np.zeros((256, 64), np.float32)
    for j in range(64):
        for m in range(4):
            for n in range(2):
                ci = 4 * j - m - n
                if 0 <= ci < 255:
                    M[ci, j] += 1.0
    t["mimp"] = M
    R = np.zeros((64, S), np.float32)
    R[kpos // 64, kpos] = 1.0
    t["ksel"] = np.concatenate([R[1:62], np.stack([np.ones(S), kpos // 128, kpos % 128])]).astype(np.float32)
    return t


def _core_tables(p):
    sl = _slopes()
    t = {}
    qa = np.zeros((NSLOT, 4, 4, 4, 128), np.float32)
    q = np.arange(128)
    for j in range(NSLOT):
        for tg in range(4):
            g = tg % 2
            for r in range(4):
                s_ = sl[g * 4 + r]
                qa[j, 0, tg, r] = -s_ * 128.0 * (2 * j + p)
                qa[j, 1, tg, r] = -s_ * q
                qa[j, 2, tg, r] = s_ * 128.0
                qa[j, 3, tg, r] = s_
    t["qaug"] = qa.reshape(NSLOT, 4, 2048)
    qs = np.zeros((NSLOT, 3, 2, 4, 128), np.float32)
    for j in range(NSLOT):
        qpos = (2 * j + p) * 128 + q
        for g in range(2):
            for r in range(4):
                s_ = sl[g * 4 + r]
                qs[j, 0, g, r] = -s_ * qpos
                qs[j, 1, g, r] = s_ * 128.0
                qs[j, 2, g, r] = s_
    t["qaugs"] = qs.reshape(NSLOT, 3, 1024)
    k = np.arange(128)[:, None]
    qq = np.arange(128)[None, :]
    masks = np.zeros((10, 128, 128), np.float32)
    for i in range(2):
        dist = (p - i) * 128 + qq - k
        masks[i] = np.where(dist >= 0, 0.0, NEGM)
    for n, i in enumerate((0, 1, 4, 5)):
        dist = (p - i + 4) * 128 + qq - k
        masks[2 + n] = np.where((dist >= 0) & (dist < 512), 0.0, NEGM)
    for i in range(3):
        dist = (p - i + 1) * 128 + qq - k
        masks[6 + i] = np.where((dist >= 0) & (dist < 128), 0.0, NEGM)
    t["masks"] = masks.transpose(1, 0, 2).copy()
    cm = np.zeros((NSLOT, 2, 128, 128), np.float32)
    for j in range(NSLOT):
        qpos = (2 * j + p) * 128 + qq
        for cc in range(2):
            c = cc * 128 + k
            ok = (qpos >= 16 * c + 31) & (c < 255)
            cm[j, cc] = np.where(ok, 0.0, NEGM)
    t["cmask"] = cm
    cf = np.zeros((NSLOT, 2, 128, 64), np.float32)
    jj = np.arange(64)[None, :]
    for j in range(NSLOT):
        qpos = (2 * j + p) * 128 + np.arange(128)[:, None]
        cur = qpos // 64
        causal = jj <= cur
        fb = np.where(causal, 0.0, -1e30)
        fb = np.where(causal & (jj == 0), 1e30, fb)
        fb = np.where(causal & (jj == cur - 1), 2e30, fb)
        fb = np.where(causal & (jj == cur), 4e30, fb)
        cf[j, 0] = causal.astype(np.float32)
        cf[j, 1] = fb
    t["cmfb"] = cf
    return t


def build_nc(debug=None, stop_after=None):
    debug = debug or []
    nc = bass.Bass("TRN2", target_bir_lowering=False)
    P = Prog()
    st = ExitStack()

    def din(name, shape):
        return nc.dram_tensor(name, list(shape), F32, kind="ExternalInput").ap()

    xf = din("xf", [S, D])
    xo = din("xo", [NSLOT * 128, D])
    cT_d = din("cT", [128, 8])
    w_ada = din("w_ada", [D, 3 * D])
    badaT = din("badaT", [128, 24])
    gpreT = din("gpreT", [128, 8])
    gpostT = din("gpostT", [128, 8])
    w_in = din("w_in", [D, 5144])
    pekT = din("pekT", [64, 64])
    pevT = din("pevT", [64, 64])
    w1k_d = din("w1k", [2048, 256])
    w2k_d = din("w2k", [256, 64])
    w1v_d = din("w1v", [2048, 256])
    w2v_d = din("w2v", [256, 64])
    wona_d = din("w_o_nsa", [512, D])
    wosw_d = din("w_o_swa", [512, D])
    wout_d = din("w_out", [D, D])
    sinks_d = din("sinksb", [128, 8])
    ident_d = din("ident", [128, 128])
    kaug_d = din("kaug", [4, S])
    caug_d = din("caug", [4, 256])
    mimp_d = din("mimp", [256, 64])
    ksel_d = din("ksel", [64, S])
    qaugs_d = din("qaugs", [NSLOT, 3, 1024])
    qaug_d = din("qaug", [NSLOT, 4, 2048])
    masks_d = din("masks", [128, 10, 128])
    cmask_d = din("cmask", [NSLOT, 2, 128, 128])
    cmfb_d = din("cmfb", [NSLOT, 2, 128, 64])
    out_d = nc.dram_tensor("out", [NSLOT * 128, D], F32, kind="ExternalOutput").ap()
    out_bufs = [P.buf("out%d" % i) for i in range(5)]
    dbg_out = {}

    A = Arena(nc, st, 212800)
    ps = st.enter_context(nc.psum_tensor("ps", [128, 4096], F32))
    bank = [ps[:, i * 512:(i + 1) * 512] for i in range(8)]
    bankb = [bank[i].bitcast(BF16) for i in range(8)]
    PB = [P.buf("bank%d" % i) for i in range(8)]

    def dma(eng, out, in_, b, reads=(), part=False):
        if part:
            return P.add(eng, lambda e: e.dma_start(out=out, in_=in_), reads=reads, pwrites=[b], dma=b)
        return P.add(eng, lambda e: e.dma_start(out=out, in_=in_), reads=reads, writes=[b], dma=b)

    def dump(name, ap, b, shape):
        if name not in debug:
            return
        dt = nc.dram_tensor("dbg_" + name, list(shape), ap.dtype, kind="ExternalOutput").ap()
        db = P.buf("dbg_" + name)
        P.add("sp", lambda e: e.dma_start(out=dt, in_=ap), reads=[b], writes=[db], dma=db)
        dbg_out[name] = db

    ident_f = A.alloc([128, 128], F32)
    ident_b = A.alloc([128, 128], BF16)
    ones_f = A.alloc([128, 128], F32)
    cols = A.alloc([128, 96], F32)
    B_const = P.buf("const")
    B_identf = P.buf("identf")
    B_ones = P.buf("ones")
    B_cols = P.buf("cols")
    B_eps = P.buf("eps")
    epsT = A.alloc([128, 8], F32)
    dma("sp", ident_f, ident_d[:, :], B_identf)
    dma("pool", ident_b, ident_d[:, :], B_const)
    P.add("pool", lambda e: e.memset(ones_f, 1.0), writes=[B_ones])
    P.add("pool", lambda e: e.memset(epsT, 1e-6), writes=[B_eps])
    P.add("pool", lambda e: e.memset(epsT[:, 1:2], 16e-6), writes=[B_eps])
    P.add("pool", lambda e: e.memset(cols, 0.0), writes=[B_cols])
    dma("sp", cols[:, 0:8], cT_d[:, :], B_cols)
    dma("sp", cols[:, 8:32], badaT[:, :], B_cols, part=True)
    dma("sp", cols[:, 32:40], gpreT[:, :], B_cols, part=True)
    dma("sp", cols[:, 40:48], gpostT[:, :], B_cols, part=True)
    sinks_t = A.alloc([128, 8], F32)
    B_sinks = P.buf("sinks")
    dma("sp", sinks_t, sinks_d[:, :], B_sinks)

    A_scr = A.alloc([128, 8], F32)
    B_scr = P.buf("scr")
    junk = A.alloc([128, 1024], BF16)
    B_junk = P.buf("junk")
    XR = A.alloc([128, 16384], BF16)
    kv_mark = A.mark()
    KT = [[A.alloc([128, S], BF16) for g in range(2)] for t in range(3)]
    B_KT = [[P.buf("KT%d%d" % (t, g)) for g in range(2)] for t in range(3)]
    Vt = A.alloc([128, 32, 6, 65], BF16)
    B_Vt = P.buf("Vt")
    KC = [A.alloc([128, 256], BF16) for g in range(2)]
    B_KC = [P.buf("KC%d" % g) for g in range(2)]
    VC = A.alloc([128, 2, 2, 129], BF16)
    B_VC = P.buf("VC")
    for t in range(3):
        for g in range(2):
            if t == 0:
                dma("pool", KT[t][g][0:64, :], ksel_d[:, :], B_KT[t][g], part=True)
            else:
                dma("pool", KT[t][g][64:68, :], kaug_d[:, :], B_KT[t][g], part=True)
    for g in range(2):
        P.add("pool", lambda e, g=g: e.memset(KC[g][0:64, :], 0.0), pwrites=[B_KC[g]])
        dma("pool", KC[g][64:68, :], caug_d[:, :], B_KC[g], part=True)
    P.add("pool", lambda e: e.memset(Vt[:, :, :, 64:65], 1.0), pwrites=[B_Vt])
    P.add("pool", lambda e: e.memset(VC[:, :, :, 64:65], 1.0), pwrites=[B_VC])
    for g in range(2):
        dma("pool", VC[:, :, g, 65:129], mimp_d.rearrange("(cc p) j -> p cc j", p=128), B_VC, part=True)

    small_mark = A.mark()
    if stop_after == "consts":
        dump("VC", VC, B_VC, [128, 2, 2, 129])
        dump("KC0", KC[0][0:68, :], B_KC[0], [68, 256])
        P.emit(nc, final_waits=list(dbg_out.values()))
        return nc

    wst = [A.alloc([128, 1536], F32) for _ in range(2)]
    B_wst = [P.buf("wst%d" % i) for i in range(2)]
    first = True
    for kc in range(8):
        for hf in range(2):
            i = (kc * 2 + hf) % 2
            dma("sp", wst[i], w_ada[kc * 128:(kc + 1) * 128, hf * 1536:(hf + 1) * 1536], B_wst[i])
            for n in range(12):
                col = hf * 12 + n
                P.add("pe", lambda e, i=i, n=n, col=col, kc=kc, first=first: e.matmul(
                    bank[0][:, col:col + 1], lhsT=wst[i][:, n * 128:(n + 1) * 128], rhs=cols[:, kc:kc + 1],
                    start=first, stop=(kc == 7), skip_group_check=True),
                    reads=[B_wst[i], B_cols], writes=[PB[0]])
                first = False
    P.add("dve", lambda e: e.tensor_tensor(out=cols[:, 48:72], in0=bank[0][:, 0:24], in1=cols[:, 8:32], op=ALU.add),
          reads=[B_cols], writes=[PB[0]], pwrites=[B_cols])
    P.add("dve", lambda e: e.scalar_tensor_tensor(out=cols[:, 72:80], in0=cols[:, 56:64], scalar=1.0, in1=cols[:, 32:40],
                                                  op0=ALU.add, op1=ALU.mult), reads=[B_cols], writes=[B_cols])
    P.add("dve", lambda e: e.tensor_tensor(out=cols[:, 80:88], in0=cols[:, 64:72], in1=cols[:, 40:48], op=ALU.mult),
          reads=[B_cols], writes=[B_cols])
    dump("cols", cols, B_cols, [128, 96])
    if stop_after == "phase0":
        P.emit(nc, final_waits=list(dbg_out.values()))
        return nc

    def rms_a(xt, B_xt, hn, B_hn):
        sc = A_scr
        P.add("act", lambda e: e.activation(out=junk, in_=xt, func=AF.Square, accum_out=sc[:, 0:1]),
              reads=[B_xt], writes=[B_junk, B_scr])
        P.add("act", lambda e: e.activation(out=sc[:, 1:2], in_=sc[:, 0:1], func=AF.Ln, scale=1.0 / D, bias=epsT[:, 0:1]),
              reads=[B_eps], writes=[B_scr])
        P.add("act", lambda e: e.activation(out=sc[:, 2:3], in_=sc[:, 1:2], func=AF.Exp, scale=-0.5), writes=[B_scr])
        P.add("dve", lambda e: e.tensor_scalar(out=hn, in0=xt, scalar1=sc[:, 2:3], scalar2=None, op0=ALU.mult),
              reads=[B_xt, B_scr], writes=[B_hn])

    def rms_b(hn, B_hn, hT_dst, B_hT, tbanks, act_from=4):
        for c in range(8):
            tb = tbanks[c // 4]
            P.add("pe", lambda e, c=c, tb=tb: e.transpose(out=bankb[tb][:, (c % 4) * 128:(c % 4 + 1) * 128], in_=hn[:, c * 128:(c + 1) * 128],
                                                         identity=ident_b), reads=[B_hn, B_const], writes=[PB[tb]])
        for c in range(8):
            tb = tbanks[c // 4]
            src = bankb[tb][:, (c % 4) * 128:(c % 4 + 1) * 128]
            if c >= act_from:
                P.add("act", lambda e, c=c, src=src: e.activation(out=hT_dst(c), in_=src, func=AF.Identity,
                                                                  scale=cols[:, 72 + c:73 + c], bias=cols[:, 48 + c:49 + c]),
                      reads=[B_cols], xreads=[PB[tb]], pwrites=[B_hT[1]])
            else:
                P.add("dve", lambda e, c=c, src=src: e.tensor_scalar(out=hT_dst(c), in0=src,
                                                                     scalar1=cols[:, 72 + c:73 + c], scalar2=cols[:, 48 + c:49 + c],
                                                                     op0=ALU.mult, op1=ALU.add),
                      reads=[B_cols], xreads=[PB[tb]], pwrites=[B_hT[0]])

    wkv = A.alloc([128, 8, 1024], BF16)
    B_wkv = P.buf("wkv")
    kvcols = [512, 640, 768, 1024, 2328, 896, 1152, 2456]
    for i, c0 in enumerate(kvcols):
        dma("pool", wkv[:, :, i * 128:(i + 1) * 128], w_in[:, c0:c0 + 128].rearrange("(kc p) n -> p kc n", p=128), B_wkv, part=True)
    xt = [A.alloc([128, 1024], F32) for _ in range(2)]
    B_xt = [P.buf("xt%d" % i) for i in range(2)]
    hn = [A.alloc([128, 1024], BF16) for _ in range(2)]
    B_hn = [P.buf("hn%d" % i) for i in range(2)]
    hTc = [A.alloc([128, 8, 512], BF16) for _ in range(2)]
    B_hTc = [(P.buf("hTc%da" % i), P.buf("hTc%db" % i)) for i in range(2)]
    rawk = A.alloc([128, S], BF16)
    rawv = A.alloc([128, S], BF16)
    B_rawk, B_rawv = P.buf("rawk"), P.buf("rawv")
    rawk3 = rawk.rearrange("p (r m) -> p r m", r=16)
    rawv3 = rawv.rearrange("p (r m) -> p r m", r=16)
    w1k = XR[:, 0:8192].rearrange("p (a b) -> p a b", a=32)
    w1v = XR[:, 8192:16384].rearrange("p (a b) -> p a b", a=32)
    w2k = A.alloc([128, 2, 64], BF16)
    w2v = A.alloc([128, 2, 64], BF16)
    peT = A.alloc([128, 2, 64], BF16)
    B_w1 = P.buf("w1")
    for hh in range(2):
        dma("pool", w1k[hh * 64:(hh + 1) * 64], w1k_d.rearrange("(l d) h -> d l h", d=64), B_w1, part=True)
        dma("pool", w1v[hh * 64:(hh + 1) * 64], w1v_d.rearrange("(l d) h -> d l h", d=64), B_w1, part=True)
    dma("pool", w2k, w2k_d.rearrange("(hc p) d -> p hc d", p=128), B_w1, part=True)
    dma("pool", w2v, w2v_d.rearrange("(hc p) d -> p hc d", p=128), B_w1, part=True)
    dma("pool", peT[0:64, 0, :], pekT[:, :], B_w1, part=True)
    dma("pool", peT[0:64, 1, :], pevT[:, :], B_w1, part=True)

    def p1_a(tb):
        xi = tb % 2
        dma("sp", xt[xi], xf[tb * 128:(tb + 1) * 128, :], B_xt[xi])
        rms_a(xt[xi], B_xt[xi], hn[xi], B_hn[xi])

    def kgroup(ch, i):
        hb = ch % 2
        pb = 1 + (i % 2)
        for kc in range(8):
            P.add("pe", lambda e, i=i, kc=kc, pb=pb, hb=hb: e.matmul(
                bank[pb], lhsT=wkv[:, kc, i * 128:(i + 1) * 128], rhs=hTc[hb][:, kc, :], start=(kc == 0), stop=(kc == 7)),
                reads=[B_wkv, *B_hTc[hb]], writes=[PB[pb]])
        csl = slice(ch * 512, (ch + 1) * 512)
        if i == 0:
            P.add("act", lambda e, pb=pb, ch=ch: e.copy(out=rawk3[:, :, ch * 32:(ch + 1) * 32], in_=bank[pb].rearrange("p (m r) -> p r m", r=16)),
                  xreads=[PB[pb]], pwrites=[B_rawk])
        elif i == 1:
            P.add("dve", lambda e, pb=pb, ch=ch: e.tensor_copy(out=rawv3[:, :, ch * 32:(ch + 1) * 32], in_=bank[pb].rearrange("p (m r) -> p r m", r=16)),
                  xreads=[PB[pb]], pwrites=[B_rawv])
        else:
            t = i - 2
            dlo = 64 if t == 0 else 0
            P.add("act", lambda e, pb=pb, csl=csl, t=t, dlo=dlo: e.copy(out=KT[t][0][dlo:dlo + 64, csl], in_=bank[pb][0:64, :]),
                  xreads=[PB[pb]], pwrites=[B_KT[t][0]])
            P.add("dve", lambda e, pb=pb, csl=csl, t=t, dlo=dlo: e.tensor_copy(out=KT[t][1][dlo:dlo + 64, csl], in_=bank[pb][64:128, :]),
                  xreads=[PB[pb]], pwrites=[B_KT[t][1]])

    def vgroup(ch, bi):
        hb = ch % 2
        tb = ch * 4 + bi
        for kc in range(8):
            P.add("pe", lambda e, kc=kc, hb=hb, bi=bi: e.matmul(
                bank[3][:, 0:384], lhsT=hTc[hb][:, kc, bi * 128:(bi + 1) * 128], rhs=wkv[:, kc, 640:1024],
                start=(kc == 0), stop=(kc == 7)), reads=[B_wkv, *B_hTc[hb]], writes=[PB[3]])
        if bi % 2 == 0:
            P.add("act", lambda e, tb=tb: e.copy(out=Vt[:, tb, :, 0:64], in_=bank[3][:, 0:384].rearrange("p (a b) -> p a b", a=6)),
                  xreads=[PB[3]], pwrites=[B_Vt])
        else:
            P.add("dve", lambda e, tb=tb: e.tensor_copy(out=Vt[:, tb, :, 0:64], in_=bank[3][:, 0:384].rearrange("p (a b) -> p a b", a=6)),
                  xreads=[PB[3]], pwrites=[B_Vt])

    def block(tb):
        ch, bi = tb // 4, tb % 4
        hb, xi = ch % 2, tb % 2
        if tb + 1 < 32:
            p1_a(tb + 1)
        rms_b(hn[xi], B_hn[xi], lambda c, hb=hb, bi=bi: hTc[hb][:, c, bi * 128:(bi + 1) * 128], B_hTc[hb], (0, 7))

    p1_a(0)
    for ch in range(9):
        groups = []
        if ch >= 1:
            groups = [lambda i=i, c_=ch - 1: kgroup(c_, i) for i in range(5)] + [lambda b_=b_, c_=ch - 1: vgroup(c_, b_) for b_ in range(4)]
        blocks = [lambda tb=ch * 4 + bi: block(tb) for bi in range(4)] if ch < 8 else []
        order = []
        gi = 0
        for bi in range(4):
            order += groups[gi:gi + 2]
            gi += 2
            if bi < len(blocks):
                order.append(blocks[bi])
        order += groups[gi:]
        for f_ in order:
            f_()

    if stop_after == "phase1a":
        dump("KTs0", KT[0][0][0:68, :], B_KT[0][0], [68, S])
        dump("Vt", Vt, B_Vt, [128, 32, 6, 65])
        P.emit(nc, final_waits=list(dbg_out.values()))
        return nc
    hid = A.alloc([128, 2, 256], BF16)
    B_hid = P.buf("hid")
    cu = A.alloc([128, 256], F32)
    ce = A.alloc([128, 256], F32)
    B_cu, B_ce = P.buf("cu"), P.buf("ce")
    cb = A.alloc([128, 8], F32)
    B_cb = P.buf("cb")
    P.add("pool", lambda e: e.memset(hid, 0.0), writes=[B_hid])
    for kv in range(2):
        w1 = w1k if kv == 0 else w1v
        for hc in range(2):
            q4 = kv * 2 + hc
            for l in range(32):
                P.add("pe", lambda e, l=l, hc=hc, w1=w1, kv=kv, q4=q4: e.matmul(
                    bank[7][:, q4 * 32:(q4 + 1) * 32], lhsT=w1[0:64, l, hc * 128:(hc + 1) * 128],
                    rhs=peT[0:64, kv, 2 * l:2 * l + 1].to_broadcast([64, 32]), start=(l == 0), stop=(l == 31)), reads=[B_w1], writes=[PB[7]])
    P.add("dve", lambda e: e.tensor_copy(out=cb[:, 0:4], in_=bank[7][:, 0:128:32]), writes=[PB[7], B_cb])
    P.add("dve", lambda e: e.tensor_scalar(out=cb[:, 4:8], in0=cb[:, 0:4], scalar1=0.5, scalar2=None, op0=ALU.mult), writes=[B_cb])
    for kv in range(2):
        w1 = w1k if kv == 0 else w1v
        w2 = w2k if kv == 0 else w2v
        raw, B_raw = (rawk3, B_rawk) if kv == 0 else (rawv3, B_rawv)
        for g in range(2):
            for hc in range(2):
                pb = 4 + hc
                for l in range(32):
                    P.add("pe", lambda e, l=l, g=g, hc=hc, w1=w1, raw=raw, pb=pb: e.matmul(
                        bank[pb][:, 0:255], lhsT=w1[g * 64:(g + 1) * 64, l, hc * 128:(hc + 1) * 128],
                        rhs=raw[g * 64:(g + 1) * 64, l % 16, (l // 16):(l // 16) + 255], start=(l == 0), stop=(l == 31)),
                        reads=[B_w1, B_raw], writes=[PB[pb]])
                q4 = kv * 2 + hc
                P.add("act", lambda e, pb=pb, q4=q4: e.activation(out=ce[:, 0:255], in_=bank[pb][:, 0:255], func=AF.Tanh, scale=0.5, bias=cb[:, 4 + q4:5 + q4]),
                      reads=[B_cb], xreads=[PB[pb]], writes=[B_ce])
                P.add("dve", lambda e, pb=pb, q4=q4: e.tensor_scalar(out=cu[:, 0:255], in0=bank[pb][:, 0:255], scalar1=cb[:, q4:q4 + 1], scalar2=None, op0=ALU.add),
                      reads=[B_cb], xreads=[PB[pb]], writes=[B_cu])
                P.add("dve", lambda e, hc=hc: e.scalar_tensor_tensor(out=hid[:, hc, 0:255], in0=ce[:, 0:255], scalar=1.0, in1=cu[:, 0:255], op0=ALU.add, op1=ALU.mult),
                      reads=[B_cu, B_ce], writes=[B_hid])
            if kv == 0:
                for hc in range(2):
                    P.add("pe", lambda e, hc=hc, w2=w2: e.matmul(bank[6][0:64, 0:256], lhsT=w2[:, hc, :], rhs=hid[:, hc, :],
                                                                start=(hc == 0), stop=(hc == 1)), reads=[B_w1, B_hid], writes=[PB[6]])
                P.add("dve", lambda e, g=g: e.tensor_scalar(out=KC[g][0:64, :], in0=bank[6][0:64, 0:256], scalar1=0.5, scalar2=None, op0=ALU.mult),
                      writes=[PB[6]], pwrites=[B_KC[g]])
            else:
                for cc in range(2):
                    for hc in range(2):
                        P.add("pe", lambda e, hc=hc, cc=cc, w2=w2: e.matmul(bank[6][:, cc * 64:(cc + 1) * 64], lhsT=hid[:, hc, cc * 128:(cc + 1) * 128],
                                                                            rhs=w2[:, hc, :], start=(hc == 0), stop=(hc == 1)),
                              reads=[B_w1, B_hid], writes=[PB[6]])
                P.add("dve", lambda e, g=g: e.tensor_scalar(out=VC[:, :, g, 0:64], in0=bank[6][:, 0:128].rearrange("p (a b) -> p a b", a=2),
                                                            scalar1=0.5, scalar2=None, op0=ALU.mult), writes=[PB[6]], pwrites=[B_VC])
    if debug:
        P.barrier()
    dump("KTs0", KT[0][0][0:68, :], B_KT[0][0], [68, S])
    dump("KTb1", KT[2][1][0:68, :], B_KT[2][1], [68, S])
    dump("Vt", Vt, B_Vt, [128, 32, 6, 65])
    dump("KC0", KC[0][0:68, :], B_KC[0], [68, 256])
    dump("VC", VC, B_VC, [128, 2, 2, 129])
    if stop_after == "phase1":
        P.emit(nc, final_waits=list(dbg_out.values()))
        return nc

    P.barrier()
    A.release(small_mark)
    OZT = XR.rearrange("p (a b c d) -> p a b c d", a=NSLOT, b=2, c=4)
    B_OZT = P.buf("OZT")
    masks = A.alloc([128, 10, 128], BF16)
    B_masks = P.buf("masks")
    dma("pool", masks, masks_d[:, :, :], B_masks)
    wq = A.alloc([128, 8, 2072], BF16)
    B_wq = [P.buf("wq%d" % i) for i in range(5)]
    for wi, (dst, src, n) in enumerate(((0, 0, 512), (512, 1816, 512), (1024, 1304, 512), (1536, 2584, 512), (2048, 1280, 24))):
        dma("pool", wq[:, :, dst:dst + n], w_in[:, src:src + n].rearrange("(kc p) n -> p kc n", p=128), B_wq[wi])
    xq1 = A.alloc([128, 1024], F32)
    xq = [xq1, xq1]
    B_xq1 = P.buf("xq")
    B_xq = [B_xq1, B_xq1]
    hq1 = A.alloc([128, 1024], BF16)
    hq = [hq1, hq1]
    B_hq1 = P.buf("hq")
    B_hq = [B_hq1, B_hq1]
    hTq = [A.alloc([128, 8, 128], BF16) for _ in range(2)]
    B_hTq = [(P.buf("hTq%da" % i), P.buf("hTq%db" % i)) for i in range(2)]
    QT = [A.alloc([128, 4, 512], BF16) for _ in range(2)]
    B_QT = [P.buf("QT%d" % i) for i in range(2)]
    B_QTa = [P.buf("QTa%d" % i) for i in range(2)]
    PT = [A.alloc([128, 1024], BF16) for _ in range(2)]
    B_PT = [P.buf("PT%d" % i) for i in range(2)]
    cmk = [A.alloc([128, 2, 128], BF16) for _ in range(2)]
    B_cmk = [P.buf("cmk%d" % i) for i in range(2)]
    cmfb = [A.alloc([128, 2, 64], F32) for _ in range(2)]
    B_cmfb = [P.buf("cmfb%d" % i) for i in range(2)]
    QS = [[A.alloc([128, 512], BF16) for _ in range(2)] for _ in range(2)]
    B_QSq = [[P.buf("QSq%d%d" % (i, g)) for g in range(2)] for i in range(2)]
    B_QSa = [[P.buf("QSa%d%d" % (i, g)) for g in range(2)] for i in range(2)]
    B_QSs = [[P.buf("QSs%d%d" % (i, g)) for g in range(2)] for i in range(2)]
    zs = [[A.alloc([128, 512], F32) for _ in range(2)] for _ in range(2)]
    B_zs = [[P.buf("zs%d%d" % (i, k)) for k in range(2)] for i in range(2)]
    ze = A.alloc([128, 512], F32)
    B_ze = P.buf("ze")
    sg = [A.alloc([128, 24], F32) for _ in range(2)]
    B_sg = [P.buf("sg%d" % i) for i in range(2)]
    oa = A.alloc([128, 2, 4, 64], F32)
    ob = A.alloc([128, 2, 4, 64], F32)
    B_oa = [P.buf("oa0"), P.buf("oa1")]
    B_ob = [P.buf("ob0"), P.buf("ob1")]
    otmp = A.alloc([128, 4, 64], F32)
    B_otmp = P.buf("otmp")
    ozb = [A.alloc([128, 512], BF16) for _ in range(2)]
    B_ozb = [P.buf("ozb0"), P.buf("ozb1")]
    sm = A.alloc([128, 64], F32)
    B_sm = P.buf("sm")
    imp = A.alloc([128, 64], F32)
    imp4 = A.alloc([128, 256], F32)
    B_imp4 = P.buf("imp4")
    sc1 = A.alloc([128, 64], F32)
    sc2 = A.alloc([128, 64], F32)
    selb = [A.alloc([128, 64], BF16) for _ in range(2)]
    B_imp, B_sc1, B_sc2 = P.buf("imp"), P.buf("sc1"), P.buf("sc2")
    B_selb = [P.buf("selb0"), P.buf("selb1")]
    es = A.alloc([128, 8], F32)
    B_es = P.buf("es")
    woa = A.alloc([128, 4, 1024], BF16)
    B_woa = P.buf("woa")
    wo_top = A.mark()
    dma("pool", woa, wona_d.rearrange("(c p) n -> p c n", p=128), B_woa)
    P.add("act", lambda e: e.activation(out=es, in_=sinks_t, func=AF.Exp), reads=[B_sinks], writes=[B_es])

    st_pairs = [(3, 4), (5, 6)]
    st_ctr = [0]
    o_ctr = [0]
    pending = []

    delayed = []

    def flush(all_=False):
        fs = list(pending)
        del pending[:]
        for f_ in fs:
            f_()
        keep = []
        for item in list(delayed):
            item[0] -= 1
            if item[0] <= 0 or all_:
                item[1]()
            else:
                keep.append(item)
        delayed[:] = keep

    def O3v(obk):
        return bank[obk][:, 0:260].rearrange("p (a b) -> p a b", a=4)

    def attention(sp, tg, Kt, B_K, kblocks, v_of, B_V, selg, addmask, B_am, extra_rhs=None):
        obk = (7, 2)[o_ctr[0] % 2]
        o_ctr[0] += 1
        n = len(kblocks)
        first_pv = [True]
        for i0 in range(0, n, 2):
            grp = kblocks[i0:i0 + 2]
            pr = st_pairs[st_ctr[0] % 2]
            pt = st_ctr[0] % 2
            st_ctr[0] += 1
            for ii, kb in enumerate(grp):
                bk = pr[ii]
                am = addmask(kb) if addmask is not None else None
                nmm = 1 + (1 if am is not None else 0)
                if selg is None:
                    P.add("pe", lambda e, bk=bk, kb=kb, last=(nmm == 1): e.matmul(
                        bank[bk], lhsT=Kt[0:68, kb * 128:(kb + 1) * 128], rhs=QT[sp][0:68, tg, :], start=True, stop=last),
                        reads=[B_K, B_QT[sp], B_QTa[sp]], writes=[PB[bk]])
                else:
                    P.add("pe", lambda e, bk=bk, kb=kb, last=(nmm == 1): e.matmul(
                        bank[bk], lhsT=Kt[:, kb * 128:(kb + 1) * 128], rhs=QS[sp][selg], start=True, stop=last),
                        reads=[B_K, B_QSq[sp][selg], B_QSa[sp][selg], B_QSs[sp][selg]], writes=[PB[bk]])
                if am is not None:
                    P.add("pe", lambda e, bk=bk, am=am: e.matmul(
                        bank[bk].rearrange("p (a b) -> p a b", a=4), lhsT=ident_b,
                        rhs=am.unsqueeze(1).to_broadcast([128, 4, 128]), start=False, stop=True),
                        reads=[B_const, B_am], writes=[PB[bk]])
            w = len(grp) * 512
            b0 = pr[0]
            P.add("act", lambda e, b0=b0, w=w, pt=pt: e.activation(out=PT[pt][:, 0:w], in_=ps[:, b0 * 512:b0 * 512 + w], func=AF.Exp),
                  xreads=[PB[pr[ii]] for ii in range(len(grp))], writes=[B_PT[pt]])
            flush()

            def pv(grp=grp, i0=i0, pt=pt):
                for ii, kb in enumerate(grp):
                    lastk = (i0 + ii == n - 1)
                    for r in range(4):
                        P.add("pe", lambda e, ii=ii, kb=kb, r=r, fp=first_pv[0], lastk=lastk: e.matmul(
                            bank[obk][:, r * 65:(r + 1) * 65], lhsT=PT[pt][:, ii * 512 + r * 128: ii * 512 + (r + 1) * 128], rhs=v_of(kb),
                            start=fp, stop=lastk, skip_group_check=True), reads=[B_PT[pt], B_V], writes=[PB[obk]])
                        if extra_rhs is not None:
                            P.add("pe", lambda e, ii=ii, kb=kb, r=r, fp=first_pv[0], lastk=lastk: e.matmul(
                                bank[1][:, r * 64:(r + 1) * 64], lhsT=PT[pt][:, ii * 512 + r * 128: ii * 512 + (r + 1) * 128], rhs=extra_rhs(kb),
                                start=fp, stop=lastk, skip_group_check=True), reads=[B_PT[pt], B_V], writes=[PB[1]])
                        first_pv[0] = False
            pending.append(pv)
        return obk

    def finish(obk, sp, gate_col0, dst, B_dst, first_branch, extra_den=None):
        O3 = O3v(obk)
        if extra_den is not None:
            P.add("dve", lambda e: e.tensor_tensor(out=sm[:, 0:4], in0=O3[:, :, 64], in1=extra_den, op=ALU.add), reads=[B_es], xreads=[PB[obk]], writes=[B_sm])
        else:
            P.add("dve", lambda e: e.tensor_scalar(out=sm[:, 0:4], in0=O3[:, :, 64], scalar1=1e-30, scalar2=None, op0=ALU.max),
                  xreads=[PB[obk]], writes=[B_sm])
        P.add("dve", lambda e: e.reciprocal(out=sm[:, 4:8], in_=sm[:, 0:4]), writes=[B_sm])
        if gate_col0 is not None:
            P.add("dve", lambda e: e.tensor_tensor(out=sm[:, 8:12], in0=sm[:, 4:8], in1=sg[sp][:, gate_col0:gate_col0 + 4], op=ALU.mult),
                  reads=[B_sg[sp]], writes=[B_sm])
            fac = sm[:, 8:12]
        else:
            fac = sm[:, 4:8]
        facb = fac.unsqueeze(2).to_broadcast([128, 4, 64])
        if first_branch:
            P.add("dve", lambda e: e.tensor_tensor(out=dst, in0=O3[:, :, 0:64], in1=facb, op=ALU.mult),
                  reads=[B_sm], xreads=[PB[obk]], writes=[B_dst])
        else:
            P.add("dve", lambda e: e.tensor_tensor(out=otmp, in0=O3[:, :, 0:64], in1=facb, op=ALU.mult),
                  reads=[B_sm], xreads=[PB[obk]], writes=[B_otmp])
            P.add("dve", lambda e: e.tensor_tensor(out=dst, in0=dst, in1=otmp, op=ALU.add), reads=[B_otmp], writes=[B_dst])

    def importance(obk, sp, g):
        O3 = O3v(obk)
        P.add("dve", lambda e: e.tensor_scalar(out=sm[:, 16:20], in0=O3[:, :, 64], scalar1=1e-30, scalar2=None, op0=ALU.max),
              xreads=[PB[obk]], writes=[B_sm])
        P.add("dve", lambda e: e.reciprocal(out=sm[:, 20:24], in_=sm[:, 16:20]), writes=[B_sm])
        P.add("dve", lambda e: e.tensor_tensor(out=imp4.rearrange("p (a b) -> p a b", a=4), in0=bank[1][:, 0:256].rearrange("p (a b) -> p a b", a=4),
                                               in1=sm[:, 20:24].unsqueeze(2).to_broadcast([128, 4, 64]), op=ALU.mult),
              reads=[B_sm], xreads=[PB[1]], writes=[B_imp4])
        P.add("dve", lambda e: e.tensor_reduce(out=imp, in_=imp4.rearrange("p (r j) -> p j r", r=4), axis=mybir.AxisListType.X, op=ALU.add),
              reads=[B_imp4], writes=[B_imp])
        P.add("dve", lambda e: e.tensor_tensor(out=sc1, in0=imp, in1=cmfb[sp][:, 1, :], op=ALU.add), reads=[B_imp, B_cmfb[sp]], writes=[B_sc1])
        P.add("dve", lambda e: e.max(out=sm[:, 24:32], in_=sc1), reads=[B_sc1], writes=[B_sm])
        P.add("dve", lambda e: e.match_replace(out=sc2, in_to_replace=sm[:, 24:32], in_values=sc1, imm_value=-3e38),
              reads=[B_sc1, B_sm], writes=[B_sc2])
        P.add("dve", lambda e: e.max(out=sm[:, 32:40], in_=sc2), reads=[B_sc2], writes=[B_sm])
        P.add("dve", lambda e, g=g: e.tensor_scalar(out=selb[g][:, 0:61], in0=sc1[:, 1:62], scalar1=sm[:, 39:40], scalar2=NEGM, op0=ALU.is_lt, op1=ALU.mult),
              reads=[B_sc1, B_sm], writes=[B_selb[g]])

    def importance_b(g, sp):
        P.add("pe", lambda e, g=g: e.transpose(out=bankb[0][0:61, g * 128:(g + 1) * 128], in_=selb[g][:, 0:61], identity=ident_b),
              reads=[B_selb[g], B_const], writes=[PB[0]])
        P.add("dve", lambda e, g=g, sp=sp: e.tensor_copy(out=QS[sp][g][0:61, :].rearrange("p (a b) -> p a b", a=4),
                                                        in_=bankb[0][0:61, g * 128:(g + 1) * 128].unsqueeze(1).to_broadcast([61, 4, 128])),
              xreads=[PB[0]], writes=[B_QSs[sp][g]])

    def prologue_a(j):
        sp = j % 2
        dma("sp", xq[sp], xo[j * 128:(j + 1) * 128, :], B_xq[sp])
        rms_a(xq[sp], B_xq[sp], hq[sp], B_hq[sp])

    def prologue(j):
        sp = j % 2
        dma("pool", QT[sp][64:68, :, :], qaug_d[j].rearrange("r (a b) -> r a b", a=4), B_QTa[sp])
        for g in range(2):
            dma("pool", QS[sp][g][61:64, :], qaugs_d[j][:, g * 512:(g + 1) * 512], B_QSa[sp][g])
        dma("pool", cmk[sp], cmask_d[j].rearrange("c k q -> k c q"), B_cmk[sp])
        dma("sp", cmfb[sp], cmfb_d[j].rearrange("c q n -> q c n"), B_cmfb[sp])
        rms_b(hq[sp], B_hq[sp], lambda c, sp=sp: hTq[sp][:, c, :], B_hTq[sp], (0, 1), act_from=6)
        for tg in range(4):
            t, g = tg // 2, tg % 2
            pb = (1, 0)[tg % 2]
            for kc in range(8):
                for hp in range(2):
                    c0 = t * 512 + (g * 4 + 2 * hp) * 64
                    P.add("pe", lambda e, pb=pb, hp=hp, c0=c0, kc=kc, sp=sp: e.matmul(
                        bank[pb][:, hp * 128:(hp + 1) * 128], lhsT=wq[:, kc, c0:c0 + 128], rhs=hTq[sp][:, kc, :],
                        start=(kc == 0 and hp == 0), stop=(kc == 7), skip_group_check=True), reads=[B_wq[t], *B_hTq[sp]], writes=[PB[pb]])
            for odd in range(2):
                src = bank[pb][odd * 64:(odd + 1) * 64, 0:256].rearrange("p (a b) -> p a b", a=2)
                dstv = QT[sp][0:64, tg, :].rearrange("p (a two b) -> p a two b", a=2, two=2)[:, :, odd, :]
                P.add("dve", lambda e, src=src, dstv=dstv: e.tensor_scalar(out=dstv, in0=src, scalar1=0.125, scalar2=None, op0=ALU.mult),
                      xreads=[PB[pb]], pwrites=[B_QT[sp]])
                if t == 0:
                    dsts = QS[sp][g][64:128, :].rearrange("p (a two b) -> p a two b", a=2, two=2)[:, :, odd, :]
                    P.add("dve", lambda e, src=src, dsts=dsts: e.tensor_scalar(out=dsts, in0=src, scalar1=0.125, scalar2=None, op0=ALU.mult),
                          xreads=[PB[pb]], pwrites=[B_QSq[sp][g]])
        for zi in range(2):
            pb = (1, 0)[zi]
            for kc in range(8):
                P.add("pe", lambda e, pb=pb, zi=zi, kc=kc, sp=sp: e.matmul(
                    bank[pb], lhsT=hTq[sp][:, kc, :], rhs=wq[:, kc, 1024 + zi * 512:1536 + zi * 512], start=(kc == 0), stop=(kc == 7)),
                    reads=[B_wq[2 + zi], *B_hTq[sp]], writes=[PB[pb]])
            P.add("act", lambda e, pb=pb: e.activation(out=ze, in_=bank[pb], func=AF.Tanh, scale=0.5), xreads=[PB[pb]], writes=[B_ze])
            P.add("dve", lambda e, pb=pb, zi=zi, sp=sp: e.scalar_tensor_tensor(out=zs[sp][zi], in0=ze, scalar=1.0, in1=bank[pb],
                                                                              op0=ALU.add, op1=ALU.mult),
                  reads=[B_ze], xreads=[PB[pb]], writes=[B_zs[sp][zi]])
        for kc in range(8):
            P.add("pe", lambda e, kc=kc, sp=sp: e.matmul(bank[1][:, 0:24], lhsT=hTq[sp][:, kc, :], rhs=wq[:, kc, 2048:2072], start=(kc == 0), stop=(kc == 7)),
                  reads=[B_wq[4], *B_hTq[sp]], writes=[PB[1]])
        P.add("act", lambda e, sp=sp: e.activation(out=sg[sp], in_=bank[1][:, 0:24], func=AF.Exp, scale=-1.0), xreads=[PB[1]], writes=[B_sg[sp]])
        P.add("dve", lambda e, sp=sp: e.tensor_scalar(out=sg[sp], in0=sg[sp], scalar1=1.0, scalar2=None, op0=ALU.add), writes=[B_sg[sp]])
        P.add("dve", lambda e, sp=sp: e.reciprocal(out=sg[sp], in_=sg[sp]), writes=[B_sg[sp]])

    nslots = NSLOT if stop_after is None or not stop_after.startswith("slot") else int(stop_after[4:])
    prologue_a(0)
    prologue(0)
    wmi = {0: 2, 1: 3, 4: 4, 5: 5}
    for j in range(nslots):
        sp = j % 2
        if j + 1 < nslots:
            prologue_a(j + 1)
        chunks = [0] if j <= 7 else [0, 1]

        def cm_mask(cc, j=j, sp=sp):
            if cc == 0 and j >= 9:
                return None
            return cmk[sp][:, cc, :]
        for g in range(2):
            obk = attention(sp, 0 * 2 + g, KC[g], B_KC[g], chunks, lambda cc, g=g: VC[:, cc, g, 0:65], B_VC, None, cm_mask, B_cmk[sp],
                            extra_rhs=lambda cc, g=g: VC[:, cc, g, 65:129])
            pending.append(lambda obk=obk, sp=sp, g=g: (importance(obk, sp, g), delayed.append([2, lambda: importance_b(g, sp)])))
            pending.append(lambda obk=obk, sp=sp, g=g: finish(obk, sp, 0 * 8 + g * 4, oa[:, g], B_oa[g], True))
        for g in range(2):
            kbs = [kb for kb in range(2 * j - 4, 2 * j + 2) if kb >= 0]
            obk = attention(sp, 0 * 2 + g, KT[1][g], B_KT[1][g], kbs, lambda kb, g=g: Vt[:, kb, 1 * 2 + g, :], B_Vt, None,
                            lambda kb, j=j: masks[:, wmi[kb - (2 * j - 4)], :] if (kb - (2 * j - 4)) in wmi else None, B_masks)
            pending.append(lambda obk=obk, sp=sp, g=g: finish(obk, sp, 2 * 8 + g * 4, oa[:, g], B_oa[g], False))
        for g in range(2):
            kbs = [kb for kb in range(2 * j - 1, 2 * j + 2) if kb >= 0]
            obk = attention(sp, 1 * 2 + g, KT[2][g], B_KT[2][g], kbs, lambda kb, g=g: Vt[:, kb, 2 * 2 + g, :], B_Vt, None,
                            lambda kb, j=j: masks[:, 6 + kb - (2 * j - 1), :], B_masks)
            pending.append(lambda obk=obk, sp=sp, g=g: finish(obk, sp, None, ob[:, g], B_ob[g], True, extra_den=es[:, g * 4:(g + 1) * 4]))
        if j + 1 < nslots:
            flush()
            prologue(j + 1)
        flush(all_=True)
        for g in range(2):
            obk = attention(sp, 0 * 2 + g, KT[0][g], B_KT[0][g], list(range(2 * j + 2)), lambda kb, g=g: Vt[:, kb, 0 * 2 + g, :], B_Vt, g,
                            lambda kb, j=j: masks[:, kb - 2 * j, :] if kb >= 2 * j else None, B_masks)
            pending.append(lambda obk=obk, sp=sp, g=g: finish(obk, sp, 1 * 8 + g * 4, oa[:, g], B_oa[g], False))
        flush()
        for mi, (osrc, B_os) in enumerate(((oa, B_oa), (ob, B_ob))):
            P.add("dve", lambda e, osrc=osrc, mi=mi, sp=sp: e.tensor_tensor(out=ozb[mi], in0=osrc.rearrange("p a b c -> p (a b c)"), in1=zs[sp][mi], op=ALU.mult),
                  reads=[*B_os, B_zs[sp][mi]], writes=[B_ozb[mi]])

        def epi_b(j=j):
            for mi in range(2):
                for c in range(4):
                    P.add("pe", lambda e, c=c, mi=mi: e.transpose(out=bankb[0][:, c * 128:(c + 1) * 128], in_=ozb[mi][:, c * 128:(c + 1) * 128], identity=ident_b),
                          reads=[B_ozb[mi], B_const], writes=[PB[0]])
                P.add("dve", lambda e, j=j, mi=mi: e.tensor_copy(out=OZT[:, j, mi, :, :], in_=bankb[0][:, 0:512].rearrange("p (a b) -> p a b", a=4)),
                      xreads=[PB[0]], pwrites=[B_OZT])
        pending.append(epi_b)
    flush()
    dump("OZT", OZT, B_OZT, [128, NSLOT, 2, 4, 128])
    if stop_after is not None and (stop_after == "sweepA" or stop_after.startswith("slot")):
        P.emit(nc, final_waits=list(dbg_out.values()))
        return nc

    P.barrier()
    A.release(kv_mark)
    ggbc = A.alloc([128, 1024], F32)
    B_gg = P.buf("ggbc")
    dg = A.alloc([128, 128], F32)
    B_dg = P.buf("dg")
    for c in range(8):
        P.add("dve", lambda e, c=c: e.tensor_scalar(out=dg, in0=ident_f, scalar1=cols[:, 80 + c:81 + c], scalar2=None, op0=ALU.mult),
              reads=[B_identf, B_cols], writes=[B_dg])
        pb = c // 4
        P.add("pe", lambda e, c=c, pb=pb: e.matmul(bank[pb][:, (c % 4) * 128:(c % 4 + 1) * 128], lhsT=ones_f, rhs=dg, start=True, stop=True,
                                                   skip_group_check=True), reads=[B_ones, B_dg], writes=[PB[pb]])
    P.add("dve", lambda e: e.tensor_copy(out=ggbc, in_=ps[:, 0:1024]), writes=[PB[0], PB[1], B_gg])
    wm = A.alloc([128, 8, 2048], BF16)
    wob = A.alloc([128, 4, 1024], BF16)
    wout = A.alloc([128, 8, 1024], BF16)
    B_wm = [P.buf("wm%d" % i) for i in range(4)]
    B_wob, B_wout = P.buf("wob"), P.buf("wout")
    xb_ = [A.alloc([128, 1024], F32) for _ in range(5)]
    B_xb = [P.buf("xb%d" % i) for i in range(5)]
    hq2 = [A.alloc([128, 1024], BF16) for _ in range(2)]
    B_hq2 = [P.buf("hq2%d" % i) for i in range(2)]
    hTq2 = [A.alloc([128, 8, 128], BF16) for _ in range(2)]
    B_hTq2 = [(P.buf("hTq2a%d" % i), P.buf("hTq2b%d" % i)) for i in range(2)]
    sig = [[A.alloc([128, 1024], F32) for _ in range(2)] for _ in range(2)]
    B_sig = [[P.buf("sig%d%d" % (i, k)) for k in range(2)] for i in range(2)]
    uu = A.alloc([128, 1024], F32)
    B_uu = P.buf("uu")
    ub = [A.alloc([128, 1024], BF16) for _ in range(2)]
    B_ub = [P.buf("ub%d" % i) for i in range(2)]
    uT = A.alloc([128, 8, 128], BF16)
    B_uT = P.buf("uT")
    fin = A.alloc([128, 1024], F32)
    B_fin = P.buf("fin")

    def stage0a(j):
        xi, s2 = j % 5, j % 2
        if j >= 2:
            dma("pool", xb_[xi], xo[j * 128:(j + 1) * 128, :], B_xb[xi], reads=[])
        rms_a(xb_[xi], B_xb[xi], hq2[s2], B_hq2[s2])

    def stage0b(j):
        s2 = j % 2
        rms_b(hq2[s2], B_hq2[s2], lambda c, s2=s2: hTq2[s2][:, c, :], B_hTq2[s2], (0, 5))

    def stage1(j, mi):
        xi, s2 = j % 5, j % 2
        if True:
            for hf in range(2):
                pb = 1 + hf
                for kc in range(8):
                    P.add("pe", lambda e, pb=pb, mi=mi, hf=hf, kc=kc, s2=s2: e.matmul(
                        bank[pb], lhsT=hTq2[s2][:, kc, :], rhs=wm[:, kc, mi * 1024 + hf * 512: mi * 1024 + (hf + 1) * 512],
                        start=(kc == 0), stop=(kc == 7)), reads=[B_wm[mi * 2 + hf], *B_hTq2[s2]], writes=[PB[pb]])
            P.add("act", lambda e, mi=mi, s2=s2: e.activation(out=sig[s2][mi], in_=ps[:, 512:1536], func=AF.Tanh, scale=0.5),
                  xreads=[PB[1], PB[2]], writes=[B_sig[s2][mi]])

    def stage2(j):
        s2 = j % 2
        for mi in range(2):
            wo, B_wo = (woa, B_woa) if mi == 0 else (wob, B_wob)
            for hf in range(2):
                pb = 3 + hf
                for c in range(4):
                    P.add("pe", lambda e, pb=pb, mi=mi, hf=hf, c=c, wo=wo, j=j: e.matmul(
                        bank[pb], lhsT=OZT[:, j, mi, c, :], rhs=wo[:, c, hf * 512:(hf + 1) * 512], start=(c == 0), stop=(c == 3)),
                        reads=[B_wo, B_OZT], writes=[PB[pb]])
            if mi == 0:
                P.add("dve", lambda e, s2=s2: e.scalar_tensor_tensor(out=uu, in0=sig[s2][0], scalar=1.0, in1=ps[:, 1536:2560], op0=ALU.add, op1=ALU.mult),
                      reads=[B_sig[s2][0]], xreads=[PB[3], PB[4]], writes=[B_uu])
            else:
                P.add("dve", lambda e, s2=s2: e.scalar_tensor_tensor(out=sig[s2][1], in0=sig[s2][1], scalar=1.0, in1=ps[:, 1536:2560], op0=ALU.add, op1=ALU.mult),
                      xreads=[PB[3], PB[4]], writes=[B_sig[s2][1]])
                P.add("dve", lambda e, s2=s2: e.tensor_tensor(out=ub[s2], in0=uu, in1=sig[s2][1], op=ALU.add),
                      reads=[B_uu, B_sig[s2][1]], writes=[B_ub[s2]])

    def stage3a(j):
        s2 = j % 2
        for c in range(8):
            P.add("pe", lambda e, c=c, s2=s2: e.transpose(out=bankb[5][:, c * 128:(c + 1) * 128], in_=ub[s2][:, c * 128:(c + 1) * 128], identity=ident_b),
                  reads=[B_ub[s2], B_const], writes=[PB[5]])
        P.add("act", lambda e: e.copy(out=uT.rearrange("p a b -> p (a b)"), in_=bankb[5]), xreads=[PB[5]], writes=[B_uT])

    def stage3b(j):
        xi = j % 5
        xq2 = xb_[xi]
        for hf in range(2):
            pb = 6 + hf
            for c in range(8):
                P.add("pe", lambda e, pb=pb, hf=hf, c=c: e.matmul(bank[pb], lhsT=uT[:, c, :], rhs=wout[:, c, hf * 512:(hf + 1) * 512],
                                                                  start=(c == 0), stop=(c == 7)), reads=[B_wout, B_uT], writes=[PB[pb]])
        sc = A_scr
        P.add("act", lambda e: e.activation(out=junk, in_=ps[:, 3072:4096], func=AF.Square, accum_out=sc[:, 4:5]),
              xreads=[PB[6], PB[7]], writes=[B_junk, B_scr2])
        P.add("act", lambda e: e.activation(out=sc[:, 5:6], in_=sc[:, 4:5], func=AF.Ln, scale=1.0 / D, bias=epsT[:, 1:2]),
              reads=[B_eps], writes=[B_scr2])
        P.add("act", lambda e: e.activation(out=sc[:, 6:7], in_=sc[:, 5:6], func=AF.Exp, scale=-0.5), writes=[B_scr2])
        P.add("dve", lambda e: e.scalar_tensor_tensor(out=fin, in0=ps[:, 3072:4096], scalar=sc[:, 6:7], in1=ggbc, op0=ALU.mult, op1=ALU.mult),
              reads=[B_scr2, B_gg], xreads=[PB[6], PB[7]], writes=[B_fin])
        P.add("dve", lambda e, xq2=xq2: e.tensor_tensor(out=xq2, in0=xq2, in1=fin, op=ALU.add), reads=[B_fin], writes=[B_xb[xi]])
        P.add("sp", lambda e, xq2=xq2, j=j: e.dma_start(out=out_d[j * 128:(j + 1) * 128, :], in_=xq2), reads=[B_xb[xi]], pwrites=[out_bufs[xi]],
              dma=out_bufs[xi])

    B_scr2 = P.buf("scr2")
    for j0 in range(2):
        dma("pool", xb_[j0], xo[j0 * 128:(j0 + 1) * 128, :], B_xb[j0], reads=[])
    for hf in range(4):
        dma("pool", wm[:, :, hf * 512:(hf + 1) * 512], w_in[:, 3096 + hf * 512:3096 + (hf + 1) * 512].rearrange("(kc p) n -> p kc n", p=128),
            B_wm[hf])
    dma("pool", wob, wosw_d.rearrange("(c p) n -> p c n", p=128), B_wob)
    dma("pool", wout, wout_d.rearrange("(c p) n -> p c n", p=128), B_wout)
    stage0a(0)
    stage0b(0)
    stage0a(1)
    for it in range(NSLOT + 2):
        if it < NSLOT:
            stage1(it, 0)
        if it - 2 >= 0:
            stage3a(it - 2)
        if it + 1 < NSLOT:
            stage0b(it + 1)
        if it < NSLOT:
            stage1(it, 1)
        if 0 <= it - 1 < NSLOT:
            stage2(it - 1)
        if it - 2 >= 0:
            stage3b(it - 2)
        if it + 2 < NSLOT:
            stage0a(it + 2)
    P.emit(nc, final_waits=out_bufs + list(dbg_out.values()))
    return nc


def _inputs_for_core(core, inp, shared):
    b, p = core // 2, core % 2
    f = lambda a: np.ascontiguousarray(a, dtype=np.float32)
    x = inp["x"][b]
    m = dict(shared)
    m["xf"] = f(x)
    m["xo"] = f(x.reshape(32, 128, D)[p::2].reshape(NSLOT * 128, D))
    m["cT"] = f(inp["c"][b].reshape(8, 128).T)
    m.update(_core_tables(p))
    return m


def kernel(debug=None, stop_after=None, cores=None, **inp):
    inp = {k: np.asarray(v) for k, v in inp.items()}
    f = lambda a: np.ascontiguousarray(a, dtype=np.float32)
    shared = _shared_tables()
    shared["w_ada"] = f(inp["w_ada"][0])
    shared["badaT"] = f(inp["b_ada"][0].reshape(24, 128).T)
    shared["gpreT"] = f(inp["g_pre"][0].reshape(8, 128).T)
    shared["gpostT"] = f(inp["g_post"][0].reshape(8, 128).T)
    shared["w_in"] = f(inp["w_in"][0])
    shared["pekT"] = f(np.stack([inp["pe_cmp_k"][0].T, np.zeros((64, 32))], -1).reshape(64, 64))
    shared["pevT"] = f(np.stack([inp["pe_cmp_v"][0].T, np.zeros((64, 32))], -1).reshape(64, 64))
    shared["w1k"] = f(inp["w_cmp_k1"][0])
    shared["w2k"] = f(inp["w_cmp_k2"][0])
    shared["w1v"] = f(inp["w_cmp_v1"][0])
    shared["w2v"] = f(inp["w_cmp_v2"][0])
    shared["w_o_nsa"] = f(inp["w_o_nsa"][0])
    shared["w_o_swa"] = f(inp["w_o_swa"][0])
    shared["w_out"] = f(inp["w_out"][0])
    shared["sinksb"] = f(np.tile(inp["sinks"][0][None, :], (128, 1)))
    nc = build_nc(debug=debug, stop_after=stop_after)
    cores = list(range(8)) if cores is None else cores
    in_maps = [_inputs_for_core(c, inp, shared) for c in cores]
    if stop_after == "phase1" or (stop_after or "").startswith("slot") or stop_after == "sweepA":
        for m in in_maps:
            pass
    res = run_bass_kernel_spmd(nc, in_maps, core_ids=list(range(len(cores))))
    if stop_after:
        return res.results
    out = np.zeros((4, 32, 128, D), np.float32)
    for i, c in enumerate(cores):
        b, p = c // 2, c % 2
        out[b, p::2] = res.results[i]["out"].reshape(NSLOT, 128, D)
    if debug:
        return out.reshape(4, S, D), res.results
    return out.reshape(4, S, D)
```

```python
import numpy as np
from contextlib import ExitStack
import concourse.bass as bass
import concourse.mybir as mybir
from concourse.bass_utils import run_bass_kernel_spmd

F32 = mybir.dt.float32
BF16 = mybir.dt.bfloat16
AF = mybir.ActivationFunctionType
ALU = mybir.AluOpType

NEGM = -30000.0
IMP_INLINE = True
S = 4096
D = 1024
NSLOT = 16


class Buf:
    __slots__ = ("name", "w", "r", "xr", "dcount", "sem", "last")

    def __init__(self, name):
        self.name = name
        self.w = None
        self.r = {}
        self.xr = {}
        self.dcount = 0
        self.sem = None
        self.last = None


class Op:
    __slots__ = ("eng", "fn", "deps", "signal", "dma", "sigval")

    def __init__(self, eng, fn, dma):
        self.eng = eng
        self.fn = fn
        self.dma = dma
        self.deps = []
        self.signal = False
        self.sigval = 0


class Prog:
    ENGS = ("pe", "act", "dve", "pool", "sp")

    def __init__(self):
        self.ops = {e: [] for e in self.ENGS}
        self.dma_bufs = []
        self.pending = {e: [] for e in self.ENGS}

    def buf(self, name):
        return Buf(name)

    def barrier(self):
        lasts = [self.ops[e][-1] for e in self.ENGS if self.ops[e]]
        lasts += [b.last for b in self.dma_bufs if b.last is not None]
        for e in self.ENGS:
            self.pending[e] = list(lasts)

    def add(self, eng, fn, reads=(), writes=(), pwrites=(), dma=None, xreads=()):
        op = Op(eng, fn, dma)
        deps = {}
        for b in reads:
            if b.w is not None:
                deps[id(b.w)] = b.w
        for b in xreads:
            if b.w is not None:
                deps[id(b.w)] = b.w
            for k_, r in b.xr.items():
                if k_ != eng:
                    deps[id(r)] = r
        for b in list(writes) + list(pwrites):
            for r in b.r.values():
                deps[id(r)] = r
            for r in b.xr.values():
                deps[id(r)] = r
        for b in writes:
            if b.w is not None:
                deps[id(b.w)] = b.w
        for d in self.pending[eng]:
            deps[id(d)] = d
        self.pending[eng] = []
        for d in deps.values():
            if d.dma is None and dma is None and d.eng == "pe" and eng == "pe":
                continue
            if d.dma is None and d.eng == eng and eng == "sp":
                continue
            op.deps.append(d)
            d.signal = True
        key = eng if dma is None else ("dma", id(dma))
        for b in reads:
            b.r[key] = op
        for b in xreads:
            b.xr[eng] = op
        for b in list(writes) + list(pwrites):
            b.w = op
            b.r = {}
            b.xr = {}
        if dma is not None:
            if dma not in self.dma_bufs:
                self.dma_bufs.append(dma)
            dma.dcount += 16
            dma.last = op
            op.sigval = dma.dcount
            op.signal = True
        self.ops[eng].append(op)
        return op

    def emit(self, nc, final_waits=()):
        with ExitStack() as st:
            esem = {e: st.enter_context(nc.semaphore("sem_" + e)) for e in self.ENGS}
            for i, b in enumerate(self.dma_bufs):
                b.sem = st.enter_context(nc.semaphore("dsem%d" % i))
            for e in self.ENGS:
                c = 0
                for op in self.ops[e]:
                    if op.dma is None and op.signal:
                        c += 1
                        op.sigval = c
            block = st.enter_context(nc.Block())

            def run(eng_name, engine):
                waited = {}
                for op in self.ops[eng_name]:
                    for d in op.deps:
                        if d.dma is not None:
                            sem, key = d.dma.sem, ("d", id(d.dma))
                        else:
                            sem, key = esem[d.eng], d.eng
                        if waited.get(key, 0) >= d.sigval:
                            continue
                        waited[key] = d.sigval
                        engine.wait_ge(sem, d.sigval)
                    ins = op.fn(engine)
                    if op.dma is not None:
                        ins.then_inc(op.dma.sem, 16)
                    elif op.signal:
                        ins.then_inc(esem[eng_name], 1)
                if eng_name == "sp":
                    for b in final_waits:
                        engine.wait_ge(b.sem, b.dcount)

            @block.tensor
            def _(e):
                run("pe", e)

            @block.scalar
            def _(e):
                run("act", e)

            @block.vector
            def _(e):
                run("dve", e)

            @block.gpsimd
            def _(e):
                run("pool", e)

            @block.sync
            def _(e):
                run("sp", e)


class Arena:
    def __init__(self, nc, st, nbytes):
        self.t = st.enter_context(nc.sbuf_tensor("arena", [128, nbytes // 2], BF16))
        self.top = 0
        self.cap = nbytes

    def alloc(self, shape, dtype):
        esz = 4 if dtype == F32 else 2
        n = 1
        for s_ in shape[1:]:
            n *= s_
        nb = (n * esz + 31) // 32 * 32
        off = self.top
        self.top += nb
        assert self.top <= self.cap, ("SBUF arena overflow", self.top, self.cap)
        v = self.t[:, off // 2: off // 2 + n * esz // 2]
        if dtype == F32:
            v = v.bitcast(F32)
        if len(shape) == 3:
            v = v.rearrange("p (a b) -> p a b", a=shape[1])
        elif len(shape) == 4:
            v = v.rearrange("p (a b c) -> p a b c", a=shape[1], b=shape[2])
        elif len(shape) == 5:
            v = v.rearrange("p (a b c d) -> p a b c d", a=shape[1], b=shape[2], c=shape[3])
        return v

    def mark(self):
        return self.top

    def release(self, m):
        self.top = m


def _slopes():
    return 2.0 ** (-(np.arange(8) + 1.0))


def _shared_tables():
    t = {}
    t["ident"] = np.eye(128, dtype=np.float32)
    kpos = np.arange(S)
    t["kaug"] = np.stack([np.ones(S), np.ones(S), kpos // 128, kpos % 128]).astype(np.float32)
    c = np.arange(256)
    end = 16 * c + 31
    t["caug"] = np.stack([np.ones(256), np.ones(256), end // 128, end % 128]).astype(np.float32)
    M = np.zeros((256, 64), np.float32)
    for j in range(64):
        for m in range(4):
            for n in range(2):
                ci = 4 * j - m - n
                if 0 <= ci < 255:
                    M[ci, j] += 1.0
    t["mimp"] = M
    R = np.zeros((64, S), np.float32)
    R[kpos // 64, kpos] = 1.0
    t["ksel"] = np.concatenate([R[1:62], np.stack([np.ones(S), kpos // 128, kpos % 128])]).astype(np.float32)
    return t


def _core_tables(p):
    sl = _slopes()
    t = {}
    qa = np.zeros((NSLOT, 4, 4, 4, 128), np.float32)
    q = np.arange(128)
    for j in range(NSLOT):
        for tg in range(4):
            g = tg % 2
            for r in range(4):
                s_ = sl[g * 4 + r]
                qa[j, 0, tg, r] = -s_ * 128.0 * (2 * j + p)
                qa[j, 1, tg, r] = -s_ * q
                qa[j, 2, tg, r] = s_ * 128.0
                qa[j, 3, tg, r] = s_
    t["qaug"] = qa.reshape(NSLOT, 4, 2048)
    qs = np.zeros((NSLOT, 3, 2, 4, 128), np.float32)
    for j in range(NSLOT):
        qpos = (2 * j + p) * 128 + q
        for g in range(2):
            for r in range(4):
                s_ = sl[g * 4 + r]
                qs[j, 0, g, r] = -s_ * qpos
                qs[j, 1, g, r] = s_ * 128.0
                qs[j, 2, g, r] = s_
    t["qaugs"] = qs.reshape(NSLOT, 3, 1024)
    k = np.arange(128)[:, None]
    qq = np.arange(128)[None, :]
    masks = np.zeros((10, 128, 128), np.float32)
    for i in range(2):
        dist = (p - i) * 128 + qq - k
        masks[i] = np.where(dist >= 0, 0.0, NEGM)
    for n, i in enumerate((0, 1, 4, 5)):
        dist = (p - i + 4) * 128 + qq - k
        masks[2 + n] = np.where((dist >= 0) & (dist < 512), 0.0, NEGM)
    for i in range(3):
        dist = (p - i + 1) * 128 + qq - k
        masks[6 + i] = np.where((dist >= 0) & (dist < 128), 0.0, NEGM)
    t["masks"] = masks.transpose(1, 0, 2).copy()
    cm = np.zeros((NSLOT, 2, 128, 128), np.float32)
    for j in range(NSLOT):
        qpos = (2 * j + p) * 128 + qq
        for cc in range(2):
            c = cc * 128 + k
            ok = (qpos >= 16 * c + 31) & (c < 255)
            cm[j, cc] = np.where(ok, 0.0, NEGM)
    t["cmask"] = cm
    cf = np.zeros((NSLOT, 2, 128, 64), np.float32)
    jj = np.arange(64)[None, :]
    for j in range(NSLOT):
        qpos = (2 * j + p) * 128 + np.arange(128)[:, None]
        cur = qpos // 64
        causal = jj <= cur
        fb = np.where(causal, 0.0, -1e30)
        fb = np.where(causal & (jj == 0), 1e30, fb)
        fb = np.where(causal & (jj == cur - 1), 2e30, fb)
        fb = np.where(causal & (jj == cur), 4e30, fb)
        cf[j, 0] = causal.astype(np.float32)
        cf[j, 1] = fb
    t["cmfb"] = cf
    return t


def build_nc(debug=None, stop_after=None):
    debug = debug or []
    nc = bass.Bass("TRN2", target_bir_lowering=False)
    P = Prog()
    st = ExitStack()

    def din(name, shape):
        return nc.dram_tensor(name, list(shape), F32, kind="ExternalInput").ap()

    xf = din("xf", [S, D])
    xo = din("xo", [NSLOT * 128, D])
    cT_d = din("cT", [128, 8])
    w_ada = din("w_ada", [D, 3 * D])
    badaT = din("badaT", [128, 24])
    gpreT = din("gpreT", [128, 8])
    gpostT = din("gpostT", [128, 8])
    w_in = din("w_in", [D, 5144])
    pekT = din("pekT", [64, 64])
    pevT = din("pevT", [64, 64])
    w1k_d = din("w1k", [2048, 256])
    w2k_d = din("w2k", [256, 64])
    w1v_d = din("w1v", [2048, 256])
    w2v_d = din("w2v", [256, 64])
    wona_d = din("w_o_nsa", [512, D])
    wosw_d = din("w_o_swa", [512, D])
    wout_d = din("w_out", [D, D])
    sinks_d = din("sinksb", [128, 8])
    ident_d = din("ident", [128, 128])
    kaug_d = din("kaug", [4, S])
    caug_d = din("caug", [4, 256])
    mimp_d = din("mimp", [256, 64])
    ksel_d = din("ksel", [64, S])
    qaugs_d = din("qaugs", [NSLOT, 3, 1024])
    qaug_d = din("qaug", [NSLOT, 4, 2048])
    masks_d = din("masks", [128, 10, 128])
    cmask_d = din("cmask", [NSLOT, 2, 128, 128])
    cmfb_d = din("cmfb", [NSLOT, 2, 128, 64])
    out_d = nc.dram_tensor("out", [NSLOT * 128, D], F32, kind="ExternalOutput").ap()
    out_bufs = [P.buf("out%d" % i) for i in range(5)]
    dbg_out = {}

    A = Arena(nc, st, 212800)
    ps = st.enter_context(nc.psum_tensor("ps", [128, 4096], F32))
    bank = [ps[:, i * 512:(i + 1) * 512] for i in range(8)]
    bankb = [bank[i].bitcast(BF16) for i in range(8)]
    PB = [P.buf("bank%d" % i) for i in range(8)]

    def dma(eng, out, in_, b, reads=(), part=False):
        if part:
            return P.add(eng, lambda e: e.dma_start(out=out, in_=in_), reads=reads, pwrites=[b], dma=b)
        return P.add(eng, lambda e: e.dma_start(out=out, in_=in_), reads=reads, writes=[b], dma=b)

    def dump(name, ap, b, shape):
        if name not in debug:
            return
        dt = nc.dram_tensor("dbg_" + name, list(shape), ap.dtype, kind="ExternalOutput").ap()
        db = P.buf("dbg_" + name)
        P.add("sp", lambda e: e.dma_start(out=dt, in_=ap), reads=[b], writes=[db], dma=db)
        dbg_out[name] = db

    ident_f = A.alloc([128, 128], F32)
    ident_b = A.alloc([128, 128], BF16)
    ones_f = A.alloc([128, 128], F32)
    cols = A.alloc([128, 96], F32)
    B_const = P.buf("const")
    B_identf = P.buf("identf")
    B_ones = P.buf("ones")
    B_cols = P.buf("cols")
    B_eps = P.buf("eps")
    epsT = A.alloc([128, 8], F32)
    dma("sp", ident_f, ident_d[:, :], B_identf)
    dma("pool", ident_b, ident_d[:, :], B_const)
    P.add("pool", lambda e: e.memset(ones_f, 1.0), writes=[B_ones])
    P.add("pool", lambda e: e.memset(epsT, 1e-6), writes=[B_eps])
    P.add("pool", lambda e: e.memset(epsT[:, 1:2], 16e-6), writes=[B_eps])
    P.add("pool", lambda e: e.memset(cols, 0.0), writes=[B_cols])
    dma("sp", cols[:, 0:8], cT_d[:, :], B_cols)
    dma("sp", cols[:, 8:32], badaT[:, :], B_cols, part=True)
    dma("sp", cols[:, 32:40], gpreT[:, :], B_cols, part=True)
    dma("sp", cols[:, 40:48], gpostT[:, :], B_cols, part=True)
    sinks_t = A.alloc([128, 8], F32)
    B_sinks = P.buf("sinks")
    dma("sp", sinks_t, sinks_d[:, :], B_sinks)

    A_scr = A.alloc([128, 8], F32)
    B_scr = P.buf("scr")
    junk = A.alloc([128, 1024], BF16)
    B_junk = P.buf("junk")
    XR = A.alloc([128, 16384], BF16)
    kv_mark = A.mark()
    KT = [[A.alloc([128, S], BF16) for g in range(2)] for t in range(3)]
    B_KT = [[P.buf("KT%d%d" % (t, g)) for g in range(2)] for t in range(3)]
    Vt = A.alloc([128, 32, 6, 65], BF16)
    B_Vt = P.buf("Vt")
    KC = [A.alloc([128, 256], BF16) for g in range(2)]
    B_KC = [P.buf("KC%d" % g) for g in range(2)]
    VC = A.alloc([128, 2, 2, 129], BF16)
    B_VC = P.buf("VC")
    for t in range(3):
        for g in range(2):
            if t == 0:
                dma("pool", KT[t][g][0:64, :], ksel_d[:, :], B_KT[t][g], part=True)
            else:
                dma("pool", KT[t][g][64:68, :], kaug_d[:, :], B_KT[t][g], part=True)
    for g in range(2):
        P.add("pool", lambda e, g=g: e.memset(KC[g][0:64, :], 0.0), pwrites=[B_KC[g]])
        dma("pool", KC[g][64:68, :], caug_d[:, :], B_KC[g], part=True)
    P.add("pool", lambda e: e.memset(Vt[:, :, :, 64:65], 1.0), pwrites=[B_Vt])
    P.add("pool", lambda e: e.memset(VC[:, :, :, 64:65], 1.0), pwrites=[B_VC])
    for g in range(2):
        dma("pool", VC[:, :, g, 65:129], mimp_d.rearrange("(cc p) j -> p cc j", p=128), B_VC, part=True)

    small_mark = A.mark()
    if stop_after == "consts":
        dump("VC", VC, B_VC, [128, 2, 2, 129])
        dump("KC0", KC[0][0:68, :], B_KC[0], [68, 256])
        P.emit(nc, final_waits=list(dbg_out.values()))
        return nc

    wst = [A.alloc([128, 1536], F32) for _ in range(3)]
    B_wst = [P.buf("wst%d" % i) for i in range(3)]
    first = True
    for kc in range(8):
        for hf in range(2):
            i = (kc * 2 + hf) % 3
            dma("sp", wst[i], w_ada[kc * 128:(kc + 1) * 128, hf * 1536:(hf + 1) * 1536], B_wst[i])
            for n in range(12):
                col = hf * 12 + n
                P.add("pe", lambda e, i=i, n=n, col=col, kc=kc, first=first: e.matmul(
                    bank[0][:, col:col + 1], lhsT=wst[i][:, n * 128:(n + 1) * 128], rhs=cols[:, kc:kc + 1],
                    start=first, stop=(kc == 7), skip_group_check=True),
                    reads=[B_wst[i], B_cols], writes=[PB[0]])
                first = False
    P.add("dve", lambda e: e.tensor_tensor(out=cols[:, 48:72], in0=bank[0][:, 0:24], in1=cols[:, 8:32], op=ALU.add),
          reads=[B_cols], writes=[PB[0]], pwrites=[B_cols])
    P.add("dve", lambda e: e.scalar_tensor_tensor(out=cols[:, 72:80], in0=cols[:, 56:64], scalar=1.0, in1=cols[:, 32:40],
                                                  op0=ALU.add, op1=ALU.mult), reads=[B_cols], writes=[B_cols])
    P.add("dve", lambda e: e.tensor_tensor(out=cols[:, 80:88], in0=cols[:, 64:72], in1=cols[:, 40:48], op=ALU.mult),
          reads=[B_cols], writes=[B_cols])
    dump("cols", cols, B_cols, [128, 96])
    if stop_after == "phase0":
        P.emit(nc, final_waits=list(dbg_out.values()))
        return nc

    def rms_a(xt, B_xt, hn, B_hn):
        sc = A_scr
        P.add("act", lambda e: e.activation(out=junk, in_=xt, func=AF.Square, accum_out=sc[:, 0:1]),
              reads=[B_xt], writes=[B_junk, B_scr])
        P.add("act", lambda e: e.activation(out=sc[:, 1:2], in_=sc[:, 0:1], func=AF.Ln, scale=1.0 / D, bias=epsT[:, 0:1]),
              reads=[B_eps], writes=[B_scr])
        P.add("act", lambda e: e.activation(out=sc[:, 2:3], in_=sc[:, 1:2], func=AF.Exp, scale=-0.5), writes=[B_scr])
        P.add("dve", lambda e: e.tensor_scalar(out=hn, in0=xt, scalar1=sc[:, 2:3], scalar2=None, op0=ALU.mult),
              reads=[B_xt, B_scr], writes=[B_hn])

    def rms_b(hn, B_hn, hT_dst, B_hT, tbanks):
        for c in range(8):
            tb = tbanks[c // 4]
            P.add("pe", lambda e, c=c, tb=tb: e.transpose(out=bankb[tb][:, (c % 4) * 128:(c % 4 + 1) * 128], in_=hn[:, c * 128:(c + 1) * 128],
                                                         identity=ident_b), reads=[B_hn, B_const], writes=[PB[tb]])
        for c in range(8):
            tb = tbanks[c // 4]
            src = bankb[tb][:, (c % 4) * 128:(c % 4 + 1) * 128]
            if c >= 4:
                P.add("act", lambda e, c=c, src=src: e.activation(out=hT_dst(c), in_=src, func=AF.Identity,
                                                                  scale=cols[:, 72 + c:73 + c], bias=cols[:, 48 + c:49 + c]),
                      reads=[B_cols], xreads=[PB[tb]], pwrites=[B_hT[1]])
            else:
                P.add("dve", lambda e, c=c, src=src: e.tensor_scalar(out=hT_dst(c), in0=src,
                                                                     scalar1=cols[:, 72 + c:73 + c], scalar2=cols[:, 48 + c:49 + c],
                                                                     op0=ALU.mult, op1=ALU.add),
                      reads=[B_cols], xreads=[PB[tb]], pwrites=[B_hT[0]])

    wkv = A.alloc([128, 8, 1024], BF16)
    B_wkv = P.buf("wkv")
    kvcols = [512, 640, 768, 1024, 2328, 896, 1152, 2456]
    for i, c0 in enumerate(kvcols):
        dma("pool", wkv[:, :, i * 128:(i + 1) * 128], w_in[:, c0:c0 + 128].rearrange("(kc p) n -> p kc n", p=128), B_wkv, part=True)
    xt = [A.alloc([128, 1024], F32) for _ in range(2)]
    B_xt = [P.buf("xt%d" % i) for i in range(2)]
    hn = [A.alloc([128, 1024], BF16) for _ in range(2)]
    B_hn = [P.buf("hn%d" % i) for i in range(2)]
    hTc = [A.alloc([128, 8, 512], BF16) for _ in range(2)]
    B_hTc = [(P.buf("hTc%da" % i), P.buf("hTc%db" % i)) for i in range(2)]
    rawk = A.alloc([128, S], BF16)
    rawv = A.alloc([128, S], BF16)
    B_rawk, B_rawv = P.buf("rawk"), P.buf("rawv")
    rawk3 = rawk.rearrange("p (r m) -> p r m", r=16)
    rawv3 = rawv.rearrange("p (r m) -> p r m", r=16)
    w1k = XR[:, 0:8192].rearrange("p (a b) -> p a b", a=32)
    w1v = XR[:, 8192:16384].rearrange("p (a b) -> p a b", a=32)
    w2k = A.alloc([128, 2, 64], BF16)
    w2v = A.alloc([128, 2, 64], BF16)
    peT = A.alloc([128, 2, 64], BF16)
    B_w1 = P.buf("w1")
    for hh in range(2):
        dma("pool", w1k[hh * 64:(hh + 1) * 64], w1k_d.rearrange("(l d) h -> d l h", d=64), B_w1, part=True)
        dma("pool", w1v[hh * 64:(hh + 1) * 64], w1v_d.rearrange("(l d) h -> d l h", d=64), B_w1, part=True)
    dma("pool", w2k, w2k_d.rearrange("(hc p) d -> p hc d", p=128), B_w1, part=True)
    dma("pool", w2v, w2v_d.rearrange("(hc p) d -> p hc d", p=128), B_w1, part=True)
    dma("pool", peT[0:64, 0, :], pekT[:, :], B_w1, part=True)
    dma("pool", peT[0:64, 1, :], pevT[:, :], B_w1, part=True)

    def p1_a(tb):
        xi = tb % 2
        dma("sp", xt[xi], xf[tb * 128:(tb + 1) * 128, :], B_xt[xi])
        rms_a(xt[xi], B_xt[xi], hn[xi], B_hn[xi])

    def kgroup(ch, i):
        hb = ch % 2
        pb = 1 + (i % 2)
        for kc in range(8):
            P.add("pe", lambda e, i=i, kc=kc, pb=pb, hb=hb: e.matmul(
                bank[pb], lhsT=wkv[:, kc, i * 128:(i + 1) * 128], rhs=hTc[hb][:, kc, :], start=(kc == 0), stop=(kc == 7)),
                reads=[B_wkv, *B_hTc[hb]], writes=[PB[pb]])
        csl = slice(ch * 512, (ch + 1) * 512)
        if i == 0:
            P.add("act", lambda e, pb=pb, ch=ch: e.copy(out=rawk3[:, :, ch * 32:(ch + 1) * 32], in_=bank[pb].rearrange("p (m r) -> p r m", r=16)),
                  xreads=[PB[pb]], pwrites=[B_rawk])
        elif i == 1:
            P.add("dve", lambda e, pb=pb, ch=ch: e.tensor_copy(out=rawv3[:, :, ch * 32:(ch + 1) * 32], in_=bank[pb].rearrange("p (m r) -> p r m", r=16)),
                  xreads=[PB[pb]], pwrites=[B_rawv])
        else:
            t = i - 2
            dlo = 64 if t == 0 else 0
            P.add("act", lambda e, pb=pb, csl=csl, t=t, dlo=dlo: e.copy(out=KT[t][0][dlo:dlo + 64, csl], in_=bank[pb][0:64, :]),
                  xreads=[PB[pb]], pwrites=[B_KT[t][0]])
            P.add("dve", lambda e, pb=pb, csl=csl, t=t, dlo=dlo: e.tensor_copy(out=KT[t][1][dlo:dlo + 64, csl], in_=bank[pb][64:128, :]),
                  xreads=[PB[pb]], pwrites=[B_KT[t][1]])

    def vgroup(ch, bi):
        hb = ch % 2
        tb = ch * 4 + bi
        for kc in range(8):
            P.add("pe", lambda e, kc=kc, hb=hb, bi=bi: e.matmul(
                bank[3][:, 0:384], lhsT=hTc[hb][:, kc, bi * 128:(bi + 1) * 128], rhs=wkv[:, kc, 640:1024],
                start=(kc == 0), stop=(kc == 7)), reads=[B_wkv, *B_hTc[hb]], writes=[PB[3]])
        if bi % 2 == 0:
            P.add("act", lambda e, tb=tb: e.copy(out=Vt[:, tb, :, 0:64], in_=bank[3][:, 0:384].rearrange("p (a b) -> p a b", a=6)),
                  xreads=[PB[3]], pwrites=[B_Vt])
        else:
            P.add("dve", lambda e, tb=tb: e.tensor_copy(out=Vt[:, tb, :, 0:64], in_=bank[3][:, 0:384].rearrange("p (a b) -> p a b", a=6)),
                  xreads=[PB[3]], pwrites=[B_Vt])

    def block(tb):
        ch, bi = tb // 4, tb % 4
        hb, xi = ch % 2, tb % 2
        if tb + 1 < 32:
            p1_a(tb + 1)
        rms_b(hn[xi], B_hn[xi], lambda c, hb=hb, bi=bi: hTc[hb][:, c, bi * 128:(bi + 1) * 128], B_hTc[hb], (0, 7))

    p1_a(0)
    for ch in range(9):
        groups = []
        if ch >= 1:
            groups = [lambda i=i, c_=ch - 1: kgroup(c_, i) for i in range(5)] + [lambda b_=b_, c_=ch - 1: vgroup(c_, b_) for b_ in range(4)]
        blocks = [lambda tb=ch * 4 + bi: block(tb) for bi in range(4)] if ch < 8 else []
        order = []
        gi = 0
        for bi in range(4):
            order += groups[gi:gi + 2]
            gi += 2
            if bi < len(blocks):
                order.append(blocks[bi])
        order += groups[gi:]
        for f_ in order:
            f_()

    if stop_after == "phase1a":
        dump("KTs0", KT[0][0][0:68, :], B_KT[0][0], [68, S])
        dump("Vt", Vt, B_Vt, [128, 32, 6, 65])
        P.emit(nc, final_waits=list(dbg_out.values()))
        return nc
    hid = A.alloc([128, 2, 256], BF16)
    B_hid = P.buf("hid")
    cu = A.alloc([128, 256], F32)
    ce = A.alloc([128, 256], F32)
    B_cu, B_ce = P.buf("cu"), P.buf("ce")
    cb = A.alloc([128, 8], F32)
    B_cb = P.buf("cb")
    P.add("pool", lambda e: e.memset(hid, 0.0), writes=[B_hid])
    for kv in range(2):
        w1 = w1k if kv == 0 else w1v
        for hc in range(2):
            q4 = kv * 2 + hc
            for l in range(32):
                P.add("pe", lambda e, l=l, hc=hc, w1=w1, kv=kv, q4=q4: e.matmul(
                    bank[7][:, q4 * 32:(q4 + 1) * 32], lhsT=w1[0:64, l, hc * 128:(hc + 1) * 128],
                    rhs=peT[0:64, kv, 2 * l:2 * l + 1].to_broadcast([64, 32]), start=(l == 0), stop=(l == 31)), reads=[B_w1], writes=[PB[7]])
    P.add("dve", lambda e: e.tensor_copy(out=cb[:, 0:4], in_=bank[7][:, 0:128:32]), writes=[PB[7], B_cb])
    P.add("dve", lambda e: e.tensor_scalar(out=cb[:, 4:8], in0=cb[:, 0:4], scalar1=0.5, scalar2=None, op0=ALU.mult), writes=[B_cb])
    for kv in range(2):
        w1 = w1k if kv == 0 else w1v
        w2 = w2k if kv == 0 else w2v
        raw, B_raw = (rawk3, B_rawk) if kv == 0 else (rawv3, B_rawv)
        for g in range(2):
            for hc in range(2):
                pb = 4 + hc
                for l in range(32):
                    P.add("pe", lambda e, l=l, g=g, hc=hc, w1=w1, raw=raw, pb=pb: e.matmul(
                        bank[pb][:, 0:255], lhsT=w1[g * 64:(g + 1) * 64, l, hc * 128:(hc + 1) * 128],
                        rhs=raw[g * 64:(g + 1) * 64, l % 16, (l // 16):(l // 16) + 255], start=(l == 0), stop=(l == 31)),
                        reads=[B_w1, B_raw], writes=[PB[pb]])
                q4 = kv * 2 + hc
                P.add("act", lambda e, pb=pb, q4=q4: e.activation(out=ce[:, 0:255], in_=bank[pb][:, 0:255], func=AF.Tanh, scale=0.5, bias=cb[:, 4 + q4:5 + q4]),
                      reads=[B_cb], xreads=[PB[pb]], writes=[B_ce])
                P.add("dve", lambda e, pb=pb, q4=q4: e.tensor_scalar(out=cu[:, 0:255], in0=bank[pb][:, 0:255], scalar1=cb[:, q4:q4 + 1], scalar2=None, op0=ALU.add),
                      reads=[B_cb], xreads=[PB[pb]], writes=[B_cu])
                P.add("dve", lambda e, hc=hc: e.scalar_tensor_tensor(out=hid[:, hc, 0:255], in0=ce[:, 0:255], scalar=1.0, in1=cu[:, 0:255], op0=ALU.add, op1=ALU.mult),
                      reads=[B_cu, B_ce], writes=[B_hid])
            if kv == 0:
                for hc in range(2):
                    P.add("pe", lambda e, hc=hc, w2=w2: e.matmul(bank[6][0:64, 0:256], lhsT=w2[:, hc, :], rhs=hid[:, hc, :],
                                                                start=(hc == 0), stop=(hc == 1)), reads=[B_w1, B_hid], writes=[PB[6]])
                P.add("dve", lambda e, g=g: e.tensor_scalar(out=KC[g][0:64, :], in0=bank[6][0:64, 0:256], scalar1=0.5, scalar2=None, op0=ALU.mult),
                      writes=[PB[6]], pwrites=[B_KC[g]])
            else:
                for cc in range(2):
                    for hc in range(2):
                        P.add("pe", lambda e, hc=hc, cc=cc, w2=w2: e.matmul(bank[6][:, cc * 64:(cc + 1) * 64], lhsT=hid[:, hc, cc * 128:(cc + 1) * 128],
                                                                            rhs=w2[:, hc, :], start=(hc == 0), stop=(hc == 1)),
                              reads=[B_w1, B_hid], writes=[PB[6]])
                P.add("dve", lambda e, g=g: e.tensor_scalar(out=VC[:, :, g, 0:64], in0=bank[6][:, 0:128].rearrange("p (a b) -> p a b", a=2),
                                                            scalar1=0.5, scalar2=None, op0=ALU.mult), writes=[PB[6]], pwrites=[B_VC])
    if debug:
        P.barrier()
    dump("KTs0", KT[0][0][0:68, :], B_KT[0][0], [68, S])
    dump("KTb1", KT[2][1][0:68, :], B_KT[2][1], [68, S])
    dump("Vt", Vt, B_Vt, [128, 32, 6, 65])
    dump("KC0", KC[0][0:68, :], B_KC[0], [68, 256])
    dump("VC", VC, B_VC, [128, 2, 2, 129])
    if stop_after == "phase1":
        P.emit(nc, final_waits=list(dbg_out.values()))
        return nc

    P.barrier()
    A.release(small_mark)
    OZT = XR.rearrange("p (a b c d) -> p a b c d", a=NSLOT, b=2, c=4)
    B_OZT = P.buf("OZT")
    masks = A.alloc([128, 10, 128], BF16)
    B_masks = P.buf("masks")
    dma("pool", masks, masks_d[:, :, :], B_masks)
    wq = A.alloc([128, 8, 2072], BF16)
    B_wq = [P.buf("wq%d" % i) for i in range(5)]
    for wi, (dst, src, n) in enumerate(((0, 0, 512), (512, 1816, 512), (1024, 1304, 512), (1536, 2584, 512), (2048, 1280, 24))):
        dma("pool", wq[:, :, dst:dst + n], w_in[:, src:src + n].rearrange("(kc p) n -> p kc n", p=128), B_wq[wi])
    xq1 = A.alloc([128, 1024], F32)
    xq = [xq1, xq1]
    B_xq1 = P.buf("xq")
    B_xq = [B_xq1, B_xq1]
    hq1 = A.alloc([128, 1024], BF16)
    hq = [hq1, hq1]
    B_hq1 = P.buf("hq")
    B_hq = [B_hq1, B_hq1]
    hTq = [A.alloc([128, 8, 128], BF16) for _ in range(2)]
    B_hTq = [(P.buf("hTq%da" % i), P.buf("hTq%db" % i)) for i in range(2)]
    QT = [A.alloc([128, 4, 512], BF16) for _ in range(2)]
    B_QT = [P.buf("QT%d" % i) for i in range(2)]
    B_QTa = [P.buf("QTa%d" % i) for i in range(2)]
    PT = [A.alloc([128, 1024], BF16) for _ in range(2)]
    B_PT = [P.buf("PT%d" % i) for i in range(2)]
    cmk = [A.alloc([128, 2, 128], BF16) for _ in range(2)]
    B_cmk = [P.buf("cmk%d" % i) for i in range(2)]
    cmfb = [A.alloc([128, 2, 64], F32) for _ in range(2)]
    B_cmfb = [P.buf("cmfb%d" % i) for i in range(2)]
    QS = [[A.alloc([128, 512], BF16) for _ in range(2)] for _ in range(2)]
    B_QSq = [[P.buf("QSq%d%d" % (i, g)) for g in range(2)] for i in range(2)]
    B_QSa = [[P.buf("QSa%d%d" % (i, g)) for g in range(2)] for i in range(2)]
    B_QSs = [[P.buf("QSs%d%d" % (i, g)) for g in range(2)] for i in range(2)]
    zs = [[A.alloc([128, 512], F32) for _ in range(2)] for _ in range(2)]
    B_zs = [[P.buf("zs%d%d" % (i, k)) for k in range(2)] for i in range(2)]
    ze = A.alloc([128, 512], F32)
    B_ze = P.buf("ze")
    sg = [A.alloc([128, 24], F32) for _ in range(2)]
    B_sg = [P.buf("sg%d" % i) for i in range(2)]
    oa = A.alloc([128, 2, 4, 64], F32)
    ob = A.alloc([128, 2, 4, 64], F32)
    B_oa = [P.buf("oa0"), P.buf("oa1")]
    B_ob = [P.buf("ob0"), P.buf("ob1")]
    otmp = A.alloc([128, 4, 64], F32)
    B_otmp = P.buf("otmp")
    ozb = [A.alloc([128, 512], BF16) for _ in range(2)]
    B_ozb = [P.buf("ozb0"), P.buf("ozb1")]
    sm = A.alloc([128, 64], F32)
    B_sm = P.buf("sm")
    imp = A.alloc([128, 64], F32)
    imp4 = A.alloc([128, 256], F32)
    B_imp4 = P.buf("imp4")
    sc1 = A.alloc([128, 64], F32)
    sc2 = A.alloc([128, 64], F32)
    selb = [A.alloc([128, 64], BF16) for _ in range(2)]
    B_imp, B_sc1, B_sc2 = P.buf("imp"), P.buf("sc1"), P.buf("sc2")
    B_selb = [P.buf("selb0"), P.buf("selb1")]
    es = A.alloc([128, 8], F32)
    B_es = P.buf("es")
    woa = A.alloc([128, 4, 1024], BF16)
    B_woa = P.buf("woa")
    wo_top = A.mark()
    dma("pool", woa, wona_d.rearrange("(c p) n -> p c n", p=128), B_woa)
    P.add("act", lambda e: e.activation(out=es, in_=sinks_t, func=AF.Exp), reads=[B_sinks], writes=[B_es])

    st_pairs = [(3, 4), (5, 6)]
    st_ctr = [0]
    o_ctr = [0]
    pending = []

    delayed = []

    def flush(all_=False):
        fs = list(pending)
        del pending[:]
        for f_ in fs:
            f_()
        keep = []
        for item in list(delayed):
            item[0] -= 1
            if item[0] <= 0 or all_:
                item[1]()
            else:
                keep.append(item)
        delayed[:] = keep

    def O3v(obk):
        return bank[obk][:, 0:260].rearrange("p (a b) -> p a b", a=4)

    def attention(sp, tg, Kt, B_K, kblocks, v_of, B_V, selg, addmask, B_am, extra_rhs=None):
        obk = (7, 2)[o_ctr[0] % 2]
        o_ctr[0] += 1
        n = len(kblocks)
        first_pv = [True]
        for i0 in range(0, n, 2):
            grp = kblocks[i0:i0 + 2]
            pr = st_pairs[st_ctr[0] % 2]
            pt = st_ctr[0] % 2
            st_ctr[0] += 1
            for ii, kb in enumerate(grp):
                bk = pr[ii]
                am = addmask(kb) if addmask is not None else None
                nmm = 1 + (1 if am is not None else 0)
                if selg is None:
                    P.add("pe", lambda e, bk=bk, kb=kb, last=(nmm == 1): e.matmul(
                        bank[bk], lhsT=Kt[0:68, kb * 128:(kb + 1) * 128], rhs=QT[sp][0:68, tg, :], start=True, stop=last),
                        reads=[B_K, B_QT[sp], B_QTa[sp]], writes=[PB[bk]])
                else:
                    P.add("pe", lambda e, bk=bk, kb=kb, last=(nmm == 1): e.matmul(
                        bank[bk], lhsT=Kt[:, kb * 128:(kb + 1) * 128], rhs=QS[sp][selg], start=True, stop=last),
                        reads=[B_K, B_QSq[sp][selg], B_QSa[sp][selg], B_QSs[sp][selg]], writes=[PB[bk]])
                if am is not None:
                    P.add("pe", lambda e, bk=bk, am=am: e.matmul(
                        bank[bk].rearrange("p (a b) -> p a b", a=4), lhsT=ident_b,
                        rhs=am.unsqueeze(1).to_broadcast([128, 4, 128]), start=False, stop=True),
                        reads=[B_const, B_am], writes=[PB[bk]])
            w = len(grp) * 512
            b0 = pr[0]
            P.add("act", lambda e, b0=b0, w=w, pt=pt: e.activation(out=PT[pt][:, 0:w], in_=ps[:, b0 * 512:b0 * 512 + w], func=AF.Exp),
                  xreads=[PB[pr[ii]] for ii in range(len(grp))], writes=[B_PT[pt]])
            flush()

            def pv(grp=grp, i0=i0, pt=pt):
                for ii, kb in enumerate(grp):
                    lastk = (i0 + ii == n - 1)
                    for r in range(4):
                        P.add("pe", lambda e, ii=ii, kb=kb, r=r, fp=first_pv[0], lastk=lastk: e.matmul(
                            bank[obk][:, r * 65:(r + 1) * 65], lhsT=PT[pt][:, ii * 512 + r * 128: ii * 512 + (r + 1) * 128], rhs=v_of(kb),
                            start=fp, stop=lastk, skip_group_check=True), reads=[B_PT[pt], B_V], writes=[PB[obk]])
                        if extra_rhs is not None:
                            P.add("pe", lambda e, ii=ii, kb=kb, r=r, fp=first_pv[0], lastk=lastk: e.matmul(
                                bank[1][:, r * 64:(r + 1) * 64], lhsT=PT[pt][:, ii * 512 + r * 128: ii * 512 + (r + 1) * 128], rhs=extra_rhs(kb),
                                start=fp, stop=lastk, skip_group_check=True), reads=[B_PT[pt], B_V], writes=[PB[1]])
                        first_pv[0] = False
            pending.append(pv)
        return obk

    def finish(obk, sp, gate_col0, dst, B_dst, first_branch, extra_den=None):
        O3 = O3v(obk)
        if extra_den is not None:
            P.add("dve", lambda e: e.tensor_tensor(out=sm[:, 0:4], in0=O3[:, :, 64], in1=extra_den, op=ALU.add), reads=[B_es], xreads=[PB[obk]], writes=[B_sm])
        else:
            P.add("dve", lambda e: e.tensor_scalar(out=sm[:, 0:4], in0=O3[:, :, 64], scalar1=1e-30, scalar2=None, op0=ALU.max),
                  xreads=[PB[obk]], writes=[B_sm])
        P.add("dve", lambda e: e.reciprocal(out=sm[:, 4:8], in_=sm[:, 0:4]), writes=[B_sm])
        if gate_col0 is not None:
            P.add("dve", lambda e: e.tensor_tensor(out=sm[:, 8:12], in0=sm[:, 4:8], in1=sg[sp][:, gate_col0:gate_col0 + 4], op=ALU.mult),
                  reads=[B_sg[sp]], writes=[B_sm])
            fac = sm[:, 8:12]
        else:
            fac = sm[:, 4:8]
        facb = fac.unsqueeze(2).to_broadcast([128, 4, 64])
        if first_branch:
            P.add("dve", lambda e: e.tensor_tensor(out=dst, in0=O3[:, :, 0:64], in1=facb, op=ALU.mult),
                  reads=[B_sm], xreads=[PB[obk]], writes=[B_dst])
        else:
            P.add("dve", lambda e: e.tensor_tensor(out=otmp, in0=O3[:, :, 0:64], in1=facb, op=ALU.mult),
                  reads=[B_sm], xreads=[PB[obk]], writes=[B_otmp])
            P.add("dve", lambda e: e.tensor_tensor(out=dst, in0=dst, in1=otmp, op=ALU.add), reads=[B_otmp], writes=[B_dst])

    def importance(obk, sp, g):
        O3 = O3v(obk)
        P.add("dve", lambda e: e.tensor_scalar(out=sm[:, 16:20], in0=O3[:, :, 64], scalar1=1e-30, scalar2=None, op0=ALU.max),
              xreads=[PB[obk]], writes=[B_sm])
        P.add("dve", lambda e: e.reciprocal(out=sm[:, 20:24], in_=sm[:, 16:20]), writes=[B_sm])
        P.add("dve", lambda e: e.tensor_tensor(out=imp4.rearrange("p (a b) -> p a b", a=4), in0=bank[1][:, 0:256].rearrange("p (a b) -> p a b", a=4),
                                               in1=sm[:, 20:24].unsqueeze(2).to_broadcast([128, 4, 64]), op=ALU.mult),
              reads=[B_sm], xreads=[PB[1]], writes=[B_imp4])
        P.add("dve", lambda e: e.tensor_reduce(out=imp, in_=imp4.rearrange("p (r j) -> p j r", r=4), axis=mybir.AxisListType.X, op=ALU.add),
              reads=[B_imp4], writes=[B_imp])
        P.add("dve", lambda e: e.tensor_tensor(out=sc1, in0=imp, in1=cmfb[sp][:, 1, :], op=ALU.add), reads=[B_imp, B_cmfb[sp]], writes=[B_sc1])
        P.add("dve", lambda e: e.max(out=sm[:, 24:32], in_=sc1), reads=[B_sc1], writes=[B_sm])
        P.add("dve", lambda e: e.match_replace(out=sc2, in_to_replace=sm[:, 24:32], in_values=sc1, imm_value=-3e38),
              reads=[B_sc1, B_sm], writes=[B_sc2])
        P.add("dve", lambda e: e.max(out=sm[:, 32:40], in_=sc2), reads=[B_sc2], writes=[B_sm])
        P.add("dve", lambda e, g=g: e.tensor_scalar(out=selb[g][:, 0:61], in0=sc1[:, 1:62], scalar1=sm[:, 39:40], scalar2=NEGM, op0=ALU.is_lt, op1=ALU.mult),
              reads=[B_sc1, B_sm], writes=[B_selb[g]])

    def importance_b(g, sp):
        P.add("pe", lambda e, g=g: e.transpose(out=bankb[0][0:61, g * 128:(g + 1) * 128], in_=selb[g][:, 0:61], identity=ident_b),
              reads=[B_selb[g], B_const], writes=[PB[0]])
        P.add("dve", lambda e, g=g, sp=sp: e.tensor_copy(out=QS[sp][g][0:61, :].rearrange("p (a b) -> p a b", a=4),
                                                        in_=bankb[0][0:61, g * 128:(g + 1) * 128].unsqueeze(1).to_broadcast([61, 4, 128])),
              xreads=[PB[0]], writes=[B_QSs[sp][g]])

    def prologue_a(j):
        sp = j % 2
        dma("sp", xq[sp], xo[j * 128:(j + 1) * 128, :], B_xq[sp])
        rms_a(xq[sp], B_xq[sp], hq[sp], B_hq[sp])

    def prologue(j):
        sp = j % 2
        dma("pool", QT[sp][64:68, :, :], qaug_d[j].rearrange("r (a b) -> r a b", a=4), B_QTa[sp])
        for g in range(2):
            dma("pool", QS[sp][g][61:64, :], qaugs_d[j][:, g * 512:(g + 1) * 512], B_QSa[sp][g])
        dma("pool", cmk[sp], cmask_d[j].rearrange("c k q -> k c q"), B_cmk[sp])
        dma("sp", cmfb[sp], cmfb_d[j].rearrange("c q n -> q c n"), B_cmfb[sp])
        rms_b(hq[sp], B_hq[sp], lambda c, sp=sp: hTq[sp][:, c, :], B_hTq[sp], (0, 1))
        for tg in range(4):
            t, g = tg // 2, tg % 2
            pb = (1, 0)[tg % 2]
            for kc in range(8):
                for hp in range(2):
                    c0 = t * 512 + (g * 4 + 2 * hp) * 64
                    P.add("pe", lambda e, pb=pb, hp=hp, c0=c0, kc=kc, sp=sp: e.matmul(
                        bank[pb][:, hp * 128:(hp + 1) * 128], lhsT=wq[:, kc, c0:c0 + 128], rhs=hTq[sp][:, kc, :],
                        start=(kc == 0 and hp == 0), stop=(kc == 7), skip_group_check=True), reads=[B_wq[t], *B_hTq[sp]], writes=[PB[pb]])
            for odd in range(2):
                src = bank[pb][odd * 64:(odd + 1) * 64, 0:256].rearrange("p (a b) -> p a b", a=2)
                dstv = QT[sp][0:64, tg, :].rearrange("p (a two b) -> p a two b", a=2, two=2)[:, :, odd, :]
                P.add("dve", lambda e, src=src, dstv=dstv: e.tensor_scalar(out=dstv, in0=src, scalar1=0.125, scalar2=None, op0=ALU.mult),
                      xreads=[PB[pb]], pwrites=[B_QT[sp]])
                if t == 0:
                    dsts = QS[sp][g][64:128, :].rearrange("p (a two b) -> p a two b", a=2, two=2)[:, :, odd, :]
                    P.add("dve", lambda e, src=src, dsts=dsts: e.tensor_scalar(out=dsts, in0=src, scalar1=0.125, scalar2=None, op0=ALU.mult),
                          xreads=[PB[pb]], pwrites=[B_QSq[sp][g]])
        for zi in range(2):
            pb = (1, 0)[zi]
            for kc in range(8):
                P.add("pe", lambda e, pb=pb, zi=zi, kc=kc, sp=sp: e.matmul(
                    bank[pb], lhsT=hTq[sp][:, kc, :], rhs=wq[:, kc, 1024 + zi * 512:1536 + zi * 512], start=(kc == 0), stop=(kc == 7)),
                    reads=[B_wq[2 + zi], *B_hTq[sp]], writes=[PB[pb]])
            P.add("act", lambda e, pb=pb: e.activation(out=ze, in_=bank[pb], func=AF.Tanh, scale=0.5), xreads=[PB[pb]], writes=[B_ze])
            P.add("dve", lambda e, pb=pb, zi=zi, sp=sp: e.scalar_tensor_tensor(out=zs[sp][zi], in0=ze, scalar=1.0, in1=bank[pb],
                                                                              op0=ALU.add, op1=ALU.mult),
                  reads=[B_ze], xreads=[PB[pb]], writes=[B_zs[sp][zi]])
        for kc in range(8):
            P.add("pe", lambda e, kc=kc, sp=sp: e.matmul(bank[1][:, 0:24], lhsT=hTq[sp][:, kc, :], rhs=wq[:, kc, 2048:2072], start=(kc == 0), stop=(kc == 7)),
                  reads=[B_wq[4], *B_hTq[sp]], writes=[PB[1]])
        P.add("act", lambda e, sp=sp: e.activation(out=sg[sp], in_=bank[1][:, 0:24], func=AF.Exp, scale=-1.0), xreads=[PB[1]], writes=[B_sg[sp]])
        P.add("dve", lambda e, sp=sp: e.tensor_scalar(out=sg[sp], in0=sg[sp], scalar1=1.0, scalar2=None, op0=ALU.add), writes=[B_sg[sp]])
        P.add("dve", lambda e, sp=sp: e.reciprocal(out=sg[sp], in_=sg[sp]), writes=[B_sg[sp]])

    nslots = NSLOT if stop_after is None or not stop_after.startswith("slot") else int(stop_after[4:])
    prologue_a(0)
    prologue(0)
    wmi = {0: 2, 1: 3, 4: 4, 5: 5}
    for j in range(nslots):
        sp = j % 2
        if j + 1 < nslots:
            prologue_a(j + 1)
        chunks = [0] if j <= 7 else [0, 1]

        def cm_mask(cc, j=j, sp=sp):
            if cc == 0 and j >= 9:
                return None
            return cmk[sp][:, cc, :]
        for g in range(2):
            obk = attention(sp, 0 * 2 + g, KC[g], B_KC[g], chunks, lambda cc, g=g: VC[:, cc, g, 0:65], B_VC, None, cm_mask, B_cmk[sp],
                            extra_rhs=lambda cc, g=g: VC[:, cc, g, 65:129])
            pending.append(lambda obk=obk, sp=sp, g=g: (importance(obk, sp, g), delayed.append([2, lambda: importance_b(g, sp)])))
            pending.append(lambda obk=obk, sp=sp, g=g: finish(obk, sp, 0 * 8 + g * 4, oa[:, g], B_oa[g], True))
        for g in range(2):
            kbs = [kb for kb in range(2 * j - 4, 2 * j + 2) if kb >= 0]
            obk = attention(sp, 0 * 2 + g, KT[1][g], B_KT[1][g], kbs, lambda kb, g=g: Vt[:, kb, 1 * 2 + g, :], B_Vt, None,
                            lambda kb, j=j: masks[:, wmi[kb - (2 * j - 4)], :] if (kb - (2 * j - 4)) in wmi else None, B_masks)
            pending.append(lambda obk=obk, sp=sp, g=g: finish(obk, sp, 2 * 8 + g * 4, oa[:, g], B_oa[g], False))
        for g in range(2):
            kbs = [kb for kb in range(2 * j - 1, 2 * j + 2) if kb >= 0]
            obk = attention(sp, 1 * 2 + g, KT[2][g], B_KT[2][g], kbs, lambda kb, g=g: Vt[:, kb, 2 * 2 + g, :], B_Vt, None,
                            lambda kb, j=j: masks[:, 6 + kb - (2 * j - 1), :], B_masks)
            pending.append(lambda obk=obk, sp=sp, g=g: finish(obk, sp, None, ob[:, g], B_ob[g], True, extra_den=es[:, g * 4:(g + 1) * 4]))
        if j + 1 < nslots:
            flush()
            prologue(j + 1)
        flush(all_=True)
        for g in range(2):
            obk = attention(sp, 0 * 2 + g, KT[0][g], B_KT[0][g], list(range(2 * j + 2)), lambda kb, g=g: Vt[:, kb, 0 * 2 + g, :], B_Vt, g,
                            lambda kb, j=j: masks[:, kb - 2 * j, :] if kb >= 2 * j else None, B_masks)
            pending.append(lambda obk=obk, sp=sp, g=g: finish(obk, sp, 1 * 8 + g * 4, oa[:, g], B_oa[g], False))
        flush()
        for mi, (osrc, B_os) in enumerate(((oa, B_oa), (ob, B_ob))):
            P.add("dve", lambda e, osrc=osrc, mi=mi, sp=sp: e.tensor_tensor(out=ozb[mi], in0=osrc.rearrange("p a b c -> p (a b c)"), in1=zs[sp][mi], op=ALU.mult),
                  reads=[*B_os, B_zs[sp][mi]], writes=[B_ozb[mi]])

        def epi_b(j=j):
            for mi in range(2):
                for c in range(4):
                    P.add("pe", lambda e, c=c, mi=mi: e.transpose(out=bankb[0][:, c * 128:(c + 1) * 128], in_=ozb[mi][:, c * 128:(c + 1) * 128], identity=ident_b),
                          reads=[B_ozb[mi], B_const], writes=[PB[0]])
                P.add("dve", lambda e, j=j, mi=mi: e.tensor_copy(out=OZT[:, j, mi, :, :], in_=bankb[0][:, 0:512].rearrange("p (a b) -> p a b", a=4)),
                      xreads=[PB[0]], pwrites=[B_OZT])
        pending.append(epi_b)
    flush()
    dump("OZT", OZT, B_OZT, [128, NSLOT, 2, 4, 128])
    if stop_after is not None and (stop_after == "sweepA" or stop_after.startswith("slot")):
        P.emit(nc, final_waits=list(dbg_out.values()))
        return nc

    P.barrier()
    A.release(kv_mark)
    ggbc = A.alloc([128, 1024], F32)
    B_gg = P.buf("ggbc")
    dg = A.alloc([128, 128], F32)
    B_dg = P.buf("dg")
    for c in range(8):
        P.add("dve", lambda e, c=c: e.tensor_scalar(out=dg, in0=ident_f, scalar1=cols[:, 80 + c:81 + c], scalar2=None, op0=ALU.mult),
              reads=[B_identf, B_cols], writes=[B_dg])
        pb = c // 4
        P.add("pe", lambda e, c=c, pb=pb: e.matmul(bank[pb][:, (c % 4) * 128:(c % 4 + 1) * 128], lhsT=ones_f, rhs=dg, start=True, stop=True,
                                                   skip_group_check=True), reads=[B_ones, B_dg], writes=[PB[pb]])
    P.add("dve", lambda e: e.tensor_copy(out=ggbc, in_=ps[:, 0:1024]), writes=[PB[0], PB[1], B_gg])
    wm = A.alloc([128, 8, 2048], BF16)
    wob = A.alloc([128, 4, 1024], BF16)
    wout = A.alloc([128, 8, 1024], BF16)
    B_wm = [P.buf("wm%d" % i) for i in range(4)]
    B_wob, B_wout = P.buf("wob"), P.buf("wout")
    xb_ = [A.alloc([128, 1024], F32) for _ in range(5)]
    B_xb = [P.buf("xb%d" % i) for i in range(5)]
    hq2 = [A.alloc([128, 1024], BF16) for _ in range(2)]
    B_hq2 = [P.buf("hq2%d" % i) for i in range(2)]
    hTq2 = [A.alloc([128, 8, 128], BF16) for _ in range(2)]
    B_hTq2 = [(P.buf("hTq2a%d" % i), P.buf("hTq2b%d" % i)) for i in range(2)]
    sig = [[A.alloc([128, 1024], F32) for _ in range(2)] for _ in range(2)]
    B_sig = [[P.buf("sig%d%d" % (i, k)) for k in range(2)] for i in range(2)]
    uu = A.alloc([128, 1024], F32)
    B_uu = P.buf("uu")
    ub = [A.alloc([128, 1024], BF16) for _ in range(2)]
    B_ub = [P.buf("ub%d" % i) for i in range(2)]
    uT = A.alloc([128, 8, 128], BF16)
    B_uT = P.buf("uT")
    fin = A.alloc([128, 1024], F32)
    B_fin = P.buf("fin")

    def stage0a(j):
        xi, s2 = j % 5, j % 2
        if j >= 2:
            dma("pool", xb_[xi], xo[j * 128:(j + 1) * 128, :], B_xb[xi], reads=[])
        rms_a(xb_[xi], B_xb[xi], hq2[s2], B_hq2[s2])

    def stage0b(j):
        s2 = j % 2
        rms_b(hq2[s2], B_hq2[s2], lambda c, s2=s2: hTq2[s2][:, c, :], B_hTq2[s2], (0, 5))

    def stage1(j, mi):
        xi, s2 = j % 5, j % 2
        if True:
            for hf in range(2):
                pb = 1 + hf
                for kc in range(8):
                    P.add("pe", lambda e, pb=pb, mi=mi, hf=hf, kc=kc, s2=s2: e.matmul(
                        bank[pb], lhsT=hTq2[s2][:, kc, :], rhs=wm[:, kc, mi * 1024 + hf * 512: mi * 1024 + (hf + 1) * 512],
                        start=(kc == 0), stop=(kc == 7)), reads=[B_wm[mi * 2 + hf], *B_hTq2[s2]], writes=[PB[pb]])
            P.add("act", lambda e, mi=mi, s2=s2: e.activation(out=sig[s2][mi], in_=ps[:, 512:1536], func=AF.Tanh, scale=0.5),
                  xreads=[PB[1], PB[2]], writes=[B_sig[s2][mi]])

    def stage2(j):
        s2 = j % 2
        for mi in range(2):
            wo, B_wo = (woa, B_woa) if mi == 0 else (wob, B_wob)
            for hf in range(2):
                pb = 3 + hf
                for c in range(4):
                    P.add("pe", lambda e, pb=pb, mi=mi, hf=hf, c=c, wo=wo, j=j: e.matmul(
                        bank[pb], lhsT=OZT[:, j, mi, c, :], rhs=wo[:, c, hf * 512:(hf + 1) * 512], start=(c == 0), stop=(c == 3)),
                        reads=[B_wo, B_OZT], writes=[PB[pb]])
            if mi == 0:
                P.add("dve", lambda e, s2=s2: e.scalar_tensor_tensor(out=uu, in0=sig[s2][0], scalar=1.0, in1=ps[:, 1536:2560], op0=ALU.add, op1=ALU.mult),
                      reads=[B_sig[s2][0]], xreads=[PB[3], PB[4]], writes=[B_uu])
            else:
                P.add("dve", lambda e, s2=s2: e.scalar_tensor_tensor(out=sig[s2][1], in0=sig[s2][1], scalar=1.0, in1=ps[:, 1536:2560], op0=ALU.add, op1=ALU.mult),
                      xreads=[PB[3], PB[4]], writes=[B_sig[s2][1]])
                P.add("dve", lambda e, s2=s2: e.tensor_tensor(out=ub[s2], in0=uu, in1=sig[s2][1], op=ALU.add),
                      reads=[B_uu, B_sig[s2][1]], writes=[B_ub[s2]])

    def stage3a(j):
        s2 = j % 2
        for c in range(8):
            P.add("pe", lambda e, c=c, s2=s2: e.transpose(out=bankb[5][:, c * 128:(c + 1) * 128], in_=ub[s2][:, c * 128:(c + 1) * 128], identity=ident_b),
                  reads=[B_ub[s2], B_const], writes=[PB[5]])
        P.add("act", lambda e: e.copy(out=uT.rearrange("p a b -> p (a b)"), in_=bankb[5]), xreads=[PB[5]], writes=[B_uT])

    def stage3b(j):
        xi = j % 5
        xq2 = xb_[xi]
        for hf in range(2):
            pb = 6 + hf
            for c in range(8):
                P.add("pe", lambda e, pb=pb, hf=hf, c=c: e.matmul(bank[pb], lhsT=uT[:, c, :], rhs=wout[:, c, hf * 512:(hf + 1) * 512],
                                                                  start=(c == 0), stop=(c == 7)), reads=[B_wout, B_uT], writes=[PB[pb]])
        sc = A_scr
        P.add("act", lambda e: e.activation(out=junk, in_=ps[:, 3072:4096], func=AF.Square, accum_out=sc[:, 4:5]),
              xreads=[PB[6], PB[7]], writes=[B_junk, B_scr2])
        P.add("act", lambda e: e.activation(out=sc[:, 5:6], in_=sc[:, 4:5], func=AF.Ln, scale=1.0 / D, bias=epsT[:, 1:2]),
              reads=[B_eps], writes=[B_scr2])
        P.add("act", lambda e: e.activation(out=sc[:, 6:7], in_=sc[:, 5:6], func=AF.Exp, scale=-0.5), writes=[B_scr2])
        P.add("dve", lambda e: e.scalar_tensor_tensor(out=fin, in0=ps[:, 3072:4096], scalar=sc[:, 6:7], in1=ggbc, op0=ALU.mult, op1=ALU.mult),
              reads=[B_scr2, B_gg], xreads=[PB[6], PB[7]], writes=[B_fin])
        P.add("dve", lambda e, xq2=xq2: e.tensor_tensor(out=xq2, in0=xq2, in1=fin, op=ALU.add), reads=[B_fin], writes=[B_xb[xi]])
        P.add("sp", lambda e, xq2=xq2, j=j: e.dma_start(out=out_d[j * 128:(j + 1) * 128, :], in_=xq2), reads=[B_xb[xi]], pwrites=[out_bufs[xi]],
              dma=out_bufs[xi])

    B_scr2 = P.buf("scr2")
    for j0 in range(2):
        dma("pool", xb_[j0], xo[j0 * 128:(j0 + 1) * 128, :], B_xb[j0], reads=[])
    for hf in range(4):
        dma("pool", wm[:, :, hf * 512:(hf + 1) * 512], w_in[:, 3096 + hf * 512:3096 + (hf + 1) * 512].rearrange("(kc p) n -> p kc n", p=128),
            B_wm[hf])
    dma("pool", wob, wosw_d.rearrange("(c p) n -> p c n", p=128), B_wob)
    dma("pool", wout, wout_d.rearrange("(c p) n -> p c n", p=128), B_wout)
    stage0a(0)
    stage0b(0)
    stage0a(1)
    for it in range(NSLOT + 2):
        if it < NSLOT:
            stage1(it, 0)
        if it - 2 >= 0:
            stage3a(it - 2)
        if it + 1 < NSLOT:
            stage0b(it + 1)
        if it < NSLOT:
            stage1(it, 1)
        if 0 <= it - 1 < NSLOT:
            stage2(it - 1)
        if it - 2 >= 0:
            stage3b(it - 2)
        if it + 2 < NSLOT:
            stage0a(it + 2)
    P.emit(nc, final_waits=out_bufs + list(dbg_out.values()))
    return nc


def _inputs_for_core(core, inp, shared):
    b, p = core // 2, core % 2
    f = lambda a: np.ascontiguousarray(a, dtype=np.float32)
    x = inp["x"][b]
    m = dict(shared)
    m["xf"] = f(x)
    m["xo"] = f(x.reshape(32, 128, D)[p::2].reshape(NSLOT * 128, D))
    m["cT"] = f(inp["c"][b].reshape(8, 128).T)
    m.update(_core_tables(p))
    return m


def kernel(debug=None, stop_after=None, cores=None, **inp):
    inp = {k: np.asarray(v) for k, v in inp.items()}
    f = lambda a: np.ascontiguousarray(a, dtype=np.float32)
    shared = _shared_tables()
    shared["w_ada"] = f(inp["w_ada"][0])
    shared["badaT"] = f(inp["b_ada"][0].reshape(24, 128).T)
    shared["gpreT"] = f(inp["g_pre"][0].reshape(8, 128).T)
    shared["gpostT"] = f(inp["g_post"][0].reshape(8, 128).T)
    shared["w_in"] = f(inp["w_in"][0])
    shared["pekT"] = f(np.stack([inp["pe_cmp_k"][0].T, np.zeros((64, 32))], -1).reshape(64, 64))
    shared["pevT"] = f(np.stack([inp["pe_cmp_v"][0].T, np.zeros((64, 32))], -1).reshape(64, 64))
    shared["w1k"] = f(inp["w_cmp_k1"][0])
    shared["w2k"] = f(inp["w_cmp_k2"][0])
    shared["w1v"] = f(inp["w_cmp_v1"][0])
    shared["w2v"] = f(inp["w_cmp_v2"][0])
    shared["w_o_nsa"] = f(inp["w_o_nsa"][0])
    shared["w_o_swa"] = f(inp["w_o_swa"][0])
    shared["w_out"] = f(inp["w_out"][0])
    shared["sinksb"] = f(np.tile(inp["sinks"][0][None, :], (128, 1)))
    nc = build_nc(debug=debug, stop_after=stop_after)
    cores = list(range(8)) if cores is None else cores
    in_maps = [_inputs_for_core(c, inp, shared) for c in cores]
    if stop_after == "phase1" or (stop_after or "").startswith("slot") or stop_after == "sweepA":
        for m in in_maps:
            pass
    res = run_bass_kernel_spmd(nc, in_maps, core_ids=list(range(len(cores))))
    if stop_after:
        return res.results
    out = np.zeros((4, 32, 128, D), np.float32)
    for i, c in enumerate(cores):
        b, p = c // 2, c % 2
        out[b, p::2] = res.results[i]["out"].reshape(NSLOT, 128, D)
    if debug:
        return out.reshape(4, S, D), res.results
    return out.reshape(4, S, D)
```

```python
import numpy as np
from contextlib import ExitStack
import concourse.bass as bass
import concourse.mybir as mybir
from concourse.bass_utils import run_bass_kernel_spmd

F32 = mybir.dt.float32
BF16 = mybir.dt.bfloat16
AF = mybir.ActivationFunctionType
ALU = mybir.AluOpType

NEGM = -30000.0
IMP_INLINE = True
S = 4096
D = 1024
NSLOT = 16


class Buf:
    __slots__ = ("name", "w", "r", "xr", "dcount", "sem", "last")

    def __init__(self, name):
        self.name = name
        self.w = None
        self.r = {}
        self.xr = {}
        self.dcount = 0
        self.sem = None
        self.last = None


class Op:
    __slots__ = ("eng", "fn", "deps", "signal", "dma", "sigval")

    def __init__(self, eng, fn, dma):
        self.eng = eng
        self.fn = fn
        self.dma = dma
        self.deps = []
        self.signal = False
        self.sigval = 0


class Prog:
    ENGS = ("pe", "act", "dve", "pool", "sp")

    def __init__(self):
        self.ops = {e: [] for e in self.ENGS}
        self.dma_bufs = []
        self.pending = {e: [] for e in self.ENGS}

    def buf(self, name):
        return Buf(name)

    def barrier(self):
        lasts = [self.ops[e][-1] for e in self.ENGS if self.ops[e]]
        lasts += [b.last for b in self.dma_bufs if b.last is not None]
        for e in self.ENGS:
            self.pending[e] = list(lasts)

    def add(self, eng, fn, reads=(), writes=(), pwrites=(), dma=None, xreads=()):
        op = Op(eng, fn, dma)
        deps = {}
        for b in reads:
            if b.w is not None:
                deps[id(b.w)] = b.w
        for b in xreads:
            if b.w is not None:
                deps[id(b.w)] = b.w
            for k_, r in b.xr.items():
                if k_ != eng:
                    deps[id(r)] = r
        for b in list(writes) + list(pwrites):
            for r in b.r.values():
                deps[id(r)] = r
            for r in b.xr.values():
                deps[id(r)] = r
        for b in writes:
            if b.w is not None:
                deps[id(b.w)] = b.w
        for d in self.pending[eng]:
            deps[id(d)] = d
        self.pending[eng] = []
        for d in deps.values():
            if d.dma is None and dma is None and d.eng == "pe" and eng == "pe":
                continue
            if d.dma is None and d.eng == eng and eng == "sp":
                continue
            op.deps.append(d)
            d.signal = True
        key = eng if dma is None else ("dma", id(dma))
        for b in reads:
            b.r[key] = op
        for b in xreads:
            b.xr[eng] = op
        for b in list(writes) + list(pwrites):
            b.w = op
            b.r = {}
            b.xr = {}
        if dma is not None:
            if dma not in self.dma_bufs:
                self.dma_bufs.append(dma)
            dma.dcount += 16
            dma.last = op
            op.sigval = dma.dcount
            op.signal = True
        self.ops[eng].append(op)
        return op

    def emit(self, nc, final_waits=()):
        with ExitStack() as st:
            esem = {e: st.enter_context(nc.semaphore("sem_" + e)) for e in self.ENGS}
            for i, b in enumerate(self.dma_bufs):
                b.sem = st.enter_context(nc.semaphore("dsem%d" % i))
            for e in self.ENGS:
                c = 0
                for op in self.ops[e]:
                    if op.dma is None and op.signal:
                        c += 1
                        op.sigval = c
            block = st.enter_context(nc.Block())

            def run(eng_name, engine):
                waited = {}
                for op in self.ops[eng_name]:
                    for d in op.deps:
                        if d.dma is not None:
                            sem, key = d.dma.sem, ("d", id(d.dma))
                        else:
                            sem, key = esem[d.eng], d.eng
                        if waited.get(key, 0) >= d.sigval:
                            continue
                        waited[key] = d.sigval
                        engine.wait_ge(sem, d.sigval)
                    ins = op.fn(engine)
                    if op.dma is not None:
                        ins.then_inc(op.dma.sem, 16)
                    elif op.signal:
                        ins.then_inc(esem[eng_name], 1)
                if eng_name == "sp":
                    for b in final_waits:
                        engine.wait_ge(b.sem, b.dcount)

            @block.tensor
            def _(e):
                run("pe", e)

            @block.scalar
            def _(e):
                run("act", e)

            @block.vector
            def _(e):
                run("dve", e)

            @block.gpsimd
            def _(e):
                run("pool", e)

            @block.sync
            def _(e):
                run("sp", e)


class Arena:
    def __init__(self, nc, st, nbytes):
        self.t = st.enter_context(nc.sbuf_tensor("arena", [128, nbytes // 2], BF16))
        self.top = 0
        self.cap = nbytes

    def alloc(self, shape, dtype):
        esz = 4 if dtype == F32 else 2
        n = 1
        for s_ in shape[1:]:
            n *= s_
        nb = (n * esz + 31) // 32 * 32
        off = self.top
        self.top += nb
        assert self.top <= self.cap, ("SBUF arena overflow", self.top, self.cap)
        v = self.t[:, off // 2: off // 2 + n * esz // 2]
        if dtype == F32:
            v = v.bitcast(F32)
        if len(shape) == 3:
            v = v.rearrange("p (a b) -> p a b", a=shape[1])
        elif len(shape) == 4:
            v = v.rearrange("p (a b c) -> p a b c", a=shape[1], b=shape[2])
        elif len(shape) == 5:
            v = v.rearrange("p (a b c d) -> p a b c d", a=shape[1], b=shape[2], c=shape[3])
        return v

    def mark(self):
        return self.top

    def release(self, m):
        self.top = m


def _slopes():
    return 2.0 ** (-(np.arange(8) + 1.0))


def _shared_tables():
    t = {}
    t["ident"] = np.eye(128, dtype=np.float32)
    kpos = np.arange(S)
    t["kaug"] = np.stack([np.ones(S), np.ones(S), kpos // 128, kpos % 128]).astype(np.float32)
    c = np.arange(256)
    end = 16 * c + 31
    t["caug"] = np.stack([np.ones(256), np.ones(256), end // 128, end % 128]).astype(np.float32)
    M = np.zeros((256, 64), np.float32)
    for j in range(64):
        for m in range(4):
            for n in range(2):
                ci = 4 * j - m - n
                if 0 <= ci < 255:
                    M[ci, j] += 1.0
    t["mimp"] = M
    R = np.zeros((64, S), np.float32)
    R[kpos // 64, kpos] = 1.0
    t["ksel"] = np.concatenate([R[1:62], np.stack([np.ones(S), kpos // 128, kpos % 128])]).astype(np.float32)
    return t


def _core_tables(p):
    sl = _slopes()
    t = {}
    qa = np.zeros((NSLOT, 4, 4, 4, 128), np.float32)
    q = np.arange(128)
    for j in range(NSLOT):
        for tg in range(4):
            g = tg % 2
            for r in range(4):
                s_ = sl[g * 4 + r]
                qa[j, 0, tg, r] = -s_ * 128.0 * (2 * j + p)
                qa[j, 1, tg, r] = -s_ * q
                qa[j, 2, tg, r] = s_ * 128.0
                qa[j, 3, tg, r] = s_
    t["qaug"] = qa.reshape(NSLOT, 4, 2048)
    qs = np.zeros((NSLOT, 3, 2, 4, 128), np.float32)
    for j in range(NSLOT):
        qpos = (2 * j + p) * 128 + q
        for g in range(2):
            for r in range(4):
                s_ = sl[g * 4 + r]
                qs[j, 0, g, r] = -s_ * qpos
                qs[j, 1, g, r] = s_ * 128.0
                qs[j, 2, g, r] = s_
    t["qaugs"] = qs.reshape(NSLOT, 3, 1024)
    k = np.arange(128)[:, None]
    qq = np.arange(128)[None, :]
    masks = np.zeros((10, 128, 128), np.float32)
    for i in range(2):
        dist = (p - i) * 128 + qq - k
        masks[i] = np.where(dist >= 0, 0.0, NEGM)
    for n, i in enumerate((0, 1, 4, 5)):
        dist = (p - i + 4) * 128 + qq - k
        masks[2 + n] = np.where((dist >= 0) & (dist < 512), 0.0, NEGM)
    for i in range(3):
        dist = (p - i + 1) * 128 + qq - k
        masks[6 + i] = np.where((dist >= 0) & (dist < 128), 0.0, NEGM)
    t["masks"] = masks.transpose(1, 0, 2).copy()
    cm = np.zeros((NSLOT, 2, 128, 128), np.float32)
    for j in range(NSLOT):
        qpos = (2 * j + p) * 128 + qq
        for cc in range(2):
            c = cc * 128 + k
            ok = (qpos >= 16 * c + 31) & (c < 255)
            cm[j, cc] = np.where(ok, 0.0, NEGM)
    t["cmask"] = cm
    cf = np.zeros((NSLOT, 2, 128, 64), np.float32)
    jj = np.arange(64)[None, :]
    for j in range(NSLOT):
        qpos = (2 * j + p) * 128 + np.arange(128)[:, None]
        cur = qpos // 64
        causal = jj <= cur
        fb = np.where(causal, 0.0, -1e30)
        fb = np.where(causal & (jj == 0), 1e30, fb)
        fb = np.where(causal & (jj == cur - 1), 2e30, fb)
        fb = np.where(causal & (jj == cur), 4e30, fb)
        cf[j, 0] = causal.astype(np.float32)
        cf[j, 1] = fb
    t["cmfb"] = cf
    return t


def build_nc(debug=None, stop_after=None):
    debug = debug or []
    nc = bass.Bass("TRN2", target_bir_lowering=False)
    P = Prog()
    st = ExitStack()

    def din(name, shape):
        return nc.dram_tensor(name, list(shape), F32, kind="ExternalInput").ap()

    xf = din("xf", [S, D])
    xo = din("xo", [NSLOT * 128, D])
    cT_d = din("cT", [128, 8])
    w_ada = din("w_ada", [D, 3 * D])
    badaT = din("badaT", [128, 24])
    gpreT = din("gpreT", [128, 8])
    gpostT = din("gpostT", [128, 8])
    w_in = din("w_in", [D, 5144])
    pekT = din("pekT", [64, 64])
    pevT = din("pevT", [64, 64])
    w1k_d = din("w1k", [2048, 256])
    w2k_d = din("w2k", [256, 64])
    w1v_d = din("w1v", [2048, 256])
    w2v_d = din("w2v", [256, 64])
    wona_d = din("w_o_nsa", [512, D])
    wosw_d = din("w_o_swa", [512, D])
    wout_d = din("w_out", [D, D])
    sinks_d = din("sinksb", [128, 8])
    ident_d = din("ident", [128, 128])
    kaug_d = din("kaug", [4, S])
    caug_d = din("caug", [4, 256])
    mimp_d = din("mimp", [256, 64])
    ksel_d = din("ksel", [64, S])
    qaugs_d = din("qaugs", [NSLOT, 3, 1024])
    qaug_d = din("qaug", [NSLOT, 4, 2048])
    masks_d = din("masks", [128, 10, 128])
    cmask_d = din("cmask", [NSLOT, 2, 128, 128])
    cmfb_d = din("cmfb", [NSLOT, 2, 128, 64])
    out_d = nc.dram_tensor("out", [NSLOT * 128, D], F32, kind="ExternalOutput").ap()
    out_bufs = [P.buf("out%d" % i) for i in range(5)]
    dbg_out = {}

    A = Arena(nc, st, 212800)
    ps = st.enter_context(nc.psum_tensor("ps", [128, 4096], F32))
    bank = [ps[:, i * 512:(i + 1) * 512] for i in range(8)]
    bankb = [bank[i].bitcast(BF16) for i in range(8)]
    PB = [P.buf("bank%d" % i) for i in range(8)]

    def dma(eng, out, in_, b, reads=(), part=False):
        if part:
            return P.add(eng, lambda e: e.dma_start(out=out, in_=in_), reads=reads, pwrites=[b], dma=b)
        return P.add(eng, lambda e: e.dma_start(out=out, in_=in_), reads=reads, writes=[b], dma=b)

    def dump(name, ap, b, shape):
        if name not in debug:
            return
        dt = nc.dram_tensor("dbg_" + name, list(shape), ap.dtype, kind="ExternalOutput").ap()
        db = P.buf("dbg_" + name)
        P.add("sp", lambda e: e.dma_start(out=dt, in_=ap), reads=[b], writes=[db], dma=db)
        dbg_out[name] = db

    ident_f = A.alloc([128, 128], F32)
    ident_b = A.alloc([128, 128], BF16)
    ones_f = A.alloc([128, 128], F32)
    cols = A.alloc([128, 96], F32)
    B_const = P.buf("const")
    B_identf = P.buf("identf")
    B_ones = P.buf("ones")
    B_cols = P.buf("cols")
    B_eps = P.buf("eps")
    epsT = A.alloc([128, 8], F32)
    dma("sp", ident_f, ident_d[:, :], B_identf)
    dma("pool", ident_b, ident_d[:, :], B_const)
    P.add("pool", lambda e: e.memset(ones_f, 1.0), writes=[B_ones])
    P.add("pool", lambda e: e.memset(epsT, 1e-6), writes=[B_eps])
    P.add("pool", lambda e: e.memset(epsT[:, 1:2], 16e-6), writes=[B_eps])
    P.add("pool", lambda e: e.memset(cols, 0.0), writes=[B_cols])
    dma("sp", cols[:, 0:8], cT_d[:, :], B_cols)
    dma("sp", cols[:, 8:32], badaT[:, :], B_cols, part=True)
    dma("sp", cols[:, 32:40], gpreT[:, :], B_cols, part=True)
    dma("sp", cols[:, 40:48], gpostT[:, :], B_cols, part=True)
    sinks_t = A.alloc([128, 8], F32)
    B_sinks = P.buf("sinks")
    dma("sp", sinks_t, sinks_d[:, :], B_sinks)

    A_scr = A.alloc([128, 8], F32)
    B_scr = P.buf("scr")
    junk = A.alloc([128, 1024], BF16)
    B_junk = P.buf("junk")
    XR = A.alloc([128, 16384], BF16)
    kv_mark = A.mark()
    KT = [[A.alloc([128, S], BF16) for g in range(2)] for t in range(3)]
    B_KT = [[P.buf("KT%d%d" % (t, g)) for g in range(2)] for t in range(3)]
    Vt = A.alloc([128, 32, 6, 65], BF16)
    B_Vt = P.buf("Vt")
    KC = [A.alloc([128, 256], BF16) for g in range(2)]
    B_KC = [P.buf("KC%d" % g) for g in range(2)]
    VC = A.alloc([128, 2, 2, 129], BF16)
    B_VC = P.buf("VC")
    for t in range(3):
        for g in range(2):
            if t == 0:
                dma("pool", KT[t][g][0:64, :], ksel_d[:, :], B_KT[t][g], part=True)
            else:
                dma("pool", KT[t][g][64:68, :], kaug_d[:, :], B_KT[t][g], part=True)
    for g in range(2):
        P.add("pool", lambda e, g=g: e.memset(KC[g][0:64, :], 0.0), pwrites=[B_KC[g]])
        dma("pool", KC[g][64:68, :], caug_d[:, :], B_KC[g], part=True)
    P.add("pool", lambda e: e.memset(Vt[:, :, :, 64:65], 1.0), pwrites=[B_Vt])
    P.add("pool", lambda e: e.memset(VC[:, :, :, 64:65], 1.0), pwrites=[B_VC])
    for g in range(2):
        dma("pool", VC[:, :, g, 65:129], mimp_d.rearrange("(cc p) j -> p cc j", p=128), B_VC, part=True)

    small_mark = A.mark()
    if stop_after == "consts":
        dump("VC", VC, B_VC, [128, 2, 2, 129])
        dump("KC0", KC[0][0:68, :], B_KC[0], [68, 256])
        P.emit(nc, final_waits=list(dbg_out.values()))
        return nc

    wst = [A.alloc([128, 1536], F32) for _ in range(2)]
    B_wst = [P.buf("wst%d" % i) for i in range(2)]
    first = True
    for kc in range(8):
        for hf in range(2):
            i = (kc * 2 + hf) % 2
            dma("sp", wst[i], w_ada[kc * 128:(kc + 1) * 128, hf * 1536:(hf + 1) * 1536], B_wst[i])
            for n in range(12):
                col = hf * 12 + n
                P.add("pe", lambda e, i=i, n=n, col=col, kc=kc, first=first: e.matmul(
                    bank[0][:, col:col + 1], lhsT=wst[i][:, n * 128:(n + 1) * 128], rhs=cols[:, kc:kc + 1],
                    start=first, stop=(kc == 7), skip_group_check=True),
                    reads=[B_wst[i], B_cols], writes=[PB[0]])
                first = False
    P.add("dve", lambda e: e.tensor_tensor(out=cols[:, 48:72], in0=bank[0][:, 0:24], in1=cols[:, 8:32], op=ALU.add),
          reads=[B_cols], writes=[PB[0]], pwrites=[B_cols])
    P.add("dve", lambda e: e.scalar_tensor_tensor(out=cols[:, 72:80], in0=cols[:, 56:64], scalar=1.0, in1=cols[:, 32:40],
                                                  op0=ALU.add, op1=ALU.mult), reads=[B_cols], writes=[B_cols])
    P.add("dve", lambda e: e.tensor_tensor(out=cols[:, 80:88], in0=cols[:, 64:72], in1=cols[:, 40:48], op=ALU.mult),
          reads=[B_cols], writes=[B_cols])
    dump("cols", cols, B_cols, [128, 96])
    if stop_after == "phase0":
        P.emit(nc, final_waits=list(dbg_out.values()))
        return nc

    def rms_a(xt, B_xt, hn, B_hn):
        sc = A_scr
        P.add("act", lambda e: e.activation(out=junk, in_=xt, func=AF.Square, accum_out=sc[:, 0:1]),
              reads=[B_xt], writes=[B_junk, B_scr])
        P.add("act", lambda e: e.activation(out=sc[:, 1:2], in_=sc[:, 0:1], func=AF.Ln, scale=1.0 / D, bias=epsT[:, 0:1]),
              reads=[B_eps], writes=[B_scr])
        P.add("act", lambda e: e.activation(out=sc[:, 2:3], in_=sc[:, 1:2], func=AF.Exp, scale=-0.5), writes=[B_scr])
        P.add("dve", lambda e: e.tensor_scalar(out=hn, in0=xt, scalar1=sc[:, 2:3], scalar2=None, op0=ALU.mult),
              reads=[B_xt, B_scr], writes=[B_hn])

    def rms_b(hn, B_hn, hT_dst, B_hT, tbanks):
        for c in range(8):
            tb = tbanks[c // 4]
            P.add("pe", lambda e, c=c, tb=tb: e.transpose(out=bankb[tb][:, (c % 4) * 128:(c % 4 + 1) * 128], in_=hn[:, c * 128:(c + 1) * 128],
                                                         identity=ident_b), reads=[B_hn, B_const], writes=[PB[tb]])
        for c in range(8):
            tb = tbanks[c // 4]
            src = bankb[tb][:, (c % 4) * 128:(c % 4 + 1) * 128]
            if c >= 4:
                P.add("act", lambda e, c=c, src=src: e.activation(out=hT_dst(c), in_=src, func=AF.Identity,
                                                                  scale=cols[:, 72 + c:73 + c], bias=cols[:, 48 + c:49 + c]),
                      reads=[B_cols], xreads=[PB[tb]], pwrites=[B_hT[1]])
            else:
                P.add("dve", lambda e, c=c, src=src: e.tensor_scalar(out=hT_dst(c), in0=src,
                                                                     scalar1=cols[:, 72 + c:73 + c], scalar2=cols[:, 48 + c:49 + c],
                                                                     op0=ALU.mult, op1=ALU.add),
                      reads=[B_cols], xreads=[PB[tb]], pwrites=[B_hT[0]])

    wkv = A.alloc([128, 8, 1024], BF16)
    B_wkv = P.buf("wkv")
    kvcols = [512, 640, 768, 1024, 2328, 896, 1152, 2456]
    for i, c0 in enumerate(kvcols):
        dma("pool", wkv[:, :, i * 128:(i + 1) * 128], w_in[:, c0:c0 + 128].rearrange("(kc p) n -> p kc n", p=128), B_wkv, part=True)
    xt = [A.alloc([128, 1024], F32) for _ in range(2)]
    B_xt = [P.buf("xt%d" % i) for i in range(2)]
    hn = [A.alloc([128, 1024], BF16) for _ in range(2)]
    B_hn = [P.buf("hn%d" % i) for i in range(2)]
    hTc = [A.alloc([128, 8, 512], BF16) for _ in range(2)]
    B_hTc = [(P.buf("hTc%da" % i), P.buf("hTc%db" % i)) for i in range(2)]
    rawk = A.alloc([128, S], BF16)
    rawv = A.alloc([128, S], BF16)
    B_rawk, B_rawv = P.buf("rawk"), P.buf("rawv")
    rawk3 = rawk.rearrange("p (r m) -> p r m", r=16)
    rawv3 = rawv.rearrange("p (r m) -> p r m", r=16)
    w1k = XR[:, 0:8192].rearrange("p (a b) -> p a b", a=32)
    w1v = XR[:, 8192:16384].rearrange("p (a b) -> p a b", a=32)
    w2k = A.alloc([128, 2, 64], BF16)
    w2v = A.alloc([128, 2, 64], BF16)
    peT = A.alloc([128, 2, 64], BF16)
    B_w1 = P.buf("w1")
    for hh in range(2):
        dma("pool", w1k[hh * 64:(hh + 1) * 64], w1k_d.rearrange("(l d) h -> d l h", d=64), B_w1, part=True)
        dma("pool", w1v[hh * 64:(hh + 1) * 64], w1v_d.rearrange("(l d) h -> d l h", d=64), B_w1, part=True)
    dma("pool", w2k, w2k_d.rearrange("(hc p) d -> p hc d", p=128), B_w1, part=True)
    dma("pool", w2v, w2v_d.rearrange("(hc p) d -> p hc d", p=128), B_w1, part=True)
    dma("pool", peT[0:64, 0, :], pekT[:, :], B_w1, part=True)
    dma("pool", peT[0:64, 1, :], pevT[:, :], B_w1, part=True)

    def p1_a(tb):
        xi = tb % 2
        dma("sp", xt[xi], xf[tb * 128:(tb + 1) * 128, :], B_xt[xi])
        rms_a(xt[xi], B_xt[xi], hn[xi], B_hn[xi])

    def kgroup(ch, i):
        hb = ch % 2
        pb = 1 + (i % 2)
        for kc in range(8):
            P.add("pe", lambda e, i=i, kc=kc, pb=pb, hb=hb: e.matmul(
                bank[pb], lhsT=wkv[:, kc, i * 128:(i + 1) * 128], rhs=hTc[hb][:, kc, :], start=(kc == 0), stop=(kc == 7)),
                reads=[B_wkv, *B_hTc[hb]], writes=[PB[pb]])
        csl = slice(ch * 512, (ch + 1) * 512)
        if i == 0:
            P.add("dve", lambda e, pb=pb, ch=ch: e.tensor_copy(out=rawk3[:, :, ch * 32:(ch + 1) * 32], in_=bank[pb].rearrange("p (m r) -> p r m", r=16)),
                  xreads=[PB[pb]], pwrites=[B_rawk])
        elif i == 1:
            P.add("dve", lambda e, pb=pb, ch=ch: e.tensor_copy(out=rawv3[:, :, ch * 32:(ch + 1) * 32], in_=bank[pb].rearrange("p (m r) -> p r m", r=16)),
                  xreads=[PB[pb]], pwrites=[B_rawv])
        else:
            t = i - 2
            dlo = 64 if t == 0 else 0
            P.add("act", lambda e, pb=pb, csl=csl, t=t, dlo=dlo: e.copy(out=KT[t][0][dlo:dlo + 64, csl], in_=bank[pb][0:64, :]),
                  xreads=[PB[pb]], pwrites=[B_KT[t][0]])
            P.add("dve", lambda e, pb=pb, csl=csl, t=t, dlo=dlo: e.tensor_copy(out=KT[t][1][dlo:dlo + 64, csl], in_=bank[pb][64:128, :]),
                  xreads=[PB[pb]], pwrites=[B_KT[t][1]])

    def vgroup(ch, bi):
        hb = ch % 2
        tb = ch * 4 + bi
        for kc in range(8):
            P.add("pe", lambda e, kc=kc, hb=hb, bi=bi: e.matmul(
                bank[3][:, 0:384], lhsT=hTc[hb][:, kc, bi * 128:(bi + 1) * 128], rhs=wkv[:, kc, 640:1024],
                start=(kc == 0), stop=(kc == 7)), reads=[B_wkv, *B_hTc[hb]], writes=[PB[3]])
        if bi % 2 == 0:
            P.add("act", lambda e, tb=tb: e.copy(out=Vt[:, tb, :, 0:64], in_=bank[3][:, 0:384].rearrange("p (a b) -> p a b", a=6)),
                  xreads=[PB[3]], pwrites=[B_Vt])
        else:
            P.add("dve", lambda e, tb=tb: e.tensor_copy(out=Vt[:, tb, :, 0:64], in_=bank[3][:, 0:384].rearrange("p (a b) -> p a b", a=6)),
                  xreads=[PB[3]], pwrites=[B_Vt])

    def block(tb):
        ch, bi = tb // 4, tb % 4
        hb, xi = ch % 2, tb % 2
        if tb + 1 < 32:
            p1_a(tb + 1)
        rms_b(hn[xi], B_hn[xi], lambda c, hb=hb, bi=bi: hTc[hb][:, c, bi * 128:(bi + 1) * 128], B_hTc[hb], (0, 7))

    p1_a(0)
    for ch in range(9):
        groups = []
        if ch >= 1:
            groups = [lambda i=i, c_=ch - 1: kgroup(c_, i) for i in range(5)] + [lambda b_=b_, c_=ch - 1: vgroup(c_, b_) for b_ in range(4)]
        blocks = [lambda tb=ch * 4 + bi: block(tb) for bi in range(4)] if ch < 8 else []
        order = []
        gi = 0
        for bi in range(4):
            order += groups[gi:gi + 2]
            gi += 2
            if bi < len(blocks):
                order.append(blocks[bi])
        order += groups[gi:]
        for f_ in order:
            f_()

    if stop_after == "phase1a":
        dump("KTs0", KT[0][0][0:68, :], B_KT[0][0], [68, S])
        dump("Vt", Vt, B_Vt, [128, 32, 6, 65])
        P.emit(nc, final_waits=list(dbg_out.values()))
        return nc
    hid = A.alloc([128, 2, 256], BF16)
    B_hid = P.buf("hid")
    cu = A.alloc([128, 256], F32)
    ce = A.alloc([128, 256], F32)
    B_cu, B_ce = P.buf("cu"), P.buf("ce")
    cb = A.alloc([128, 8], F32)
    B_cb = P.buf("cb")
    P.add("pool", lambda e: e.memset(hid, 0.0), writes=[B_hid])
    for kv in range(2):
        w1 = w1k if kv == 0 else w1v
        for hc in range(2):
            q4 = kv * 2 + hc
            for l in range(32):
                P.add("pe", lambda e, l=l, hc=hc, w1=w1, kv=kv, q4=q4: e.matmul(
                    bank[7][:, q4 * 32:(q4 + 1) * 32], lhsT=w1[0:64, l, hc * 128:(hc + 1) * 128],
                    rhs=peT[0:64, kv, 2 * l:2 * l + 1].to_broadcast([64, 32]), start=(l == 0), stop=(l == 31)), reads=[B_w1], writes=[PB[7]])
    P.add("dve", lambda e: e.tensor_copy(out=cb[:, 0:4], in_=bank[7][:, 0:128:32]), writes=[PB[7], B_cb])
    P.add("dve", lambda e: e.tensor_scalar(out=cb[:, 4:8], in0=cb[:, 0:4], scalar1=0.5, scalar2=None, op0=ALU.mult), writes=[B_cb])
    for kv in range(2):
        w1 = w1k if kv == 0 else w1v
        w2 = w2k if kv == 0 else w2v
        raw, B_raw = (rawk3, B_rawk) if kv == 0 else (rawv3, B_rawv)
        for g in range(2):
            for hc in range(2):
                pb = 4 + hc
                for l in range(32):
                    P.add("pe", lambda e, l=l, g=g, hc=hc, w1=w1, raw=raw, pb=pb: e.matmul(
                        bank[pb][:, 0:255], lhsT=w1[g * 64:(g + 1) * 64, l, hc * 128:(hc + 1) * 128],
                        rhs=raw[g * 64:(g + 1) * 64, l % 16, (l // 16):(l // 16) + 255], start=(l == 0), stop=(l == 31)),
                        reads=[B_w1, B_raw], writes=[PB[pb]])
                q4 = kv * 2 + hc
                P.add("act", lambda e, pb=pb, q4=q4: e.activation(out=ce[:, 0:255], in_=bank[pb][:, 0:255], func=AF.Tanh, scale=0.5, bias=cb[:, 4 + q4:5 + q4]),
                      reads=[B_cb], xreads=[PB[pb]], writes=[B_ce])
                P.add("dve", lambda e, pb=pb, q4=q4: e.tensor_scalar(out=cu[:, 0:255], in0=bank[pb][:, 0:255], scalar1=cb[:, q4:q4 + 1], scalar2=None, op0=ALU.add),
                      reads=[B_cb], xreads=[PB[pb]], writes=[B_cu])
                P.add("dve", lambda e, hc=hc: e.scalar_tensor_tensor(out=hid[:, hc, 0:255], in0=ce[:, 0:255], scalar=1.0, in1=cu[:, 0:255], op0=ALU.add, op1=ALU.mult),
                      reads=[B_cu, B_ce], writes=[B_hid])
            if kv == 0:
                for hc in range(2):
                    P.add("pe", lambda e, hc=hc, w2=w2: e.matmul(bank[6][0:64, 0:256], lhsT=w2[:, hc, :], rhs=hid[:, hc, :],
                                                                start=(hc == 0), stop=(hc == 1)), reads=[B_w1, B_hid], writes=[PB[6]])
                P.add("dve", lambda e, g=g: e.tensor_scalar(out=KC[g][0:64, :], in0=bank[6][0:64, 0:256], scalar1=0.5, scalar2=None, op0=ALU.mult),
                      writes=[PB[6]], pwrites=[B_KC[g]])
            else:
                for cc in range(2):
                    for hc in range(2):
                        P.add("pe", lambda e, hc=hc, cc=cc, w2=w2: e.matmul(bank[6][:, cc * 64:(cc + 1) * 64], lhsT=hid[:, hc, cc * 128:(cc + 1) * 128],
                                                                            rhs=w2[:, hc, :], start=(hc == 0), stop=(hc == 1)),
                              reads=[B_w1, B_hid], writes=[PB[6]])
                P.add("dve", lambda e, g=g: e.tensor_scalar(out=VC[:, :, g, 0:64], in0=bank[6][:, 0:128].rearrange("p (a b) -> p a b", a=2),
                                                            scalar1=0.5, scalar2=None, op0=ALU.mult), writes=[PB[6]], pwrites=[B_VC])
    if debug:
        P.barrier()
    dump("KTs0", KT[0][0][0:68, :], B_KT[0][0], [68, S])
    dump("KTb1", KT[2][1][0:68, :], B_KT[2][1], [68, S])
    dump("Vt", Vt, B_Vt, [128, 32, 6, 65])
    dump("KC0", KC[0][0:68, :], B_KC[0], [68, 256])
    dump("VC", VC, B_VC, [128, 2, 2, 129])
    if stop_after == "phase1":
        P.emit(nc, final_waits=list(dbg_out.values()))
        return nc

    P.barrier()
    A.release(small_mark)
    OZT = XR.rearrange("p (a b c d) -> p a b c d", a=NSLOT, b=2, c=4)
    B_OZT = P.buf("OZT")
    masks = A.alloc([128, 10, 128], BF16)
    B_masks = P.buf("masks")
    dma("pool", masks, masks_d[:, :, :], B_masks)
    wq = A.alloc([128, 8, 2072], BF16)
    B_wq = [P.buf("wq%d" % i) for i in range(5)]
    for wi, (dst, src, n) in enumerate(((0, 0, 512), (512, 1816, 512), (1024, 1304, 512), (1536, 2584, 512), (2048, 1280, 24))):
        dma("pool", wq[:, :, dst:dst + n], w_in[:, src:src + n].rearrange("(kc p) n -> p kc n", p=128), B_wq[wi])
    xq1 = A.alloc([128, 1024], F32)
    xq = [xq1, xq1]
    B_xq1 = P.buf("xq")
    B_xq = [B_xq1, B_xq1]
    hq1 = A.alloc([128, 1024], BF16)
    hq = [hq1, hq1]
    B_hq1 = P.buf("hq")
    B_hq = [B_hq1, B_hq1]
    hTq = [A.alloc([128, 8, 128], BF16) for _ in range(2)]
    B_hTq = [(P.buf("hTq%da" % i), P.buf("hTq%db" % i)) for i in range(2)]
    QT = [A.alloc([128, 4, 512], BF16) for _ in range(2)]
    B_QT = [P.buf("QT%d" % i) for i in range(2)]
    B_QTa = [P.buf("QTa%d" % i) for i in range(2)]
    PT = [A.alloc([128, 1024], BF16) for _ in range(2)]
    B_PT = [P.buf("PT%d" % i) for i in range(2)]
    cmk = [A.alloc([128, 2, 128], BF16) for _ in range(2)]
    B_cmk = [P.buf("cmk%d" % i) for i in range(2)]
    cmfb = [A.alloc([128, 2, 64], F32) for _ in range(2)]
    B_cmfb = [P.buf("cmfb%d" % i) for i in range(2)]
    QS = [[A.alloc([128, 512], BF16) for _ in range(2)] for _ in range(2)]
    B_QSq = [[P.buf("QSq%d%d" % (i, g)) for g in range(2)] for i in range(2)]
    B_QSa = [[P.buf("QSa%d%d" % (i, g)) for g in range(2)] for i in range(2)]
    B_QSs = [[P.buf("QSs%d%d" % (i, g)) for g in range(2)] for i in range(2)]
    zs = [[A.alloc([128, 512], F32) for _ in range(2)] for _ in range(2)]
    B_zs = [[P.buf("zs%d%d" % (i, k)) for k in range(2)] for i in range(2)]
    ze = A.alloc([128, 512], F32)
    B_ze = P.buf("ze")
    sg = [A.alloc([128, 24], F32) for _ in range(2)]
    B_sg = [P.buf("sg%d" % i) for i in range(2)]
    oa = A.alloc([128, 2, 4, 64], F32)
    ob = A.alloc([128, 2, 4, 64], F32)
    B_oa = [P.buf("oa0"), P.buf("oa1")]
    B_ob = [P.buf("ob0"), P.buf("ob1")]
    otmp = A.alloc([128, 4, 64], F32)
    B_otmp = P.buf("otmp")
    ozb = [A.alloc([128, 512], BF16) for _ in range(2)]
    B_ozb = [P.buf("ozb0"), P.buf("ozb1")]
    sm = A.alloc([128, 64], F32)
    B_sm = P.buf("sm")
    imp = A.alloc([128, 64], F32)
    imp4 = A.alloc([128, 256], F32)
    B_imp4 = P.buf("imp4")
    sc1 = A.alloc([128, 64], F32)
    sc2 = A.alloc([128, 64], F32)
    selb = [A.alloc([128, 64], BF16) for _ in range(2)]
    B_imp, B_sc1, B_sc2 = P.buf("imp"), P.buf("sc1"), P.buf("sc2")
    B_selb = [P.buf("selb0"), P.buf("selb1")]
    es = A.alloc([128, 8], F32)
    B_es = P.buf("es")
    woa = A.alloc([128, 4, 1024], BF16)
    B_woa = P.buf("woa")
    wo_top = A.mark()
    dma("pool", woa, wona_d.rearrange("(c p) n -> p c n", p=128), B_woa)
    P.add("act", lambda e: e.activation(out=es, in_=sinks_t, func=AF.Exp), reads=[B_sinks], writes=[B_es])

    st_pairs = [(3, 4), (5, 6)]
    st_ctr = [0]
    o_ctr = [0]
    pending = []

    delayed = []

    def flush(all_=False):
        fs = list(pending)
        del pending[:]
        for f_ in fs:
            f_()
        keep = []
        for item in list(delayed):
            item[0] -= 1
            if item[0] <= 0 or all_:
                item[1]()
            else:
                keep.append(item)
        delayed[:] = keep

    def O3v(obk):
        return bank[obk][:, 0:260].rearrange("p (a b) -> p a b", a=4)

    def attention(sp, tg, Kt, B_K, kblocks, v_of, B_V, selg, addmask, B_am, extra_rhs=None):
        obk = (7, 2)[o_ctr[0] % 2]
        o_ctr[0] += 1
        n = len(kblocks)
        first_pv = [True]
        for i0 in range(0, n, 2):
            grp = kblocks[i0:i0 + 2]
            pr = st_pairs[st_ctr[0] % 2]
            pt = st_ctr[0] % 2
            st_ctr[0] += 1
            for ii, kb in enumerate(grp):
                bk = pr[ii]
                am = addmask(kb) if addmask is not None else None
                nmm = 1 + (1 if am is not None else 0)
                if selg is None:
                    P.add("pe", lambda e, bk=bk, kb=kb, last=(nmm == 1): e.matmul(
                        bank[bk], lhsT=Kt[0:68, kb * 128:(kb + 1) * 128], rhs=QT[sp][0:68, tg, :], start=True, stop=last),
                        reads=[B_K, B_QT[sp], B_QTa[sp]], writes=[PB[bk]])
                else:
                    P.add("pe", lambda e, bk=bk, kb=kb, last=(nmm == 1): e.matmul(
                        bank[bk], lhsT=Kt[:, kb * 128:(kb + 1) * 128], rhs=QS[sp][selg], start=True, stop=last),
                        reads=[B_K, B_QSq[sp][selg], B_QSa[sp][selg], B_QSs[sp][selg]], writes=[PB[bk]])
                if am is not None:
                    P.add("pe", lambda e, bk=bk, am=am: e.matmul(
                        bank[bk].rearrange("p (a b) -> p a b", a=4), lhsT=ident_b,
                        rhs=am.unsqueeze(1).to_broadcast([128, 4, 128]), start=False, stop=True),
                        reads=[B_const, B_am], writes=[PB[bk]])
            w = len(grp) * 512
            b0 = pr[0]
            P.add("act", lambda e, b0=b0, w=w, pt=pt: e.activation(out=PT[pt][:, 0:w], in_=ps[:, b0 * 512:b0 * 512 + w], func=AF.Exp),
                  xreads=[PB[pr[ii]] for ii in range(len(grp))], writes=[B_PT[pt]])
            flush()

            def pv(grp=grp, i0=i0, pt=pt):
                for ii, kb in enumerate(grp):
                    lastk = (i0 + ii == n - 1)
                    for r in range(4):
                        P.add("pe", lambda e, ii=ii, kb=kb, r=r, fp=first_pv[0], lastk=lastk: e.matmul(
                            bank[obk][:, r * 65:(r + 1) * 65], lhsT=PT[pt][:, ii * 512 + r * 128: ii * 512 + (r + 1) * 128], rhs=v_of(kb),
                            start=fp, stop=lastk, skip_group_check=True), reads=[B_PT[pt], B_V], writes=[PB[obk]])
                        if extra_rhs is not None:
                            P.add("pe", lambda e, ii=ii, kb=kb, r=r, fp=first_pv[0], lastk=lastk: e.matmul(
                                bank[1][:, r * 64:(r + 1) * 64], lhsT=PT[pt][:, ii * 512 + r * 128: ii * 512 + (r + 1) * 128], rhs=extra_rhs(kb),
                                start=fp, stop=lastk, skip_group_check=True), reads=[B_PT[pt], B_V], writes=[PB[1]])
                        first_pv[0] = False
            pending.append(pv)
        return obk

    def finish(obk, sp, gate_col0, dst, B_dst, first_branch, extra_den=None):
        O3 = O3v(obk)
        if extra_den is not None:
            P.add("dve", lambda e: e.tensor_tensor(out=sm[:, 0:4], in0=O3[:, :, 64], in1=extra_den, op=ALU.add), reads=[B_es], xreads=[PB[obk]], writes=[B_sm])
        else:
            P.add("dve", lambda e: e.tensor_scalar(out=sm[:, 0:4], in0=O3[:, :, 64], scalar1=1e-30, scalar2=None, op0=ALU.max),
                  xreads=[PB[obk]], writes=[B_sm])
        P.add("dve", lambda e: e.reciprocal(out=sm[:, 4:8], in_=sm[:, 0:4]), writes=[B_sm])
        if gate_col0 is not None:
            P.add("dve", lambda e: e.tensor_tensor(out=sm[:, 8:12], in0=sm[:, 4:8], in1=sg[sp][:, gate_col0:gate_col0 + 4], op=ALU.mult),
                  reads=[B_sg[sp]], writes=[B_sm])
            fac = sm[:, 8:12]
        else:
            fac = sm[:, 4:8]
        facb = fac.unsqueeze(2).to_broadcast([128, 4, 64])
        if first_branch:
            P.add("dve", lambda e: e.tensor_tensor(out=dst, in0=O3[:, :, 0:64], in1=facb, op=ALU.mult),
                  reads=[B_sm], xreads=[PB[obk]], writes=[B_dst])
        else:
            P.add("dve", lambda e: e.tensor_tensor(out=otmp, in0=O3[:, :, 0:64], in1=facb, op=ALU.mult),
                  reads=[B_sm], xreads=[PB[obk]], writes=[B_otmp])
            P.add("dve", lambda e: e.tensor_tensor(out=dst, in0=dst, in1=otmp, op=ALU.add), reads=[B_otmp], writes=[B_dst])

    def importance(obk, sp, g):
        O3 = O3v(obk)
        P.add("dve", lambda e: e.tensor_scalar(out=sm[:, 16:20], in0=O3[:, :, 64], scalar1=1e-30, scalar2=None, op0=ALU.max),
              xreads=[PB[obk]], writes=[B_sm])
        P.add("dve", lambda e: e.reciprocal(out=sm[:, 20:24], in_=sm[:, 16:20]), writes=[B_sm])
        P.add("dve", lambda e: e.tensor_tensor(out=imp4.rearrange("p (a b) -> p a b", a=4), in0=bank[1][:, 0:256].rearrange("p (a b) -> p a b", a=4),
                                               in1=sm[:, 20:24].unsqueeze(2).to_broadcast([128, 4, 64]), op=ALU.mult),
              reads=[B_sm], xreads=[PB[1]], writes=[B_imp4])
        P.add("dve", lambda e: e.tensor_reduce(out=imp, in_=imp4.rearrange("p (r j) -> p j r", r=4), axis=mybir.AxisListType.X, op=ALU.add),
              reads=[B_imp4], writes=[B_imp])
        P.add("dve", lambda e: e.tensor_tensor(out=sc1, in0=imp, in1=cmfb[sp][:, 1, :], op=ALU.add), reads=[B_imp, B_cmfb[sp]], writes=[B_sc1])
        P.add("dve", lambda e: e.max(out=sm[:, 24:32], in_=sc1), reads=[B_sc1], writes=[B_sm])
        P.add("dve", lambda e: e.match_replace(out=sc2, in_to_replace=sm[:, 24:32], in_values=sc1, imm_value=-3e38),
              reads=[B_sc1, B_sm], writes=[B_sc2])
        P.add("dve", lambda e: e.max(out=sm[:, 32:40], in_=sc2), reads=[B_sc2], writes=[B_sm])
        P.add("dve", lambda e, g=g: e.tensor_scalar(out=selb[g][:, 0:61], in0=sc1[:, 1:62], scalar1=sm[:, 39:40], scalar2=NEGM, op0=ALU.is_lt, op1=ALU.mult),
              reads=[B_sc1, B_sm], writes=[B_selb[g]])

    def importance_b(g, sp):
        P.add("pe", lambda e, g=g: e.transpose(out=bankb[0][0:61, g * 128:(g + 1) * 128], in_=selb[g][:, 0:61], identity=ident_b),
              reads=[B_selb[g], B_const], writes=[PB[0]])
        P.add("dve", lambda e, g=g, sp=sp: e.tensor_copy(out=QS[sp][g][0:61, :].rearrange("p (a b) -> p a b", a=4),
                                                        in_=bankb[0][0:61, g * 128:(g + 1) * 128].unsqueeze(1).to_broadcast([61, 4, 128])),
              xreads=[PB[0]], writes=[B_QSs[sp][g]])

    def prologue_a(j):
        sp = j % 2
        dma("sp", xq[sp], xo[j * 128:(j + 1) * 128, :], B_xq[sp])
        rms_a(xq[sp], B_xq[sp], hq[sp], B_hq[sp])

    def prologue(j):
        sp = j % 2
        dma("pool", QT[sp][64:68, :, :], qaug_d[j].rearrange("r (a b) -> r a b", a=4), B_QTa[sp])
        for g in range(2):
            dma("pool", QS[sp][g][61:64, :], qaugs_d[j][:, g * 512:(g + 1) * 512], B_QSa[sp][g])
        dma("pool", cmk[sp], cmask_d[j].rearrange("c k q -> k c q"), B_cmk[sp])
        dma("sp", cmfb[sp], cmfb_d[j].rearrange("c q n -> q c n"), B_cmfb[sp])
        rms_b(hq[sp], B_hq[sp], lambda c, sp=sp: hTq[sp][:, c, :], B_hTq[sp], (0, 1))
        for tg in range(4):
            t, g = tg // 2, tg % 2
            pb = (1, 0)[tg % 2]
            for kc in range(8):
                for hp in range(2):
                    c0 = t * 512 + (g * 4 + 2 * hp) * 64
                    P.add("pe", lambda e, pb=pb, hp=hp, c0=c0, kc=kc, sp=sp: e.matmul(
                        bank[pb][:, hp * 128:(hp + 1) * 128], lhsT=wq[:, kc, c0:c0 + 128], rhs=hTq[sp][:, kc, :],
                        start=(kc == 0 and hp == 0), stop=(kc == 7), skip_group_check=True), reads=[B_wq[t], *B_hTq[sp]], writes=[PB[pb]])
            for odd in range(2):
                src = bank[pb][odd * 64:(odd + 1) * 64, 0:256].rearrange("p (a b) -> p a b", a=2)
                dstv = QT[sp][0:64, tg, :].rearrange("p (a two b) -> p a two b", a=2, two=2)[:, :, odd, :]
                P.add("dve", lambda e, src=src, dstv=dstv: e.tensor_scalar(out=dstv, in0=src, scalar1=0.125, scalar2=None, op0=ALU.mult),
                      xreads=[PB[pb]], pwrites=[B_QT[sp]])
                if t == 0:
                    dsts = QS[sp][g][64:128, :].rearrange("p (a two b) -> p a two b", a=2, two=2)[:, :, odd, :]
                    P.add("dve", lambda e, src=src, dsts=dsts: e.tensor_scalar(out=dsts, in0=src, scalar1=0.125, scalar2=None, op0=ALU.mult),
                          xreads=[PB[pb]], pwrites=[B_QSq[sp][g]])
        for zi in range(2):
            pb = (1, 0)[zi]
            for kc in range(8):
                P.add("pe", lambda e, pb=pb, zi=zi, kc=kc, sp=sp: e.matmul(
                    bank[pb], lhsT=hTq[sp][:, kc, :], rhs=wq[:, kc, 1024 + zi * 512:1536 + zi * 512], start=(kc == 0), stop=(kc == 7)),
                    reads=[B_wq[2 + zi], *B_hTq[sp]], writes=[PB[pb]])
            P.add("act", lambda e, pb=pb: e.activation(out=ze, in_=bank[pb], func=AF.Tanh, scale=0.5), xreads=[PB[pb]], writes=[B_ze])
            P.add("dve", lambda e, pb=pb, zi=zi, sp=sp: e.scalar_tensor_tensor(out=zs[sp][zi], in0=ze, scalar=1.0, in1=bank[pb],
                                                                              op0=ALU.add, op1=ALU.mult),
                  reads=[B_ze], xreads=[PB[pb]], writes=[B_zs[sp][zi]])
        for kc in range(8):
            P.add("pe", lambda e, kc=kc, sp=sp: e.matmul(bank[1][:, 0:24], lhsT=hTq[sp][:, kc, :], rhs=wq[:, kc, 2048:2072], start=(kc == 0), stop=(kc == 7)),
                  reads=[B_wq[4], *B_hTq[sp]], writes=[PB[1]])
        P.add("act", lambda e, sp=sp: e.activation(out=sg[sp], in_=bank[1][:, 0:24], func=AF.Exp, scale=-1.0), xreads=[PB[1]], writes=[B_sg[sp]])
        P.add("dve", lambda e, sp=sp: e.tensor_scalar(out=sg[sp], in0=sg[sp], scalar1=1.0, scalar2=None, op0=ALU.add), writes=[B_sg[sp]])
        P.add("dve", lambda e, sp=sp: e.reciprocal(out=sg[sp], in_=sg[sp]), writes=[B_sg[sp]])

    nslots = NSLOT if stop_after is None or not stop_after.startswith("slot") else int(stop_after[4:])
    prologue_a(0)
    prologue(0)
    wmi = {0: 2, 1: 3, 4: 4, 5: 5}
    for j in range(nslots):
        sp = j % 2
        if j + 1 < nslots:
            prologue_a(j + 1)
        chunks = [0] if j <= 7 else [0, 1]

        def cm_mask(cc, j=j, sp=sp):
            if cc == 0 and j >= 9:
                return None
            return cmk[sp][:, cc, :]
        for g in range(2):
            obk = attention(sp, 0 * 2 + g, KC[g], B_KC[g], chunks, lambda cc, g=g: VC[:, cc, g, 0:65], B_VC, None, cm_mask, B_cmk[sp],
                            extra_rhs=lambda cc, g=g: VC[:, cc, g, 65:129])
            pending.append(lambda obk=obk, sp=sp, g=g: (importance(obk, sp, g), delayed.append([2, lambda: importance_b(g, sp)])))
            pending.append(lambda obk=obk, sp=sp, g=g: finish(obk, sp, 0 * 8 + g * 4, oa[:, g], B_oa[g], True))
        for g in range(2):
            kbs = [kb for kb in range(2 * j - 4, 2 * j + 2) if kb >= 0]
            obk = attention(sp, 0 * 2 + g, KT[1][g], B_KT[1][g], kbs, lambda kb, g=g: Vt[:, kb, 1 * 2 + g, :], B_Vt, None,
                            lambda kb, j=j: masks[:, wmi[kb - (2 * j - 4)], :] if (kb - (2 * j - 4)) in wmi else None, B_masks)
            pending.append(lambda obk=obk, sp=sp, g=g: finish(obk, sp, 2 * 8 + g * 4, oa[:, g], B_oa[g], False))
        for g in range(2):
            kbs = [kb for kb in range(2 * j - 1, 2 * j + 2) if kb >= 0]
            obk = attention(sp, 1 * 2 + g, KT[2][g], B_KT[2][g], kbs, lambda kb, g=g: Vt[:, kb, 2 * 2 + g, :], B_Vt, None,
                            lambda kb, j=j: masks[:, 6 + kb - (2 * j - 1), :], B_masks)
            pending.append(lambda obk=obk, sp=sp, g=g: finish(obk, sp, None, ob[:, g], B_ob[g], True, extra_den=es[:, g * 4:(g + 1) * 4]))
        if j + 1 < nslots:
            flush()
            prologue(j + 1)
        flush(all_=True)
        for g in range(2):
            obk = attention(sp, 0 * 2 + g, KT[0][g], B_KT[0][g], list(range(2 * j + 2)), lambda kb, g=g: Vt[:, kb, 0 * 2 + g, :], B_Vt, g,
                            lambda kb, j=j: masks[:, kb - 2 * j, :] if kb >= 2 * j else None, B_masks)
            pending.append(lambda obk=obk, sp=sp, g=g: finish(obk, sp, 1 * 8 + g * 4, oa[:, g], B_oa[g], False))
        flush()
        for mi, (osrc, B_os) in enumerate(((oa, B_oa), (ob, B_ob))):
            P.add("dve", lambda e, osrc=osrc, mi=mi, sp=sp: e.tensor_tensor(out=ozb[mi], in0=osrc.rearrange("p a b c -> p (a b c)"), in1=zs[sp][mi], op=ALU.mult),
                  reads=[*B_os, B_zs[sp][mi]], writes=[B_ozb[mi]])

        def epi_b(j=j):
            for mi in range(2):
                for c in range(4):
                    P.add("pe", lambda e, c=c, mi=mi: e.transpose(out=bankb[0][:, c * 128:(c + 1) * 128], in_=ozb[mi][:, c * 128:(c + 1) * 128], identity=ident_b),
                          reads=[B_ozb[mi], B_const], writes=[PB[0]])
                P.add("dve", lambda e, j=j, mi=mi: e.tensor_copy(out=OZT[:, j, mi, :, :], in_=bankb[0][:, 0:512].rearrange("p (a b) -> p a b", a=4)),
                      xreads=[PB[0]], pwrites=[B_OZT])
        pending.append(epi_b)
    flush()
    dump("OZT", OZT, B_OZT, [128, NSLOT, 2, 4, 128])
    if stop_after is not None and (stop_after == "sweepA" or stop_after.startswith("slot")):
        P.emit(nc, final_waits=list(dbg_out.values()))
        return nc

    P.barrier()
    A.release(kv_mark)
    ggbc = A.alloc([128, 1024], F32)
    B_gg = P.buf("ggbc")
    dg = A.alloc([128, 128], F32)
    B_dg = P.buf("dg")
    for c in range(8):
        P.add("dve", lambda e, c=c: e.tensor_scalar(out=dg, in0=ident_f, scalar1=cols[:, 80 + c:81 + c], scalar2=None, op0=ALU.mult),
              reads=[B_identf, B_cols], writes=[B_dg])
        pb = c // 4
        P.add("pe", lambda e, c=c, pb=pb: e.matmul(bank[pb][:, (c % 4) * 128:(c % 4 + 1) * 128], lhsT=ones_f, rhs=dg, start=True, stop=True,
                                                   skip_group_check=True), reads=[B_ones, B_dg], writes=[PB[pb]])
    P.add("dve", lambda e: e.tensor_copy(out=ggbc, in_=ps[:, 0:1024]), writes=[PB[0], PB[1], B_gg])
    wm = A.alloc([128, 8, 2048], BF16)
    wob = A.alloc([128, 4, 1024], BF16)
    wout = A.alloc([128, 8, 1024], BF16)
    B_wm = [P.buf("wm%d" % i) for i in range(4)]
    B_wob, B_wout = P.buf("wob"), P.buf("wout")
    xb_ = [A.alloc([128, 1024], F32) for _ in range(5)]
    B_xb = [P.buf("xb%d" % i) for i in range(5)]
    hq2 = [A.alloc([128, 1024], BF16) for _ in range(2)]
    B_hq2 = [P.buf("hq2%d" % i) for i in range(2)]
    hTq2 = [A.alloc([128, 8, 128], BF16) for _ in range(2)]
    B_hTq2 = [(P.buf("hTq2a%d" % i), P.buf("hTq2b%d" % i)) for i in range(2)]
    sig = [[A.alloc([128, 1024], F32) for _ in range(2)] for _ in range(2)]
    B_sig = [[P.buf("sig%d%d" % (i, k)) for k in range(2)] for i in range(2)]
    uu = A.alloc([128, 1024], F32)
    B_uu = P.buf("uu")
    ub = [A.alloc([128, 1024], BF16) for _ in range(2)]
    B_ub = [P.buf("ub%d" % i) for i in range(2)]
    uT = A.alloc([128, 8, 128], BF16)
    B_uT = P.buf("uT")
    fin = A.alloc([128, 1024], F32)
    B_fin = P.buf("fin")

    def stage0a(j):
        xi, s2 = j % 5, j % 2
        if j >= 2:
            dma("pool", xb_[xi], xo[j * 128:(j + 1) * 128, :], B_xb[xi], reads=[])
        rms_a(xb_[xi], B_xb[xi], hq2[s2], B_hq2[s2])

    def stage0b(j):
        s2 = j % 2
        rms_b(hq2[s2], B_hq2[s2], lambda c, s2=s2: hTq2[s2][:, c, :], B_hTq2[s2], (0, 5))

    def stage1(j, mi):
        xi, s2 = j % 5, j % 2
        if True:
            for hf in range(2):
                pb = 1 + hf
                for kc in range(8):
                    P.add("pe", lambda e, pb=pb, mi=mi, hf=hf, kc=kc, s2=s2: e.matmul(
                        bank[pb], lhsT=hTq2[s2][:, kc, :], rhs=wm[:, kc, mi * 1024 + hf * 512: mi * 1024 + (hf + 1) * 512],
                        start=(kc == 0), stop=(kc == 7)), reads=[B_wm[mi * 2 + hf], *B_hTq2[s2]], writes=[PB[pb]])
            P.add("act", lambda e, mi=mi, s2=s2: e.activation(out=sig[s2][mi], in_=ps[:, 512:1536], func=AF.Tanh, scale=0.5),
                  xreads=[PB[1], PB[2]], writes=[B_sig[s2][mi]])

    def stage2(j):
        s2 = j % 2
        for mi in range(2):
            wo, B_wo = (woa, B_woa) if mi == 0 else (wob, B_wob)
            for hf in range(2):
                pb = 3 + hf
                for c in range(4):
                    P.add("pe", lambda e, pb=pb, mi=mi, hf=hf, c=c, wo=wo, j=j: e.matmul(
                        bank[pb], lhsT=OZT[:, j, mi, c, :], rhs=wo[:, c, hf * 512:(hf + 1) * 512], start=(c == 0), stop=(c == 3)),
                        reads=[B_wo, B_OZT], writes=[PB[pb]])
            if mi == 0:
                P.add("dve", lambda e, s2=s2: e.scalar_tensor_tensor(out=uu, in0=sig[s2][0], scalar=1.0, in1=ps[:, 1536:2560], op0=ALU.add, op1=ALU.mult),
                      reads=[B_sig[s2][0]], xreads=[PB[3], PB[4]], writes=[B_uu])
            else:
                P.add("dve", lambda e, s2=s2: e.scalar_tensor_tensor(out=sig[s2][1], in0=sig[s2][1], scalar=1.0, in1=ps[:, 1536:2560], op0=ALU.add, op1=ALU.mult),
                      xreads=[PB[3], PB[4]], writes=[B_sig[s2][1]])
                P.add("dve", lambda e, s2=s2: e.tensor_tensor(out=ub[s2], in0=uu, in1=sig[s2][1], op=ALU.add),
                      reads=[B_uu, B_sig[s2][1]], writes=[B_ub[s2]])

    def stage3a(j):
        s2 = j % 2
        for c in range(8):
            P.add("pe", lambda e, c=c, s2=s2: e.transpose(out=bankb[5][:, c * 128:(c + 1) * 128], in_=ub[s2][:, c * 128:(c + 1) * 128], identity=ident_b),
                  reads=[B_ub[s2], B_const], writes=[PB[5]])
        P.add("act", lambda e: e.copy(out=uT.rearrange("p a b -> p (a b)"), in_=bankb[5]), xreads=[PB[5]], writes=[B_uT])

    def stage3b(j):
        xi = j % 5
        xq2 = xb_[xi]
        for hf in range(2):
            pb = 6 + hf
            for c in range(8):
                P.add("pe", lambda e, pb=pb, hf=hf, c=c: e.matmul(bank[pb], lhsT=uT[:, c, :], rhs=wout[:, c, hf * 512:(hf + 1) * 512],
                                                                  start=(c == 0), stop=(c == 7)), reads=[B_wout, B_uT], writes=[PB[pb]])
        sc = A_scr
        P.add("act", lambda e: e.activation(out=junk, in_=ps[:, 3072:4096], func=AF.Square, accum_out=sc[:, 4:5]),
              xreads=[PB[6], PB[7]], writes=[B_junk, B_scr2])
        P.add("act", lambda e: e.activation(out=sc[:, 5:6], in_=sc[:, 4:5], func=AF.Ln, scale=1.0 / D, bias=epsT[:, 1:2]),
              reads=[B_eps], writes=[B_scr2])
        P.add("act", lambda e: e.activation(out=sc[:, 6:7], in_=sc[:, 5:6], func=AF.Exp, scale=-0.5), writes=[B_scr2])
        P.add("dve", lambda e: e.scalar_tensor_tensor(out=fin, in0=ps[:, 3072:4096], scalar=sc[:, 6:7], in1=ggbc, op0=ALU.mult, op1=ALU.mult),
              reads=[B_scr2, B_gg], xreads=[PB[6], PB[7]], writes=[B_fin])
        P.add("dve", lambda e, xq2=xq2: e.tensor_tensor(out=xq2, in0=xq2, in1=fin, op=ALU.add), reads=[B_fin], writes=[B_xb[xi]])
        P.add("sp", lambda e, xq2=xq2, j=j: e.dma_start(out=out_d[j * 128:(j + 1) * 128, :], in_=xq2), reads=[B_xb[xi]], pwrites=[out_bufs[xi]],
              dma=out_bufs[xi])

    B_scr2 = P.buf("scr2")
    for j0 in range(2):
        dma("pool", xb_[j0], xo[j0 * 128:(j0 + 1) * 128, :], B_xb[j0], reads=[])
    for hf in range(4):
        dma("pool", wm[:, :, hf * 512:(hf + 1) * 512], w_in[:, 3096 + hf * 512:3096 + (hf + 1) * 512].rearrange("(kc p) n -> p kc n", p=128),
            B_wm[hf])
    dma("pool", wob, wosw_d.rearrange("(c p) n -> p c n", p=128), B_wob)
    dma("pool", wout, wout_d.rearrange("(c p) n -> p c n", p=128), B_wout)
    stage0a(0)
    stage0b(0)
    stage0a(1)
    for it in range(NSLOT + 2):
        if it < NSLOT:
            stage1(it, 0)
        if it - 2 >= 0:
            stage3a(it - 2)
        if it + 1 < NSLOT:
            stage0b(it + 1)
        if it < NSLOT:
            stage1(it, 1)
        if 0 <= it - 1 < NSLOT:
            stage2(it - 1)
        if it - 2 >= 0:
            stage3b(it - 2)
        if it + 2 < NSLOT:
            stage0a(it + 2)
    P.emit(nc, final_waits=out_bufs + list(dbg_out.values()))
    return nc


def _inputs_for_core(core, inp, shared):
    b, p = core // 2, core % 2
    f = lambda a: np.ascontiguousarray(a, dtype=np.float32)
    x = inp["x"][b]
    m = dict(shared)
    m["xf"] = f(x)
    m["xo"] = f(x.reshape(32, 128, D)[p::2].reshape(NSLOT * 128, D))
    m["cT"] = f(inp["c"][b].reshape(8, 128).T)
    m.update(_core_tables(p))
    return m


def kernel(debug=None, stop_after=None, cores=None, **inp):
    inp = {k: np.asarray(v) for k, v in inp.items()}
    f = lambda a: np.ascontiguousarray(a, dtype=np.float32)
    shared = _shared_tables()
    shared["w_ada"] = f(inp["w_ada"][0])
    shared["badaT"] = f(inp["b_ada"][0].reshape(24, 128).T)
    shared["gpreT"] = f(inp["g_pre"][0].reshape(8, 128).T)
    shared["gpostT"] = f(inp["g_post"][0].reshape(8, 128).T)
    shared["w_in"] = f(inp["w_in"][0])
    shared["pekT"] = f(np.stack([inp["pe_cmp_k"][0].T, np.zeros((64, 32))], -1).reshape(64, 64))
    shared["pevT"] = f(np.stack([inp["pe_cmp_v"][0].T, np.zeros((64, 32))], -1).reshape(64, 64))
    shared["w1k"] = f(inp["w_cmp_k1"][0])
    shared["w2k"] = f(inp["w_cmp_k2"][0])
    shared["w1v"] = f(inp["w_cmp_v1"][0])
    shared["w2v"] = f(inp["w_cmp_v2"][0])
    shared["w_o_nsa"] = f(inp["w_o_nsa"][0])
    shared["w_o_swa"] = f(inp["w_o_swa"][0])
    shared["w_out"] = f(inp["w_out"][0])
    shared["sinksb"] = f(np.tile(inp["sinks"][0][None, :], (128, 1)))
    nc = build_nc(debug=debug, stop_after=stop_after)
    cores = list(range(8)) if cores is None else cores
    in_maps = [_inputs_for_core(c, inp, shared) for c in cores]
    if stop_after == "phase1" or (stop_after or "").startswith("slot") or stop_after == "sweepA":
        for m in in_maps:
            pass
    res = run_bass_kernel_spmd(nc, in_maps, core_ids=list(range(len(cores))))
    if stop_after:
        return res.results
    out = np.zeros((4, 32, 128, D), np.float32)
    for i, c in enumerate(cores):
        b, p = c // 2, c % 2
        out[b, p::2] = res.results[i]["out"].reshape(NSLOT, 128, D)
    if debug:
        return out.reshape(4, S, D), res.results
    return out.reshape(4, S, D)
```
